# Optimizing a Trainium2 kernel written in Bass

```python
import math
import jax, jax.numpy as jnp
from jax import lax
import numpy as np

D_MODEL = 1024
BATCH = 4
SEQ = 4096
DEPTH = 2

BRANCH_WIDTH = 512
N_BRANCH = 3
S5_GROUP = 16
S5_GROUPS = BRANCH_WIDTH // S5_GROUP
S5_STATE = 64
S5_DT_MIN = 1e-3
S5_DT_MAX = 1e-1
S5_EIG_MAX = -1e-4
HG_HEADS = 4
HG_DK = 128
HG_DV = BRANCH_WIDTH // HG_HEADS
HG_KEY_WIDTH = HG_HEADS * HG_DK
HG_CHUNK = 64
RG_BLOCKS = 8
RG_BLOCK = BRANCH_WIDTH // RG_BLOCKS
RG_C = 8.0
CONV_WIDTH = 4
D_FF = 2816
EPS = 1e-6
IN_SPLIT_SIZES = (BRANCH_WIDTH, HG_KEY_WIDTH, HG_KEY_WIDTH, BRANCH_WIDTH, BRANCH_WIDTH, BRANCH_WIDTH, BRANCH_WIDTH)
IN_TOTAL = BRANCH_WIDTH + 2 * HG_KEY_WIDTH + 2 * BRANCH_WIDTH + 2 * BRANCH_WIDTH + N_BRANCH * D_MODEL

kernel_name = 'hybrid_s5_hgrn2_rglru_macaron'


def rms_norm(x, w):
    xf = x.astype(jnp.float32)
    y = xf * lax.rsqrt(jnp.mean(xf * xf, axis=-1, keepdims=True) + EPS)
    return (y * w.astype(jnp.float32)).astype(x.dtype)


def swiglu(h, w_gate, w_up, w_down):
    return (jax.nn.silu(h @ w_gate) * (h @ w_up)) @ w_down


def _complex_affine_combine(e1, e2):
    a1r, a1i, b1r, b1i = e1
    a2r, a2i, b2r, b2i = e2
    return (a2r * a1r - a2i * a1i,
            a2r * a1i + a2i * a1r,
            a2r * b1r - a2i * b1i + b2r,
            a2r * b1i + a2i * b1r + b2i)


def _real_affine_combine(e1, e2):
    a1, b1 = e1
    a2, b2 = e2
    return (a2 * a1, a2 * b1 + b2)


def s5_mixer(u, lam_re, lam_im, log_dt, b_re, b_im, c_re, c_im, d_skip, glu_w, glu_b):
    f32 = jnp.float32
    bsz, seq, _ = u.shape
    uf = u.astype(f32)
    ug = uf.reshape(bsz, seq, S5_GROUPS, S5_GROUP)
    lr = jnp.minimum(lam_re.astype(f32), S5_EIG_MAX)
    li = lam_im.astype(f32)
    dt = jnp.exp(log_dt.astype(f32))[:, None]
    mag = jnp.exp(lr * dt)
    ar = mag * jnp.cos(li * dt)
    ai = mag * jnp.sin(li * dt)
    den = lr * lr + li * li
    fr = ((ar - 1.0) * lr + ai * li) / den
    fi = (ai * lr - (ar - 1.0) * li) / den
    br, bi = b_re.astype(f32), b_im.astype(f32)
    bbr = fr[..., None] * br - fi[..., None] * bi
    bbi = fr[..., None] * bi + fi[..., None] * br
    bu_r = jnp.einsum('blgc,gpc->blgp', ug, bbr)
    bu_i = jnp.einsum('blgc,gpc->blgp', ug, bbi)
    a_r = jnp.broadcast_to(ar, bu_r.shape)
    a_i = jnp.broadcast_to(ai, bu_i.shape)
    _, _, xr, xi = lax.associative_scan(_complex_affine_combine, (a_r, a_i, bu_r, bu_i), axis=1)
    y = (jnp.einsum('blgp,gcp->blgc', xr, c_re.astype(f32))
         - jnp.einsum('blgp,gcp->blgc', xi, c_im.astype(f32)))
    y = y.reshape(bsz, seq, BRANCH_WIDTH) + d_skip.astype(f32) * uf
    z = jax.nn.gelu(y)
    out = z * jax.nn.sigmoid(z @ glu_w.astype(f32) + glu_b.astype(f32))
    return out.astype(u.dtype)


def hgrn2_mixer(q, z_f, v, g, lb, norm_w):
    f32 = jnp.float32
    bsz, seq, _ = q.shape
    n_chunks = seq // HG_CHUNK
    lb = lb.astype(f32).reshape(HG_HEADS, HG_DK)
    qh = jax.nn.silu(q.astype(f32)).reshape(bsz, seq, HG_HEADS, HG_DK)
    zf = z_f.astype(f32).reshape(bsz, seq, HG_HEADS, HG_DK)
    log_f = jnp.log(lb + (1.0 - lb) * jax.nn.sigmoid(zf))
    kh = (1.0 - lb) * jax.nn.sigmoid(-zf)
    vh = v.astype(f32).reshape(bsz, seq, HG_HEADS, HG_DV)

    def to_chunks(t):
        return t.reshape(bsz, n_chunks, HG_CHUNK, HG_HEADS, t.shape[-1]).transpose(1, 0, 3, 2, 4)

    causal = jnp.tril(jnp.ones((HG_CHUNK, HG_CHUNK), dtype=bool))[:, :, None]

    def chunk_step(state, inp):
        qc, kc, vc, lfc = inp
        b = jnp.cumsum(lfc, axis=2)
        o_inter = jnp.einsum('bhcd,bhde->bhce', qc * jnp.exp(b), state)
        diff = b[:, :, :, None, :] - b[:, :, None, :, :]
        decay = jnp.where(causal, jnp.exp(jnp.where(causal, diff, 0.0)), 0.0)
        scores = jnp.einsum('bhtd,bhtsd,bhsd->bhts', qc, decay, kc)
        o_intra = jnp.einsum('bhts,bhse->bhte', scores, vc)
        b_last = b[:, :, -1:, :]
        new_state = (jnp.exp(b_last[:, :, 0, :])[..., None] * state
                     + jnp.einsum('bhsd,bhse->bhde', kc * jnp.exp(b_last - b), vc))
        return new_state, o_inter + o_intra

    s0 = jnp.zeros((bsz, HG_HEADS, HG_DK, HG_DV), f32)
    _, o = lax.scan(chunk_step, s0, (to_chunks(qh), to_chunks(kh), to_chunks(vh), to_chunks(log_f)))
    o = o.transpose(1, 0, 3, 2, 4).reshape(bsz, seq, HG_HEADS, HG_DV)
    o = o * lax.rsqrt(jnp.mean(o * o, axis=-1, keepdims=True) + EPS)
    o = o * norm_w.astype(f32).reshape(HG_HEADS, HG_DV)
    out = o.reshape(bsz, seq, BRANCH_WIDTH) * jax.nn.silu(g.astype(f32))
    return out.astype(q.dtype)


def rglru_mixer(xb, gate, conv_w, conv_b, wa, ba, wx, bx, lam):
    f32 = jnp.float32
    bsz, seq, _ = xb.shape
    xc = lax.conv_general_dilated(
        xb, conv_w[:, None, :], window_strides=(1,), padding=[(CONV_WIDTH - 1, 0)],
        dimension_numbers=('NWC', 'WIO', 'NWC'), feature_group_count=BRANCH_WIDTH) + conv_b
    xcf = xc.astype(f32)
    xblk = xcf.reshape(bsz, seq, RG_BLOCKS, RG_BLOCK)
    r = jax.nn.sigmoid(jnp.einsum('blhi,hij->blhj', xblk, wa.astype(f32)).reshape(bsz, seq, BRANCH_WIDTH) + ba.astype(f32))
    i = jax.nn.sigmoid(jnp.einsum('blhi,hij->blhj', xblk, wx.astype(f32)).reshape(bsz, seq, BRANCH_WIDTH) + bx.astype(f32))
    log_a = -RG_C * jax.nn.softplus(-lam.astype(f32)) * r
    a = jnp.exp(log_a)
    b = jnp.sqrt(-jnp.expm1(2.0 * log_a)) * (i * xcf)
    _, hseq = lax.associative_scan(_real_affine_combine, (a, b), axis=1)
    return (hseq * jax.nn.gelu(gate.astype(f32))).astype(xb.dtype)


def hybrid_mixer(h, w_in, branch_proj, w_out,
                 s5_lambda_re, s5_lambda_im, s5_log_dt, s5_b_re, s5_b_im, s5_c_re, s5_c_im,
                 s5_d, s5_glu_w, s5_glu_b, hg_lb, hg_norm_w,
                 rg_conv_w, rg_conv_b, rg_wa, rg_ba, rg_wx, rg_bx, rg_lambda):
    bsz, seq, _ = h.shape
    proj = h @ w_in
    points, acc = [], 0
    for s in IN_SPLIT_SIZES:
        acc += s
        points.append(acc)
    u_a, q_b, f_b, v_b, g_b, x_c, gate_c, gate_merge = jnp.split(proj, points, axis=-1)
    y_a = s5_mixer(u_a, s5_lambda_re, s5_lambda_im, s5_log_dt, s5_b_re, s5_b_im,
                   s5_c_re, s5_c_im, s5_d, s5_glu_w, s5_glu_b)
    y_b = hgrn2_mixer(q_b, f_b, v_b, g_b, hg_lb, hg_norm_w)
    y_c = rglru_mixer(x_c, gate_c, rg_conv_w, rg_conv_b, rg_wa, rg_ba, rg_wx, rg_bx, rg_lambda)
    branches = jnp.stack([y_a, y_b, y_c], axis=2)
    up = jnp.einsum('blnw,nwd->blnd', branches, branch_proj)
    gates = jax.nn.sigmoid(gate_merge.astype(jnp.float32)).reshape(bsz, seq, N_BRANCH, D_MODEL)
    merged = jnp.sum(gates * up.astype(jnp.float32), axis=2).astype(h.dtype)
    return merged @ w_out


def setup_inputs(seed: int = 0) -> dict:
    key = jax.random.key(seed)
    ks = jax.random.split(key, 32)
    f32 = jnp.float32

    def nrm(k, shape, scale):
        return jax.random.normal(k, shape, f32) * scale

    x = nrm(ks[0], (BATCH, SEQ, D_MODEL), 1.0)
    norm_w = 1.0 + nrm(ks[1], (DEPTH, 3, D_MODEL), 0.02)
    final_norm_w = 1.0 + nrm(ks[2], (D_MODEL,), 0.02)
    ffn_gate = nrm(ks[3], (DEPTH, 2, D_MODEL, D_FF), D_MODEL ** -0.5)
    ffn_up = nrm(ks[4], (DEPTH, 2, D_MODEL, D_FF), D_MODEL ** -0.5)
    ffn_down = nrm(ks[5], (DEPTH, 2, D_FF, D_MODEL), D_FF ** -0.5)
    w_in = nrm(ks[6], (DEPTH, D_MODEL, IN_TOTAL), D_MODEL ** -0.5)
    branch_proj = nrm(ks[7], (DEPTH, N_BRANCH, BRANCH_WIDTH, D_MODEL), BRANCH_WIDTH ** -0.5)
    w_out = nrm(ks[8], (DEPTH, D_MODEL, D_MODEL), D_MODEL ** -0.5)
    s5_lambda_re = -0.5 + nrm(ks[9], (DEPTH, S5_GROUPS, S5_STATE), 0.01)
    s5_lambda_im = (math.pi * jnp.arange(S5_STATE, dtype=f32)) + nrm(ks[10], (DEPTH, S5_GROUPS, S5_STATE), 0.01)
    s5_log_dt = jax.random.uniform(ks[11], (DEPTH, S5_GROUPS), f32, math.log(S5_DT_MIN), math.log(S5_DT_MAX))
    s5_b_re = nrm(ks[12], (DEPTH, S5_GROUPS, S5_STATE, S5_GROUP), (2.0 * S5_GROUP) ** -0.5)
    s5_b_im = nrm(ks[13], (DEPTH, S5_GROUPS, S5_STATE, S5_GROUP), (2.0 * S5_GROUP) ** -0.5)
    s5_c_re = nrm(ks[14], (DEPTH, S5_GROUPS, S5_GROUP, S5_STATE), S5_STATE ** -0.5)
    s5_c_im = nrm(ks[15], (DEPTH, S5_GROUPS, S5_GROUP, S5_STATE), S5_STATE ** -0.5)
    s5_d = nrm(ks[16], (DEPTH, BRANCH_WIDTH), 1.0)
    s5_glu_w = nrm(ks[17], (DEPTH, BRANCH_WIDTH, BRANCH_WIDTH), BRANCH_WIDTH ** -0.5)
    s5_glu_b = nrm(ks[18], (DEPTH, BRANCH_WIDTH), 0.01)
    hg_lb_logits = 1.0 + nrm(ks[19], (DEPTH, HG_KEY_WIDTH), 0.1)
    hg_norm_w = 1.0 + nrm(ks[20], (DEPTH, BRANCH_WIDTH), 0.02)
    rg_conv_w = nrm(ks[21], (DEPTH, CONV_WIDTH, BRANCH_WIDTH), CONV_WIDTH ** -0.5)
    rg_conv_b = nrm(ks[22], (DEPTH, BRANCH_WIDTH), 0.01)
    rg_wa = nrm(ks[23], (DEPTH, RG_BLOCKS, RG_BLOCK, RG_BLOCK), RG_BLOCK ** -0.5)
    rg_ba = nrm(ks[24], (DEPTH, BRANCH_WIDTH), 0.01)
    rg_wx = nrm(ks[25], (DEPTH, RG_BLOCKS, RG_BLOCK, RG_BLOCK), RG_BLOCK ** -0.5)
    rg_bx = nrm(ks[26], (DEPTH, BRANCH_WIDTH), 0.01)
    a_c = jax.random.uniform(ks[27], (DEPTH, BRANCH_WIDTH), f32, 0.9, 0.999)
    s = a_c ** (1.0 / RG_C)
    rg_lambda = jnp.log(s) - jnp.log1p(-s)
    return {'x': x, 'norm_w': norm_w, 'final_norm_w': final_norm_w,
            'ffn_gate': ffn_gate, 'ffn_up': ffn_up, 'ffn_down': ffn_down,
            'w_in': w_in, 'branch_proj': branch_proj, 'w_out': w_out,
            's5_lambda_re': s5_lambda_re, 's5_lambda_im': s5_lambda_im, 's5_log_dt': s5_log_dt,
            's5_b_re': s5_b_re, 's5_b_im': s5_b_im, 's5_c_re': s5_c_re, 's5_c_im': s5_c_im,
            's5_d': s5_d, 's5_glu_w': s5_glu_w, 's5_glu_b': s5_glu_b,
            'hg_lb_logits': hg_lb_logits, 'hg_norm_w': hg_norm_w,
            'rg_conv_w': rg_conv_w, 'rg_conv_b': rg_conv_b, 'rg_wa': rg_wa, 'rg_ba': rg_ba,
            'rg_wx': rg_wx, 'rg_bx': rg_bx, 'rg_lambda': rg_lambda}


def reference(x, norm_w, final_norm_w, ffn_gate, ffn_up, ffn_down, w_in, branch_proj, w_out,
              s5_lambda_re, s5_lambda_im, s5_log_dt, s5_b_re, s5_b_im, s5_c_re, s5_c_im,
              s5_d, s5_glu_w, s5_glu_b, hg_lb_logits, hg_norm_w,
              rg_conv_w, rg_conv_b, rg_wa, rg_ba, rg_wx, rg_bx, rg_lambda):
    p = jax.nn.softmax(hg_lb_logits.astype(jnp.float32), axis=0)
    lower_bounds = jnp.cumsum(p, axis=0) - p[0]
    for l in range(DEPTH):
        h = rms_norm(x, norm_w[l, 0])
        x = x + 0.5 * swiglu(h, ffn_gate[l, 0], ffn_up[l, 0], ffn_down[l, 0])
        h = rms_norm(x, norm_w[l, 1])
        x = x + hybrid_mixer(h, w_in[l], branch_proj[l], w_out[l],
                             s5_lambda_re[l], s5_lambda_im[l], s5_log_dt[l], s5_b_re[l], s5_b_im[l],
                             s5_c_re[l], s5_c_im[l], s5_d[l], s5_glu_w[l], s5_glu_b[l],
                             lower_bounds[l], hg_norm_w[l],
                             rg_conv_w[l], rg_conv_b[l], rg_wa[l], rg_ba[l], rg_wx[l], rg_bx[l], rg_lambda[l])
        h = rms_norm(x, norm_w[l, 2])
        x = x + 0.5 * swiglu(h, ffn_gate[l, 1], ffn_up[l, 1], ffn_down[l, 1])
    return rms_norm(x, final_norm_w)
```

```python
import math
from contextlib import ExitStack
import numpy as np
import concourse.bass as bass
import concourse.mybir as mybir
from concourse.bass_utils import run_bass_kernel_spmd

F32 = mybir.dt.float32
BF16 = mybir.dt.bfloat16
AF = mybir.ActivationFunctionType
ALU = mybir.AluOpType
ENG = ("pe", "act", "dve", "pool", "sp")

D = 1024
SEQ = 4096
DFF = 2816
NM = DFF // 128
INT = 6656
EPS = 1e-6
NB = 512
TT = NB // 128
NCH = NB // 64
TBL = 128
NCORE = 4


class Buf:
    __slots__ = ("name", "w", "r", "excl")

    def __init__(self, name, excl=False):
        self.name = name
        self.excl = excl
        self.w = None
        self.r = {}


class Sched:
    NDSEM = 24

    def __init__(self, nc, es):
        self.nc = nc
        self.ops = {e: [] for e in ENG}
        self.sem = {e: es.enter_context(nc.semaphore("s_" + e)) for e in ENG if e != "sp"}
        self.seen = {e: {} for e in ENG}
        self.QN = {"sp": 0, "pool": 1, "act": 2}
        self.NDSEM = 12 * 3
        self.dsem = [es.enter_context(nc.semaphore("d%d" % i)) for i in range(self.NDSEM)]
        self.dcnt = [0] * self.NDSEM
        self.dma_i = {"sp": 0, "pool": 0, "act": 0}
        self.cc_sem = es.enter_context(nc.semaphore("cc_sem"))
        self.cc_cnt = 0

    def _wait(self, eng, tok):
        if tok is None:
            return
        if tok[0] == "c":
            _, e2, rec = tok
            if e2 == eng and eng == "pe":
                return
            key = ("c", e2)
            if self.seen[eng].get(key, -1) >= rec["idx"]:
                return
            self.seen[eng][key] = rec["idx"]
            rec["sig"] = True
            self.ops[eng].append({"kind": "wc", "eng": e2, "rec": rec})
        elif tok[0] == "k":
            v = tok[2]
            key = ("k", "cc")
            if self.seen[eng].get(key, -1) >= v:
                return
            self.seen[eng][key] = v
            self.ops[eng].append({"kind": "wk", "v": v})
        else:
            _, slot, v = tok
            key = ("d", slot)
            if self.seen[eng].get(key, -1) >= v:
                return
            self.seen[eng][key] = v
            self.ops[eng].append({"kind": "wd", "slot": slot, "v": v})

    def _deps(self, eng, R, W):
        for b in R:
            self._wait(eng, b.w)
            if b.excl:
                for k, t in list(b.r.items()):
                    if k[1] != eng:
                        self._wait(eng, t)
        for b in W:
            self._wait(eng, b.w)
            for t in b.r.values():
                self._wait(eng, t)

    def _mark(self, tok, R, W):
        for b in W:
            b.w = tok
            b.r = {}
        for b in R:
            if b in W:
                continue
            b.r[(tok[0], tok[1])] = tok

    def op(self, eng, fn, R=(), W=()):
        self._deps(eng, R, W)
        rec = {"kind": "op", "fn": fn, "sig": False, "idx": len(self.ops[eng])}
        self.ops[eng].append(rec)
        self._mark(("c", eng, rec), R, W)

    def dma(self, q, fn, R=(), W=()):
        self._deps(q, R, W)
        slot = self.QN[q] * 12 + self.dma_i[q] % 12
        self.dma_i[q] += 1
        if self.dcnt[slot] > 0:
            self._wait(q, ("d", slot, self.dcnt[slot]))
        self.dcnt[slot] += 1
        self.ops[q].append({"kind": "dma", "fn": fn, "slot": slot})
        self._mark(("d", slot, self.dcnt[slot]), R, W)

    def cc(self, fn, R=(), W=()):
        self._deps("pool", R, W)
        self.cc_cnt += 1
        self.ops["pool"].append({"kind": "cc", "fn": fn})
        self._mark(("k", "cc", self.cc_cnt), R, W)

    def emit(self):
        nc = self.nc
        if self.cc_cnt:
            self._wait("pool", ("k", "cc", self.cc_cnt))
        for slot in range(self.NDSEM):
            if self.dcnt[slot] > 0:
                self._wait("sp", ("d", slot, self.dcnt[slot]))
        for e in ENG:
            k = 0
            for o in self.ops[e]:
                if o["kind"] == "op" and o["sig"]:
                    k += 1
                    o["sidx"] = k

        def body(e):
            def f(E):
                for o in self.ops[e]:
                    kd = o["kind"]
                    if kd == "wc":
                        E.wait_ge(self.sem[o["eng"]], o["rec"]["sidx"])
                    elif kd == "wd":
                        E.wait_ge(self.dsem[o["slot"]], 16 * o["v"])
                    elif kd == "wk":
                        E.wait_ge(self.cc_sem, o["v"])
                    elif kd == "cc":
                        o["fn"](E).then_inc(self.cc_sem)
                    elif kd == "op":
                        ins = o["fn"](E)
                        if o["sig"]:
                            ins.then_inc(self.sem[e], 1)
                    else:
                        o["fn"](E).then_inc(self.dsem[o["slot"]], 16)
            return f

        with nc.Block() as blk:
            blk.tensor(body("pe"))
            blk.scalar(body("act"))
            blk.vector(body("dve"))
            blk.gpsimd(body("pool"))
            blk.sync(body("sp"))


class Slots:
    def __init__(self, tiles, name, excl=False):
        self.items = [(t, Buf("%s%d" % (name, i), excl)) for i, t in enumerate(tiles)]
        self.free = list(range(len(tiles)))
        self.name = name

    def take(self):
        if not self.free:
            raise RuntimeError("out of slots: " + self.name)
        i = self.free.pop(0)
        t, b = self.items[i]
        return (i, t, b)

    def give(self, h):
        assert h[0] not in self.free
        self.free.append(h[0])


def build(cfg):
    NL = cfg.get("n_layers", 2)
    NBLK = cfg.get("n_blocks", SEQ // NB)
    dbg = cfg.get("dbg", False)
    parts = cfg.get("parts", ("ffn0", "mix", "ffn1"))
    pipe = cfg.get("pipe", False)
    LW = 1 if pipe else 2
    nc = bass.Bass("TRN2", target_bir_lowering=False)

    def din(name, shape, dt=F32):
        return nc.dram_tensor(name, list(shape), dt, kind="ExternalInput").ap()

    x_d = din("x", [SEQ, D])
    out_d = nc.dram_tensor("out", [SEQ, D], F32, kind="ExternalOutput").ap()
    wg_d = din("ffn_gate", [LW, 2, D, DFF])
    wu_d = din("ffn_up", [LW, 2, D, DFF])
    wd_d = din("ffn_down", [LW, 2, DFF, D])
    win_d = din("w_in", [LW, D, INT])
    bp_d = din("branch_proj", [LW, 3, 512, D])
    wo_d = din("w_out", [LW, D, D])
    glu_d = din("s5_glu_w", [LW, 512, 512])
    pp_d = din("pp", [LW, 128, 128])
    fnw_d = din("fnw", [128, D])
    s5row_d = din("s5row", [LW, 128, 3, 2048])
    s5col_d = din("s5col", [LW, 128, 3, 16])
    s5bt_d = din("s5bt", [LW, 2, 128, 2048])
    s5c_d = din("s5c", [LW, 2, 128, 2048])
    rgw_d = din("rgw", [LW, 2, 128, 512])
    cst_d = din("cst", [128, 512 + NB])
    idb_d = din("idb", [128, 256], BF16)
    role_d = din("role", [128, 2])
    NSC = 48
    wsc_on = pipe and cfg.get("wscratch", True)
    if pipe:
        xsrc_d = nc.dram_tensor("xch_src", [4, NB, 256], F32)
        xdst_d = nc.dram_tensor("xch_dst", [4, 2 * NB, 256], F32)
    if wsc_on:
        wsc_t = nc.dram_tensor("wscratch", [NSC, 128, NM * 256], BF16)
    dbg_d = {}

    def dout(name, shape):
        dbg_d[name] = nc.dram_tensor(name, list(shape), F32, kind="ExternalOutput").ap()
        return dbg_d[name]

    es = ExitStack()
    with es:
        S = Sched(nc, es)

        def sb(name, shape, dt=F32):
            return es.enter_context(nc.sbuf_tensor("sb_" + name, list(shape), dt))

        def act(out, in_, func, R, W, scale=None, bias=None, accum=None):
            kw = {}
            if scale is not None:
                kw["scale"] = scale
            if bias is not None:
                kw["bias"] = bias
            if accum is not None:
                kw["accum_out"] = accum
            S.op("act", lambda E: E.activation(out=out, in_=in_, func=func, **kw), R, W)

        def tt(eng, out, a, b, op, R, W):
            S.op(eng, lambda E: E.tensor_tensor(out=out, in0=a, in1=b, op=op), R, W)

        def ts(eng, out, a, s1, s2, op0, op1, R, W):
            if op1 is None:
                S.op(eng, lambda E: E.tensor_scalar(out=out, in0=a, scalar1=s1, scalar2=None, op0=op0), R, W)
            else:
                S.op(eng, lambda E: E.tensor_scalar(out=out, in0=a, scalar1=s1, scalar2=s2, op0=op0, op1=op1), R, W)

        def stt(out, a, s, b, op0, op1, R, W):
            S.op("dve", lambda E: E.scalar_tensor_tensor(out=out, in0=a, scalar=s, in1=b, op0=op0, op1=op1), R, W)

        def mm(out, lhsT, rhs, start, stop, R, W):
            S.op("pe", lambda E: E.matmul(out, lhsT=lhsT, rhs=rhs, start=start, stop=stop), R, W)

        def tr(out, in_, ident, R, W):
            S.op("pe", lambda E: E.transpose(out, in_, ident), R, W)

        def scan(out, d0, d1, init, R, W):
            S.op("dve", lambda E: E.tensor_tensor_scan(out=out, data0=d0, data1=d1, initial=init,
                                                       op0=ALU.mult, op1=ALU.add), R, W)

        def cp(eng, out, in_, R, W):
            if eng == "act":
                act(out, in_, AF.Copy, R, W)
            else:
                S.op(eng, lambda E: E.tensor_copy(out=out, in_=in_), R, W)

        def recip(out, in_, R, W):
            S.op("dve", lambda E: E.reciprocal(out=out, in_=in_), R, W)

        def memset(eng, ap, val, W):
            S.op(eng, lambda E: E.memset(ap, val), (), W)

        def dma(q, out, in_, R, W):
            S.dma(q, lambda E: E.dma_start(out=out, in_=in_), R, W)

        NF, NH = 24, 32
        Fs = Slots([sb("F%d" % i, [128, 512]) for i in range(NF)], "F")
        Hs = Slots([sb("H%d" % i, [128, 512], BF16) for i in range(NH)], "H")
        PF = Slots([es.enter_context(nc.psum_tensor("pf%d" % i, [128, 512], F32)) for i in range(6)], "pf", True)
        PB = Slots([es.enter_context(nc.psum_tensor("pb%d" % i, [128, 1024], BF16)) for i in range(2)], "pb", True)

        cst = sb("cst", [128, 512 + NB]); b_cst = Buf("cst")
        idb = sb("idb", [128, 256], BF16); b_idb = Buf("idb")
        dma("sp", cst[:], cst_d, (), [b_cst])
        dma("sp", idb[:], idb_d, (), [b_idb])
        mask = cst[0:64, 0:512]
        rst = cst[:, 512:512 + NB]
        identb = idb[:, 0:128]
        onesb = idb[:, 128:256]

        xt = sb("xt", [128, TT, D]); b_x = [Buf("x%d" % t) for t in range(TT)]
        xs = sb("xs", [128, 2, D], BF16); b_xs = [Buf("xs0"), Buf("xs1")]
        sm = sb("sm", [128, 64]); b_sm = Buf("sm")
        pp = sb("pp", [128, 128]); b_pp = Buf("pp")
        role = sb("role", [128, 2]); b_role = Buf("role")
        dma("sp", role[:], role_d, (), [b_role])
        ppx = sb("ppx", [128, 32]); b_ppx = Buf("ppx")
        NWB = 3
        wbt = [sb("wb%d" % i, [128, NM * 256], BF16) for i in range(NWB)]
        b_wb = [Buf("wb%d" % i) for i in range(NWB)]
        wctr = [0]

        def wtake():
            i = wctr[0] % NWB
            wctr[0] += 1
            return i

        cur_it = [0]
        b_wsc = [Buf("wsc%d" % k) for k in range(NSC)]

        def wload(i, n, pieces, cid):
            if (not wsc_on) or cur_it[0] == 0:
                for view, src in pieces:
                    dma("pool", view, src, (), [b_wb[i]])
                if wsc_on:
                    dma("sp", wsc_t.ap()[cid, :, 0:n], wbt[i][:, 0:n], [b_wb[i]], [b_wsc[cid]])
            else:
                dma("sp", wbt[i][:, 0:n], wsc_t.ap()[cid, :, 0:n], [b_wsc[cid]], [b_wb[i]])

        def v_win(i):
            return wbt[i][:, 0:4096].rearrange("p (k n) -> p k n", n=512)

        def v_gu(i, j):
            return wbt[i][:, j * 2048:(j + 1) * 2048].rearrange("p (k n) -> p k n", n=256)

        def v_wd(i):
            return wbt[i][:, :].rearrange("p (m n) -> p m n", n=256)

        def v_bp(i):
            return wbt[i][:, 0:4096].rearrange("p (c n) -> p c n", n=D)
        glub = sb("glub", [128, 4, 512], BF16); b_glub = Buf("glub")
        brT = sb("brT", [128, 16, 128], BF16); biT = sb("biT", [128, 16, 128], BF16)
        crT = sb("crT", [128, 16, 128], BF16); nciT = sb("nciT", [128, 16, 128], BF16)
        b_s5m = Buf("s5m")
        cosT = sb("cosT", [128, 16, TBL]); sinT = sb("sinT", [128, 16, TBL]); b_tab = Buf("tab")
        s5c = sb("s5c", [128, 8, 16]); b_s5c = Buf("s5c")
        s5car = sb("s5car", [128, 2, 16]); b_car = [Buf("car%d" % i) for i in range(4)]
        s5cs = sb("s5cs", [128, 16]); b_s5cs = Buf("s5cs")
        rho0 = sb("rho0", [128, 16, TBL]); b_rho0 = Buf("rho0")
        rgw = sb("rgw", [128, 2, 512], BF16); b_rgw = Buf("rgw")
        xce = sb("xce", [128, 4, 3 + NB]); b_xce = [Buf("xce%d" % i) for i in range(4)]
        hcar = sb("hcar", [128, 4]); b_hcar = Buf("hcar")
        hgS = sb("hgS", [128, 4, 128]); b_hgS = [Buf("hgS%d" % i) for i in range(4)]
        smb8s = [sb("smb8_%d" % i, [128, NCH * 128], BF16) for i in range(2)]; b_smb8s = [Buf("smb8_%d" % i) for i in range(2)]
        vTs = [sb("vT%d" % i, [128, NCH * 128], BF16) for i in range(2)]; b_vTs = [Buf("vT%d" % i) for i in range(2)]
        kdTs = [sb("kdT%d" % i, [128, NCH * 128], BF16) for i in range(2)]; b_kdTs = [Buf("kdT%d" % i) for i in range(2)]
        hss = [sb("hs%d" % i, [128, 4, NCH]) for i in range(4)]; b_hss = [Buf("hs%d" % i) for i in range(4)]

        C_NORM = 0
        C_D = 24; C_GLUB = 28; C_LB0 = 32; C_LB1 = 36; C_HNW = 40
        C_CW = 44
        C_CB = 60; C_BA = 64; C_BX = 68; C_LAM = 72

        xin_v = x_d.rearrange("(b t p) d -> b p t d", p=128, t=TT)
        out_v = out_d.rearrange("(b t p) d -> b p t d", p=128, t=TT)
        b_xh = [Buf("xh%d" % i) for i in range(SEQ // NB)]

        hT = [None] * 8

        def norm_stats(col0):
            for t in range(TT):
                act(xs[:, t % 2, :], xt[:, t, :], AF.Square, [b_x[t]], [b_xs[t % 2], b_sm], accum=sm[:, col0 + t:col0 + t + 1])
            act(sm[:, col0 + 8:col0 + 8 + TT], sm[:, col0:col0 + TT], AF.Sqrt, [b_sm], [b_sm], scale=1.0 / D, bias=EPS)
            recip(sm[:, col0 + 16:col0 + 16 + TT], sm[:, col0 + 8:col0 + 8 + TT], [b_sm], [b_sm])
            return col0 + 16

        def do_norm(ncol):
            nst = cfg.get('norm_stage', 9)
            rc = norm_stats(0)
            hs_ = [Hs.take() for _ in range(8)]
            for tp in range(TT // 2):
                for j in range(2):
                    t = tp * 2 + j
                    if nst >= 1:
                        act(xs[:, j, :], xt[:, t, :], AF.Copy, [b_x[t], b_sm], [b_xs[j]], scale=sm[:, rc + t:rc + t + 1])
                for kq in range(2):
                    pbh = PB.take()
                    for kk in range(4):
                        k = kq * 4 + kk
                        for j in range(2):
                            if nst >= 2:
                                tr(pbh[1][:, (kk * 2 + j) * 128:(kk * 2 + j + 1) * 128], xs[:, j, k * 128:(k + 1) * 128], identb,
                                   [b_xs[j], b_idb], [pbh[2]])
                    for kk in range(4):
                        k = kq * 4 + kk
                        src = pbh[1][:, kk * 256:(kk + 1) * 256]
                        dst = hs_[k][1][:, tp * 256:(tp + 1) * 256]
                        wcol = pp[:, ncol + k:ncol + k + 1]
                        emode = cfg.get('evac_mode', 'mix')
                        if (kk % 2 == 0 and emode == 'mix') or emode == 'act':
                            if nst >= 3:
                                act(dst, src, AF.Copy, [pbh[2], b_pp], [hs_[k][2]], scale=wcol)
                        else:
                            if nst >= 4:
                                ts("dve", dst, src, wcol, None, ALU.mult, None, [pbh[2], b_pp], [hs_[k][2]])
                    PB.give(pbh)
            for k in range(8):
                hT[k] = hs_[k]

        def free_hT():
            for k in range(8):
                Hs.give(hT[k])
                hT[k] = None

        def ffn(l, which, ncol, after_chunk=None):
            do_norm(ncol)
            wg_v = wg_d[l, which].rearrange("(k p) n -> p k n", p=128)
            wu_v = wu_d[l, which].rearrange("(k p) n -> p k n", p=128)
            wd_v = wd_d[l, which].rearrange("(m p) n -> p m n", p=128)
            acts = []
            stage = cfg.get('ffn_stage', 2)
            if stage == 0:
                free_hT()
                return
            for c in range(NM // 2):
                i = wtake()
                wload(i, 4096, [(v_gu(i, 0), wg_v[:, :, c * 256:(c + 1) * 256]),
                                (v_gu(i, 1), wu_v[:, :, c * 256:(c + 1) * 256])], which * 15 + c)
                for mi in range(2):
                    pg = PF.take(); pu = PF.take()
                    for k in range(8):
                        mm(pg[1][:], v_gu(i, 0)[:, k, mi * 128:(mi + 1) * 128], hT[k][1][:], k == 0, k == 7,
                           [b_wb[i], hT[k][2]], [pg[2]])
                    for k in range(8):
                        mm(pu[1][:], v_gu(i, 1)[:, k, mi * 128:(mi + 1) * 128], hT[k][1][:], k == 0, k == 7,
                           [b_wb[i], hT[k][2]], [pu[2]])
                    sg = Fs.take()
                    act(sg[1][:], pg[1][:], AF.Silu, [pg[2]], [sg[2]])
                    a = Hs.take()
                    tt("dve", a[1][:], sg[1][:], pu[1][:], ALU.mult, [sg[2], pu[2]], [a[2]])
                    Fs.give(sg); PF.give(pg); PF.give(pu)
                    acts.append(a)
            free_hT()
            if stage == 1:
                for a in acts:
                    Hs.give(a)
                return
            for c4 in range(4):
                i = wtake()
                wload(i, NM * 256, [(v_wd(i), wd_v[:, :, c4 * 256:(c4 + 1) * 256])], which * 15 + 11 + c4)
                for t in range(TT):
                    pd = PF.take()
                    for m in range(NM):
                        mm(pd[1][:, 0:256], acts[m][1][:, t * 128:(t + 1) * 128], v_wd(i)[:, m, :], m == 0, m == NM - 1,
                           [acts[m][2], b_wb[i]], [pd[2]])
                    xsl = xt[:, t, c4 * 256:(c4 + 1) * 256]
                    stt(xsl, pd[1][:, 0:256], 0.5, xsl, ALU.mult, ALU.add, [pd[2], b_x[t]], [b_x[t]])
                    PF.give(pd)
                if after_chunk is not None:
                    after_chunk(c4)
            for a in acts:
                Hs.give(a)

        def layer_prep(l):
            dma("sp", pp[:], pp_d[l], (), [b_pp])
            if pipe:
                tt("dve", ppx[:, 0:4], pp[:, C_LB1:C_LB1 + 4], pp[:, C_LB0:C_LB0 + 4], ALU.subtract, [b_pp], [b_ppx])
                act(ppx[:, 0:4], ppx[:, 0:4], AF.Sigmoid, [b_ppx], [b_ppx])
                ts("dve", ppx[:, 0:4], ppx[:, 0:4], role[:, 1:2], None, ALU.mult, None, [b_ppx, b_role], [b_ppx])
            elif l == 0:
                ts("dve", ppx[:, 0:4], pp[:, C_LB0:C_LB0 + 4], 0.0, None, ALU.mult, None, [b_pp], [b_ppx])
            else:
                tt("dve", ppx[:, 0:4], pp[:, C_LB1:C_LB1 + 4], pp[:, C_LB0:C_LB0 + 4], ALU.subtract, [b_pp], [b_ppx])
                act(ppx[:, 0:4], ppx[:, 0:4], AF.Sigmoid, [b_ppx], [b_ppx])
            ts("dve", ppx[:, 4:8], ppx[:, 0:4], -1.0, 1.0, ALU.mult, ALU.add, [b_ppx], [b_ppx])
            act(ppx[:, 16:20], pp[:, C_LAM:C_LAM + 4], AF.Exp, [b_pp], [b_ppx], scale=-1.0)
            act(ppx[:, 16:20], ppx[:, 16:20], AF.Ln, [b_ppx], [b_ppx], bias=1.0)
            ts("dve", ppx[:, 8:12], ppx[:, 16:20], -8.0, None, ALU.mult, None, [b_ppx], [b_ppx])
            ts("dve", ppx[:, 12:16], ppx[:, 16:20], -16.0, None, ALU.mult, None, [b_ppx], [b_ppx])
            dma("pool", glub[:], glu_d[l].rearrange("(c p) n -> p c n", p=128), (), [b_glub])
            for j in range(2):
                dma("pool", rgw[:, j, :], rgw_d[l, j], (), [b_rgw])
            memset("pool", s5car[:], 0.0, b_car)
            memset("pool", hcar[:], 0.0, [b_hcar])
            for i in range(4):
                memset("pool", hgS[:, i, :], 0.0, [b_hgS[i]])
                memset("pool", xce[:, i, 0:3], 0.0, [b_xce[i]])
            if cfg.get('s5prep', True):
                s5_prep(l)

        def trig_chain(n, th, out_c, out_s, f, R, W, width):
            c, s_, t1, t2 = f
            act(s_, th, AF.Sin, R, W, scale=1.0 / 32)
            act(c, th, AF.Sin, R, W, scale=-1.0 / 32, bias=math.pi / 2)
            for i in range(5):
                tt("dve", t1, c, c, ALU.mult, W, W)
                tt("dve", t2, s_, s_, ALU.mult, W, W)
                stt(s_, s_, 2.0, c, ALU.mult, ALU.mult, W, W)
                tt("dve", c, t1, t2, ALU.subtract, W, W)
            return c, s_

        def s5_prep(l):
            dma("sp", s5c[:, 0:3, :], s5col_d[l], (), [b_s5c])
            W = [b_s5c]
            lr = s5c[:, 0, :]; li = s5c[:, 1, :]; dt = s5c[:, 2, :]
            act(dt, dt, AF.Exp, W, W)
            ts("dve", lr, lr, -1e-4, None, ALU.min, None, W, W)
            tt("dve", s5c[:, 3, :], lr, dt, ALU.mult, W, W)
            act(s5c[:, 3, :], s5c[:, 3, :], AF.Exp, W, W)
            tt("dve", s5c[:, 4, :], li, dt, ALU.mult, W, W)
            trig_chain(16, s5c[:, 4, :], None, None, [s5c[:, 5, :], s5c[:, 6, :], s5c[:, 7, :], s5c[:, 0, :]], W, W, 16)
            cp("dve", rho0[:], s5c[:, 3, :].unsqueeze(2).broadcast_to([128, 16, TBL]), W, [b_rho0])
            memset("dve", rho0[:, :, 0:1], 0.0, [b_rho0])
            WT = [b_tab]
            cp("dve", cosT[:, :, 0:1], s5c[:, 5, :].unsqueeze(2), W, WT)
            cp("dve", sinT[:, :, 0:1], s5c[:, 6, :].unsqueeze(2), W, WT)
            n = 1
            while n < TBL:
                for h0 in range(0, 16, 8):
                    if n * 8 > 512:
                        raise RuntimeError("tbl")
                    fa = Fs.take(); fb = Fs.take()
                    sl = slice(h0, h0 + 8)
                    cn = cosT[:, sl, n - 1:n].broadcast_to([128, 8, n])
                    sn = sinT[:, sl, n - 1:n].broadcast_to([128, 8, n])
                    c0 = cosT[:, sl, 0:n]; s0 = sinT[:, sl, 0:n]
                    ta = fa[1][:, 0:8 * n].rearrange("p (a b) -> p a b", b=n)
                    tb = fb[1][:, 0:8 * n].rearrange("p (a b) -> p a b", b=n)
                    tt("dve", ta, c0, cn, ALU.mult, WT, [fa[2]])
                    tt("dve", tb, s0, sn, ALU.mult, WT, [fb[2]])
                    tt("dve", cosT[:, sl, n:2 * n], ta, tb, ALU.subtract, [fa[2], fb[2]], WT)
                    tt("dve", ta, s0, cn, ALU.mult, WT, [fa[2]])
                    tt("dve", tb, c0, sn, ALU.mult, WT, [fb[2]])
                    tt("dve", sinT[:, sl, n:2 * n], ta, tb, ALU.add, [fa[2], fb[2]], WT)
                    Fs.give(fa); Fs.give(fb)
                n *= 2
            WM = [b_s5m]
            for cq in range(4):
                cs = slice(cq * 512, (cq + 1) * 512)
                f = [Fs.take() for _ in range(10)]
                A = [h[1][:] for h in f]
                Bf = [h[2] for h in f]
                lr, li, dt, rho, th, c, s_, t1, t2, t3 = A
                dma("sp", lr, s5row_d[l, :, 0, cs], (), [Bf[0]])
                dma("sp", li, s5row_d[l, :, 1, cs], (), [Bf[1]])
                dma("sp", dt, s5row_d[l, :, 2, cs], (), [Bf[2]])
                act(dt, dt, AF.Exp, Bf, Bf)
                ts("dve", lr, lr, -1e-4, None, ALU.min, None, Bf, Bf)
                tt("dve", rho, lr, dt, ALU.mult, Bf, Bf)
                act(rho, rho, AF.Exp, Bf, Bf)
                tt("dve", th, li, dt, ALU.mult, Bf, Bf)
                trig_chain(512, th, None, None, [c, s_, t1, t2], Bf, Bf, 512)
                tt("dve", c, c, rho, ALU.mult, Bf, Bf)
                ts("dve", c, c, -1.0, None, ALU.add, None, Bf, Bf)
                tt("dve", s_, s_, rho, ALU.mult, Bf, Bf)
                tt("dve", t1, lr, lr, ALU.mult, Bf, Bf)
                tt("dve", t2, li, li, ALU.mult, Bf, Bf)
                tt("dve", t1, t1, t2, ALU.add, Bf, Bf)
                recip(t1, t1, Bf, Bf)
                tt("dve", t2, c, lr, ALU.mult, Bf, Bf)
                tt("dve", t3, s_, li, ALU.mult, Bf, Bf)
                tt("dve", t2, t2, t3, ALU.add, Bf, Bf)
                tt("dve", t2, t2, t1, ALU.mult, Bf, Bf)
                tt("dve", t3, s_, lr, ALU.mult, Bf, Bf)
                tt("dve", th, c, li, ALU.mult, Bf, Bf)
                tt("dve", t3, t3, th, ALU.subtract, Bf, Bf)
                tt("dve", t3, t3, t1, ALU.mult, Bf, Bf)
                fr, fi = t2, t3
                dma("sp", lr, s5bt_d[l, 0, :, cs], Bf, [Bf[0]])
                dma("sp", li, s5bt_d[l, 1, :, cs], Bf, [Bf[1]])
                st4 = slice(cq * 4, cq * 4 + 4)
                tt("dve", dt, fr, lr, ALU.mult, Bf, Bf)
                tt("dve", rho, fi, li, ALU.mult, Bf, Bf)
                tt("dve", brT[:, st4, :].rearrange("p a b -> p (a b)"), dt, rho, ALU.subtract, Bf, WM)
                tt("dve", dt, fr, li, ALU.mult, Bf, Bf)
                tt("dve", rho, fi, lr, ALU.mult, Bf, Bf)
                tt("dve", biT[:, st4, :].rearrange("p a b -> p (a b)"), dt, rho, ALU.add, Bf, WM)
                dma("sp", c, s5c_d[l, 0, :, cs], Bf, [Bf[5]])
                dma("sp", s_, s5c_d[l, 1, :, cs], Bf, [Bf[6]])
                cp("dve", crT[:, st4, :].rearrange("p a b -> p (a b)"), c, Bf, WM)
                ts("dve", nciT[:, st4, :].rearrange("p a b -> p (a b)"), s_, -1.0, None, ALU.mult, None, Bf, WM)
                for h in f:
                    Fs.give(h)

        def load_win(l, chunk):
            i = wtake()
            wv = win_d[l].rearrange("(k p) n -> p k n", p=128)
            wload(i, 4096, [(v_win(i), wv[:, :, chunk * 512:(chunk + 1) * 512])], 30 + chunk)
            return i

        def proj(i, ct):
            p = PF.take()
            for k in range(8):
                mm(p[1][:], v_win(i)[:, k, ct * 128:(ct + 1) * 128], hT[k][1][:], k == 0, k == 7,
                   [b_wb[i], hT[k][2]], [p[2]])
            return p

        def s5_branch(l, blk):
            wi0 = load_win(l, 0)
            uf, ub = [], []
            for ct in range(4):
                p = proj(wi0, ct)
                f = Fs.take(); h = Hs.take()
                act(f[1][:], p[1][:], AF.Copy, [p[2]], [f[2]])
                cp("dve", h[1][:], p[1][:], [p[2]], [h[2]])
                PF.give(p)
                uf.append(f); ub.append(h)
            zf, zb = [None] * 4, [None] * 4
            M = {k: Fs.take() for k in ("m1", "m2", "m3", "m4", "bpr", "bpi", "t1", "t2", "t3", "t4")}
            Wb = [{k: Fs.take() for k in ("wr", "wi")} for _ in range(2)]
            Tb = [{k: Hs.take() for k in ("t1", "t2", "t3", "t4")} for _ in range(2)]

            def v3(h):
                return h[1][:].rearrange("p (a b) -> p a b", b=TBL)
            iters = [(ct, tb) for ct in range(4) for tb in range(NB // TBL)]
            NTB = NB // TBL
            bu = {}
            pyh = {}

            def issue_bu(i):
                ct, tb = iters[i]
                tsl = slice(tb * TBL, (tb + 1) * TBL)
                pbr = PF.take(); pbi = PF.take()
                for j in range(4):
                    st = ct * 4 + j
                    mm(pbr[1][:, j * TBL:(j + 1) * TBL], brT[:, st, :], ub[ct][1][:, tsl], True, True,
                       [b_s5m, ub[ct][2]], [pbr[2]])
                    mm(pbi[1][:, j * TBL:(j + 1) * TBL], biT[:, st, :], ub[ct][1][:, tsl], True, True,
                       [b_s5m, ub[ct][2]], [pbi[2]])
                bu[i] = (pbr, pbi)

            def rot_in(i):
                ct, tb = iters[i]
                st4 = slice(ct * 4, ct * 4 + 4)
                cT = cosT[:, st4, :]; sT = sinT[:, st4, :]
                pbr, pbi = bu.pop(i)
                tt("dve", v3(M["m1"]), v3(pbr), cT, ALU.mult, [pbr[2], b_tab], [M["m1"][2]])
                tt("dve", v3(M["m2"]), v3(pbi), sT, ALU.mult, [pbi[2], b_tab], [M["m2"][2]])
                tt("pool", M["bpr"][1][:], M["m1"][1][:], M["m2"][1][:], ALU.add, [M["m1"][2], M["m2"][2]], [M["bpr"][2]])
                tt("dve", v3(M["m3"]), v3(pbi), cT, ALU.mult, [pbi[2], b_tab], [M["m3"][2]])
                tt("dve", v3(M["m4"]), v3(pbr), sT, ALU.mult, [pbr[2], b_tab], [M["m4"][2]])
                tt("pool", M["bpi"][1][:], M["m3"][1][:], M["m4"][1][:], ALU.subtract, [M["m3"][2], M["m4"][2]], [M["bpi"][2]])
                PF.give(pbr); PF.give(pbi)

            def scans(i):
                ct, tb = iters[i]
                st4 = slice(ct * 4, ct * 4 + 4)
                Wc = Wb[i % 2]
                for nm, src, ri in (("wr", "bpr", 0), ("wi", "bpi", 1)):
                    o = ri * 8
                    tt("dve", s5cs[:, o:o + 4], s5c[:, 3, st4], s5car[:, ri, st4], ALU.mult, [b_s5c, b_car[ct]], [b_s5cs])
                    first = v3(M[src])[:, :, 0:1]
                    tt("dve", first, first, s5cs[:, o:o + 4].unsqueeze(2), ALU.add, [M[src][2], b_s5cs], [M[src][2]])
                    scan(Wc[nm][1][:], rho0[:, st4, :].rearrange("p a b -> p (a b)"), M[src][1][:], 0.0,
                         [b_rho0, M[src][2]], [Wc[nm][2]])

            def rot_out(i):
                ct, tb = iters[i]
                tsl = slice(tb * TBL, (tb + 1) * TBL)
                st4 = slice(ct * 4, ct * 4 + 4)
                cT = cosT[:, st4, :]; sT = sinT[:, st4, :]
                Wc = Wb[i % 2]; Tc = Tb[i % 2]
                L = slice(TBL - 1, TBL)
                tt("dve", v3(M["t1"]), v3(Wc["wr"]), cT, ALU.mult, [Wc["wr"][2], b_tab], [M["t1"][2]])
                tt("dve", v3(M["t2"]), v3(Wc["wi"]), sT, ALU.mult, [Wc["wi"][2], b_tab], [M["t2"][2]])
                tt("pool", v3(M["t3"]), v3(Wc["wr"]), sT, ALU.mult, [Wc["wr"][2], b_tab], [M["t3"][2]])
                tt("pool", v3(M["t4"]), v3(Wc["wi"]), cT, ALU.mult, [Wc["wi"][2], b_tab], [M["t4"][2]])
                tt("dve", s5car[:, 0, st4].unsqueeze(2), v3(M["t1"])[:, :, L], v3(M["t2"])[:, :, L], ALU.subtract,
                   [M["t1"][2], M["t2"][2]], [b_car[ct]])
                tt("dve", s5cs[:, 4:8].unsqueeze(2), v3(Wc["wr"])[:, :, L], sT[:, :, L], ALU.mult, [Wc["wr"][2], b_tab], [b_s5cs])
                tt("dve", s5cs[:, 12:16].unsqueeze(2), v3(Wc["wi"])[:, :, L], cT[:, :, L], ALU.mult, [Wc["wi"][2], b_tab], [b_s5cs])
                tt("dve", s5car[:, 1, st4], s5cs[:, 4:8], s5cs[:, 12:16], ALU.add, [b_s5cs], [b_car[ct]])
                act(Tc["t1"][1][:], M["t1"][1][:], AF.Copy, [M["t1"][2]], [Tc["t1"][2]])
                act(Tc["t2"][1][:], M["t2"][1][:], AF.Copy, [M["t2"][2]], [Tc["t2"][2]], scale=-1.0)
                act(Tc["t3"][1][:], M["t3"][1][:], AF.Copy, [M["t3"][2]], [Tc["t3"][2]])
                act(Tc["t4"][1][:], M["t4"][1][:], AF.Copy, [M["t4"][2]], [Tc["t4"][2]])
                if tb == 0:
                    pyh[ct] = PF.take()
                py = pyh[ct]
                n = 0
                for j in range(4):
                    st = ct * 4 + j
                    js = slice(j * TBL, (j + 1) * TBL)
                    for nm, wT in (("t1", crT), ("t2", crT), ("t3", nciT), ("t4", nciT)):
                        mm(py[1][:, tsl], wT[:, st, :], Tc[nm][1][:, js], n == 0, n == 15, [b_s5m, Tc[nm][2]], [py[2]])
                        n += 1
                if tb == NTB - 1:
                    yf = Fs.take()
                    stt(yf[1][:], uf[ct][1][:], pp[:, C_D + ct:C_D + ct + 1], py[1][:], ALU.mult, ALU.add,
                        [uf[ct][2], b_pp, py[2]], [yf[2]])
                    PF.give(py)
                    act(yf[1][:], yf[1][:], AF.Gelu_apprx_tanh, [yf[2]], [yf[2]])
                    zh = Hs.take()
                    cp("pool", zh[1][:], yf[1][:], [yf[2]], [zh[2]])
                    zf[ct] = yf; zb[ct] = zh
                    if dbg and blk == 0:
                        dma("sp", dbg_d["dbg_z"][ct * 128:(ct + 1) * 128, :], yf[1][:], [yf[2]], [])

            issue_bu(0)
            for i in range(len(iters)):
                if i + 1 < len(iters):
                    issue_bu(i + 1)
                rot_in(i)
                if i > 0:
                    rot_out(i - 1)
                scans(i)
            rot_out(len(iters) - 1)
            for k in M.values():
                Fs.give(k)
            for d_ in Wb:
                for k in d_.values():
                    Fs.give(k)
            for d_ in Tb:
                for k in d_.values():
                    Hs.give(k)
            for ct in range(4):
                Fs.give(uf[ct]); Hs.give(ub[ct])
            ya = []
            for ct in range(4):
                pg = PF.take()
                for c2 in range(4):
                    mm(pg[1][:], glub[:, c2, ct * 128:(ct + 1) * 128], zb[c2][1][:], c2 == 0, c2 == 3,
                       [b_glub, zb[c2][2]], [pg[2]])
                sg = Fs.take()
                act(sg[1][:], pg[1][:], AF.Sigmoid, [pg[2], b_pp], [sg[2]], bias=pp[:, C_GLUB + ct:C_GLUB + ct + 1])
                PF.give(pg)
                y = Hs.take()
                tt("dve", y[1][:], zf[ct][1][:], sg[1][:], ALU.mult, [zf[ct][2], sg[2]], [y[2]])
                Fs.give(sg)
                ya.append(y)
            for ct in range(4):
                Fs.give(zf[ct]); Hs.give(zb[ct])
            return ya

        def hg_branch(l, blk):
            qs, sg_, vb, gs = [], [], [], []
            for chunk, lst, kind in ((1, qs, "silu"), (2, sg_, "sig"), (3, vb, "v"), (4, gs, "silu")):
                i = load_win(l, chunk)
                for hd in range(4):
                    p = proj(i, hd)
                    if kind == "v":
                        h = Hs.take()
                        cp("dve", h[1][:], p[1][:], [p[2]], [h[2]])
                        lst.append(h)
                    else:
                        f = Fs.take()
                        act(f[1][:], p[1][:], AF.Silu if kind == "silu" else AF.Sigmoid, [p[2]], [f[2]])
                        lst.append(f)
                    PF.give(p)
            yb = [None] * 4

            def head_gen(hd, si):
                vT = vTs[si]; kdT = kdTs[si]; smb8 = smb8s[si]; hs = hss[hd]
                b_vT = b_vTs[si]; b_kdT = b_kdTs[si]; b_smb8 = b_smb8s[si]; b_hs = b_hss[hd]
                ff = Fs.take(); lf = Fs.take(); kk = Fs.take()
                ts("dve", ff[1][:], sg_[hd][1][:], ppx[:, 4 + hd:5 + hd], ppx[:, hd:hd + 1], ALU.mult, ALU.add,
                   [sg_[hd][2], b_ppx], [ff[2]])
                Fs.give(sg_[hd])
                act(lf[1][:], ff[1][:], AF.Ln, [ff[2]], [lf[2]])
                ts("pool", kk[1][:], ff[1][:], -1.0, 1.0, ALU.mult, ALU.add, [ff[2]], [kk[2]])
                yield
                bb = ff
                scan(bb[1][:], rst, lf[1][:], 0.0, [b_cst, lf[2], kk[2]], [bb[2]])
                bv = bb[1][:].rearrange("p (c t) -> p c t", t=64)
                bm = lf
                tt("dve", bm[1][:].rearrange("p (c t) -> p c t", t=64), bv, bv[:, :, 31:32].broadcast_to([128, NCH, 64]),
                   ALU.subtract, [bb[2]], [bm[2]])
                yield
                act(hs[:, 0, :].unsqueeze(2), bv[:, :, 31:32], AF.Exp, [bb[2]], [b_hs])
                act(hs[:, 1, :].unsqueeze(2), bv[:, :, 63:64], AF.Exp, [bb[2]], [b_hs])
                tt("dve", hs[:, 3, :].unsqueeze(2), bv[:, :, 63:64], bv[:, :, 31:32], ALU.subtract, [bb[2]], [b_hs])
                act(hs[:, 2, :], hs[:, 3, :], AF.Exp, [b_hs], [b_hs])
                e1 = Fs.take(); e2 = bb
                act(e1[1][:], bm[1][:], AF.Exp, [bm[2]], [e1[2]])
                act(e2[1][:], bm[1][:], AF.Exp, [bm[2], b_hs], [e2[2]], scale=-1.0)
                yield
                qd = Hs.take(); kd = Hs.take()
                tt("pool", qd[1][:], qs[hd][1][:], e1[1][:], ALU.mult, [qs[hd][2], e1[2]], [qd[2]])
                tt("dve", kd[1][:], kk[1][:], e2[1][:], ALU.mult, [kk[2], e2[2]], [kd[2]])
                Fs.give(qs[hd]); Fs.give(e1); Fs.give(e2); Fs.give(kk); Fs.give(lf)
                yield
                psc = PF.take()
                for c in range(NCH):
                    cs = slice(c * 64, (c + 1) * 64)
                    mm(psc[1][0:64, cs], kd[1][:, cs], qd[1][:, cs], True, True, [kd[2], qd[2]], [psc[2]])
                scm = Hs.take()
                tt("dve", scm[1][0:64, :], psc[1][0:64, :], mask, ALU.mult, [psc[2], b_cst], [scm[2]])
                PF.give(psc)
                yield
                for src, dstT, bd in ((vb[hd], vT, b_vT), (kd, kdT, b_kdT)):
                    pt = PB.take()
                    for c in range(NCH):
                        tr(pt[1][0:64, c * 128:(c + 1) * 128], src[1][:, c * 64:(c + 1) * 64], identb,
                           [src[2], b_idb], [pt[2]])
                    act(dstT[0:64, :], pt[1][0:64, 0:NCH * 128], AF.Copy, [pt[2]], [bd])
                    PB.give(pt)
                Hs.give(vb[hd])
                yield
                pkv = [PF.take(), PF.take()]
                for c in range(NCH):
                    c128 = slice(c * 128, (c + 1) * 128)
                    pk = pkv[c // 4]
                    mm(pk[1][:, (c % 4) * 128:(c % 4 + 1) * 128], kdT[0:64, c128], vT[0:64, c128], True, True,
                       [b_kdT, b_vT], [pk[2]])
                yield
                t_ = Fs.take()
                for c in range(NCH):
                    pk = pkv[c // 4]
                    ts("dve", smb8[:, c * 128:(c + 1) * 128], hgS[:, hd, :], hs[:, 0, c:c + 1], None, ALU.mult, None,
                       [b_hgS[hd], b_hs], [b_smb8])
                    ts("dve", t_[1][:, 0:128], pk[1][:, (c % 4) * 128:(c % 4 + 1) * 128], hs[:, 2, c:c + 1], None, ALU.mult, None,
                       [pk[2], b_hs], [t_[2]])
                    stt(hgS[:, hd, :], hgS[:, hd, :], hs[:, 1, c:c + 1], t_[1][:, 0:128], ALU.mult, ALU.add,
                        [b_hgS[hd], b_hs, t_[2]], [b_hgS[hd]])
                Fs.give(t_)
                PF.give(pkv[0]); PF.give(pkv[1])
                yield
                po = PF.take()
                for c in range(NCH):
                    cs = slice(c * 64, (c + 1) * 64)
                    c128 = slice(c * 128, (c + 1) * 128)
                    mm(po[1][:, cs], vT[0:64, c128], scm[1][0:64, cs], True, False, [b_vT, scm[2]], [po[2]])
                    mm(po[1][:, cs], smb8[:, c128], qd[1][:, cs], False, True, [b_smb8, qd[2]], [po[2]])
                Hs.give(qd); Hs.give(kd); Hs.give(scm)
                yield
                if dbg and blk == 0:
                    od = Fs.take()
                    cp("dve", od[1][:], po[1][:], [po[2]], [od[2]])
                    dma("sp", dbg_d["dbg_o"][hd * 128:(hd + 1) * 128, :], od[1][:], [od[2]], [])
                    Fs.give(od)
                sq = Hs.take()
                act(sq[1][:], po[1][:], AF.Square, [po[2]], [sq[2]])
                pss = PF.take()
                mm(pss[1][:], onesb, sq[1][:], True, True, [b_idb, sq[2]], [pss[2]])
                Hs.give(sq)
                yield
                sr = Fs.take()
                act(sr[1][:], pss[1][:], AF.Ln, [pss[2]], [sr[2]], scale=1.0 / 128, bias=EPS)
                PF.give(pss)
                act(sr[1][:], sr[1][:], AF.Exp, [sr[2]], [sr[2]], scale=-0.5)
                tt("dve", sr[1][:], po[1][:], sr[1][:], ALU.mult, [po[2], sr[2]], [sr[2]])
                PF.give(po)
                y = Hs.take()
                stt(y[1][:], sr[1][:], pp[:, C_HNW + hd:C_HNW + hd + 1], gs[hd][1][:], ALU.mult, ALU.mult,
                    [sr[2], b_pp, gs[hd][2]], [y[2]])
                Fs.give(sr); Fs.give(gs[hd])
                yb[hd] = y
            HG_LAG = cfg.get("hg_lag", 5)
            pending = [(0, head_gen(0, 0)), (0, head_gen(1, 1)), (HG_LAG, head_gen(2, 0)), (HG_LAG, head_gen(3, 1))]
            alive = []
            step = 0
            while pending or alive:
                while pending and pending[0][0] <= step:
                    alive.append(pending.pop(0)[1])
                for g in list(alive):
                    try:
                        next(g)
                    except StopIteration:
                        alive.remove(g)
                step += 1
            return yb

        def rg_branch(l, blk):
            wi5 = load_win(l, 5)
            for ct in range(4):
                p = proj(wi5, ct)
                act(xce[:, ct, 3:3 + NB], p[1][:], AF.Copy, [p[2]], [b_xce[ct]])
                PF.give(p)
            wi6 = load_win(l, 6)
            gg = []
            for ct in range(4):
                p = proj(wi6, ct)
                f = Fs.take()
                act(f[1][:], p[1][:], AF.Gelu_apprx_tanh, [p[2]], [f[2]])
                PF.give(p)
                gg.append(f)
            yc = [None] * 4

            def ct_gen(ct):
                xc = Fs.take()
                cw = lambda i: pp[:, C_CW + i * 4 + ct:C_CW + i * 4 + ct + 1]
                ts("dve", xc[1][:], xce[:, ct, 0:NB], cw(0), pp[:, C_CB + ct:C_CB + ct + 1], ALU.mult, ALU.add,
                   [b_xce[ct], b_pp], [xc[2]])
                for i in range(1, 4):
                    stt(xc[1][:], xce[:, ct, i:i + NB], cw(i), xc[1][:], ALU.mult, ALU.add,
                        [b_xce[ct], b_pp, xc[2]], [xc[2]])
                cp("pool", xce[:, ct, 0:3], xce[:, ct, NB:NB + 3], [b_xce[ct]], [b_xce[ct]])
                yield
                xcb = Hs.take()
                cp("pool", xcb[1][:], xc[1][:], [xc[2]], [xcb[2]])
                pr = PF.take(); pi_ = PF.take()
                mm(pr[1][:], rgw[:, 0, ct * 128:(ct + 1) * 128], xcb[1][:], True, True, [b_rgw, xcb[2]], [pr[2]])
                mm(pi_[1][:], rgw[:, 1, ct * 128:(ct + 1) * 128], xcb[1][:], True, True, [b_rgw, xcb[2]], [pi_[2]])
                Hs.give(xcb)
                r = Fs.take(); ii = Fs.take(); a = Fs.take()
                act(r[1][:], pr[1][:], AF.Sigmoid, [pr[2], b_pp], [r[2]], bias=pp[:, C_BA + ct:C_BA + ct + 1])
                act(ii[1][:], pi_[1][:], AF.Sigmoid, [pi_[2], b_pp], [ii[2]], bias=pp[:, C_BX + ct:C_BX + ct + 1])
                PF.give(pr); PF.give(pi_)
                yield
                act(a[1][:], r[1][:], AF.Exp, [r[2], b_ppx], [a[2]], scale=ppx[:, 8 + ct:9 + ct])
                act(r[1][:], r[1][:], AF.Exp, [r[2], b_ppx], [r[2]], scale=ppx[:, 12 + ct:13 + ct])
                ts("dve", r[1][:], r[1][:], 1.0, -1.0, ALU.min, ALU.mult, [r[2]], [r[2]])
                act(r[1][:], r[1][:], AF.Sqrt, [r[2]], [r[2]], bias=1.0)
                yield
                tt("pool", ii[1][:], ii[1][:], xc[1][:], ALU.mult, [ii[2], xc[2]], [ii[2]])
                tt("dve", ii[1][:], ii[1][:], r[1][:], ALU.mult, [ii[2], r[2]], [ii[2]])
                yield
                scan(xc[1][:], a[1][:], ii[1][:], hcar[:, ct:ct + 1], [a[2], ii[2], b_hcar], [xc[2]])
                cp("dve", hcar[:, ct:ct + 1], xc[1][:, NB - 1:NB], [xc[2]], [b_hcar])
                if dbg and blk == 0:
                    dma("sp", dbg_d["dbg_h"][ct * 128:(ct + 1) * 128, :], xc[1][:], [xc[2]], [])
                y = Hs.take()
                tt("dve", y[1][:], xc[1][:], gg[ct][1][:], ALU.mult, [xc[2], gg[ct][2]], [y[2]])
                Fs.give(r); Fs.give(ii); Fs.give(a); Fs.give(xc); Fs.give(gg[ct])
                yc[ct] = y
            alive = [ct_gen(ct) for ct in range(4)]
            while alive:
                for g in list(alive):
                    try:
                        next(g)
                    except StopIteration:
                        alive.remove(g)
            return yc

        def mixer(l, blk, ncol):
            do_norm(ncol)
            only = cfg.get('only_branch')
            if only is not None:
                fn = {'s5': s5_branch, 'hg': hg_branch, 'rg': rg_branch}[only]
                yy = fn(l, blk)
                if dbg and blk == 0:
                    bidx = {'s5': 0, 'hg': 1, 'rg': 2}[only]
                    for ct in range(4):
                        f = Fs.take()
                        cp("dve", f[1][:], yy[ct][1][:], [yy[ct][2]], [f[2]])
                        dma("sp", dbg_d["dbg_y"][bidx, ct * 128:(ct + 1) * 128, :], f[1][:], [f[2]], [])
                        Fs.give(f)
                for h in yy:
                    Hs.give(h)
                free_hT()
                return
            ys = [s5_branch(l, blk), hg_branch(l, blk), rg_branch(l, blk)]
            if dbg and blk == 0:
                for b in range(3):
                    for ct in range(4):
                        f = Fs.take()
                        cp("dve", f[1][:], ys[b][ct][1][:], [ys[b][ct][2]], [f[2]])
                        dma("sp", dbg_d["dbg_y"][b, ct * 128:(ct + 1) * 128, :], f[1][:], [f[2]], [])
                        Fs.give(f)
            macc = [Fs.take() for _ in range(8)]
            mg = [Hs.take() for _ in range(8)]
            for b in range(3):
                bi = wtake()
                wload(bi, 4096, [(v_bp(bi), bp_d[l, b].rearrange("(c p) n -> p c n", p=128))], 43 + b)
                for half in range(2):
                    i = load_win(l, 7 + 2 * b + half)
                    for f4 in range(4):
                        ft = half * 4 + f4
                        pup = PF.take()
                        for ct in range(4):
                            mm(pup[1][:], v_bp(bi)[:, ct, ft * 128:(ft + 1) * 128], ys[b][ct][1][:], ct == 0, ct == 3,
                               [b_wb[bi], ys[b][ct][2]], [pup[2]])
                        pgt = proj(i, f4)
                        sg = Fs.take()
                        act(sg[1][:], pgt[1][:], AF.Sigmoid, [pgt[2]], [sg[2]])
                        PF.give(pgt)
                        if b == 0:
                            tt("dve", macc[ft][1][:], sg[1][:], pup[1][:], ALU.mult, [sg[2], pup[2]], [macc[ft][2]])
                        else:
                            tt("dve", sg[1][:], sg[1][:], pup[1][:], ALU.mult, [sg[2], pup[2]], [sg[2]])
                            if b == 1:
                                tt("dve", macc[ft][1][:], macc[ft][1][:], sg[1][:], ALU.add, [macc[ft][2], sg[2]], [macc[ft][2]])
                            else:
                                tt("dve", mg[ft][1][:], macc[ft][1][:], sg[1][:], ALU.add, [macc[ft][2], sg[2]], [mg[ft][2]])
                        PF.give(pup); Fs.give(sg)
                for ct in range(4):
                    Hs.give(ys[b][ct])
            for f in macc:
                Fs.give(f)
            free_hT()
            if cfg.get('merge_stage', 9) < 2:
                for h in mg:
                    Hs.give(h)
                return
            wov = wo_d[l].rearrange("(k p) n -> p k n", p=128)
            wis = []
            for half in range(2):
                i = wtake()
                wload(i, 4096, [(v_win(i), wov[:, :, half * 512:(half + 1) * 512])], 46 + half)
                wis.append(i)
            for t in range(TT):
                for half in range(2):
                    i = wis[half]
                    po = PF.take()
                    for ft in range(8):
                        mm(po[1][:], mg[ft][1][:, t * 128:(t + 1) * 128], v_win(i)[:, ft, :], ft == 0, ft == 7,
                           [mg[ft][2], b_wb[i]], [po[2]])
                    xsl = xt[:, t, half * 512:(half + 1) * 512]
                    stt(xsl, po[1][:], 1.0, xsl, ALU.mult, ALU.add, [po[2], b_x[t]], [b_x[t]])
                    PF.give(po)
            for h in mg:
                Hs.give(h)

        if dbg:
            dout("dbg_x1", [NB, D]); dout("dbg_x2", [NB, D])
            dout("dbg_y", [3, 512, NB]); dout("dbg_z", [512, NB]); dout("dbg_o", [512, NB]); dout("dbg_h", [512, NB])

        def dump_x(name, blk):
            if dbg and blk == 0:
                dv = dbg_d[name].rearrange("(t p) d -> p t d", p=128)
                for t in range(TT):
                    dma("sp", dv[:, t, :], xt[:, t, :], [b_x[t]], [])

        def final_norm_store(dst_blk, b_dst):
            rc = norm_stats(32)
            fw = [Fs.take(), Fs.take()]
            for hf in range(2):
                dma("sp", fw[hf][1][:], fnw_d[:, hf * 512:(hf + 1) * 512], (), [fw[hf][2]])
            for t in range(TT):
                for hf in range(2):
                    o = Fs.take()
                    act(o[1][:], xt[:, t, hf * 512:(hf + 1) * 512], AF.Copy, [b_x[t], b_sm], [o[2]],
                        scale=sm[:, rc + t:rc + t + 1])
                    tt("dve", o[1][:], o[1][:], fw[hf][1][:], ALU.mult, [o[2], fw[hf][2]], [o[2]])
                    dma("sp", dst_blk[:, t, hf * 512:(hf + 1) * 512], o[1][:], [o[2]], [b_dst])
                    Fs.give(o)
            Fs.give(fw[0]); Fs.give(fw[1])

        if pipe:
            b_srcs = [Buf("xch_src%d" % i) for i in range(4)]; b_dsts = [Buf("xch_dst%d" % i) for i in range(4)]
            src_v = [xsrc_d.ap()[c].rearrange("(t p) d -> p t d", p=128) for c in range(4)]
            dst_v = [xdst_d.ap()[c].rearrange("(r t p) d -> r p t d", p=128, t=TT) for c in range(4)]

            def handoff(c4):
                for t in range(TT):
                    dma("pool", src_v[c4][:, t, :], xt[:, t, c4 * 256:(c4 + 1) * 256], [b_x[t]], [b_srcs[c4]])
                S.cc(lambda E: E.collective_compute("AllGather", ALU.bypass, replica_groups=groups,
                                                    ins=[xsrc_d.ap()[c4].opt()], outs=[xdst_d.ap()[c4].opt()]),
                     [b_srcs[c4]], [b_dsts[c4]])
            groups = [[2 * i, 2 * i + 1] for i in range(4)]
            layer_prep(0)
            NIT = NBLK + 1
            for j in range(NIT):
                cur_it[0] = j
                for t in range(TT):
                    if j < NBLK:
                        dma("sp", xt[:, t, :], xin_v[j][:, t, :], [], [b_x[t]])
                        ts("dve", xt[:, t, :], xt[:, t, :], role[:, 0:1], None, ALU.mult, None, [b_x[t], b_role], [b_x[t]])
                    for c4 in range(4):
                        if j == 0:
                            continue
                        e = Fs.take()
                        dma("sp", e[1][:, 0:256], dst_v[c4][0][:, t, :], [b_dsts[c4]], [e[2]])
                        xsl = xt[:, t, c4 * 256:(c4 + 1) * 256]
                        if j < NBLK:
                            stt(xsl, e[1][:, 0:256], role[:, 1:2], xsl, ALU.mult, ALU.add, [e[2], b_role, b_x[t]], [b_x[t]])
                        else:
                            ts("dve", xsl, e[1][:, 0:256], role[:, 1:2], None, ALU.mult, None, [e[2], b_role], [b_x[t]])
                        Fs.give(e)
                if "ffn0" in parts:
                    ffn(0, 0, C_NORM + 0)
                if "mix" in parts:
                    mixer(0, j, C_NORM + 8)
                if "ffn1" in parts:
                    ffn(0, 1, C_NORM + 16, after_chunk=handoff if j < NBLK else None)
                if j >= 1:
                    final_norm_store(out_v[j - 1], b_xh[j - 1])
                if j == 0:
                    fa = role[:, 0:1]
                    for ct in range(4):
                        ts("dve", s5car[:, :, ct * 4:ct * 4 + 4], s5car[:, :, ct * 4:ct * 4 + 4], fa, None, ALU.mult, None,
                           [b_car[ct], b_role], [b_car[ct]])
                        ts("dve", hgS[:, ct, :], hgS[:, ct, :], fa, None, ALU.mult, None, [b_hgS[ct], b_role], [b_hgS[ct]])
                        ts("dve", xce[:, ct, 0:3], xce[:, ct, 0:3], fa, None, ALU.mult, None, [b_xce[ct], b_role], [b_xce[ct]])
                    ts("dve", hcar[:], hcar[:], fa, None, ALU.mult, None, [b_hcar, b_role], [b_hcar])
        else:
            for l in range(NL):
                if cfg.get('prep', True):
                    layer_prep(l)
                else:
                    dma('sp', pp[:], pp_d[l], (), [b_pp])
                for blk in range(NBLK):
                    src = xin_v if l == 0 else out_v
                    for t in range(TT):
                        dma("sp", xt[:, t, :], src[blk][:, t, :], [b_xh[blk]], [b_x[t]])
                    if "ffn0" in parts:
                        ffn(l, 0, C_NORM + 0)
                    if l == 0:
                        dump_x("dbg_x1", blk)
                    if "mix" in parts:
                        mixer(l, blk, C_NORM + 8)
                    if l == 0:
                        dump_x("dbg_x2", blk)
                    if "ffn1" in parts:
                        ffn(l, 1, C_NORM + 16)
                    if l == NL - 1:
                        rc = norm_stats(32)
                        fw = [Fs.take(), Fs.take()]
                        for hf in range(2):
                            dma("sp", fw[hf][1][:], fnw_d[:, hf * 512:(hf + 1) * 512], (), [fw[hf][2]])
                        for t in range(TT):
                            for hf in range(2):
                                o = Fs.take()
                                act(o[1][:], xt[:, t, hf * 512:(hf + 1) * 512], AF.Copy, [b_x[t], b_sm], [o[2]],
                                    scale=sm[:, rc + t:rc + t + 1])
                                tt("dve", o[1][:], o[1][:], fw[hf][1][:], ALU.mult, [o[2], fw[hf][2]], [o[2]])
                                dma("sp", out_v[blk][:, t, hf * 512:(hf + 1) * 512], o[1][:], [o[2]], [b_xh[blk]])
                                Fs.give(o)
                        Fs.give(fw[0]); Fs.give(fw[1])
                    else:
                        for t in range(TT):
                            dma("sp", out_v[blk][:, t, :], xt[:, t, :], [b_x[t]], [b_xh[blk]])
        S.emit()
    return nc, dbg_d


def _prep_shared(inp):
    import ml_dtypes
    f = lambda a: np.ascontiguousarray(np.asarray(a, dtype=np.float32))
    L = 2
    pp = np.zeros((L, 128, 128), np.float32)

    def colv(v, n):
        return np.asarray(v, np.float32).reshape(n, 128).T

    for l in range(L):
        for n in range(3):
            pp[l, :, n * 8:(n + 1) * 8] = colv(inp["norm_w"][l, n], 8)
        pp[l, :, 24:28] = colv(inp["s5_d"][l], 4)
        pp[l, :, 28:32] = colv(inp["s5_glu_b"][l], 4)
        pp[l, :, 32:36] = colv(inp["hg_lb_logits"][0], 4)
        pp[l, :, 36:40] = colv(inp["hg_lb_logits"][1], 4)
        pp[l, :, 40:44] = colv(inp["hg_norm_w"][l], 4)
        for i in range(4):
            pp[l, :, 44 + i * 4:48 + i * 4] = colv(inp["rg_conv_w"][l, i], 4)
        pp[l, :, 60:64] = colv(inp["rg_conv_b"][l], 4)
        pp[l, :, 64:68] = colv(inp["rg_ba"][l], 4)
        pp[l, :, 68:72] = colv(inp["rg_bx"][l], 4)
        pp[l, :, 72:76] = colv(inp["rg_lambda"][l], 4)
    fnw = np.ascontiguousarray(np.broadcast_to(np.asarray(inp["final_norm_w"], np.float32)[None, :], (128, D)))
    lam_re = np.asarray(inp["s5_lambda_re"], np.float32)
    lam_im = np.asarray(inp["s5_lambda_im"], np.float32)
    ldt = np.repeat(np.asarray(inp["s5_log_dt"], np.float32)[:, :, None], 64, axis=2)
    rows = np.stack([lam_re.reshape(L, 2048), lam_im.reshape(L, 2048), ldt.reshape(L, 2048)], axis=1)
    s5row = np.ascontiguousarray(np.broadcast_to(rows[:, None, :, :], (L, 128, 3, 2048)))
    cols = np.stack([a.reshape(L, 16, 128).transpose(0, 2, 1) for a in (lam_re, lam_im, ldt)], axis=2)
    s5col = np.ascontiguousarray(cols)
    s5bt = np.zeros((L, 2, 128, 16, 2, 64), np.float32)
    s5c = np.zeros((L, 2, 2, 64, 16, 128), np.float32)
    for ri, (bsrc, csrc) in enumerate(((inp["s5_b_re"], inp["s5_c_re"]), (inp["s5_b_im"], inp["s5_c_im"]))):
        bsrc = np.asarray(bsrc, np.float32)
        csrc = np.asarray(csrc, np.float32)
        for g in range(32):
            st, gl = g // 2, g % 2
            c0 = (g % 8) * 16
            s5bt[:, ri, c0:c0 + 16, st, gl, :] = bsrc[:, g].transpose(0, 2, 1)
            s5c[:, ri, gl, :, st, c0:c0 + 16] = csrc[:, g].transpose(0, 2, 1)
    s5bt = s5bt.reshape(L, 2, 128, 2048)
    s5c = s5c.reshape(L, 2, 128, 2048)
    rgw = np.zeros((L, 2, 2, 64, 4, 2, 64), np.float32)
    for j, src in enumerate((inp["rg_wa"], inp["rg_wx"])):
        src = np.asarray(src, np.float32)
        for h in range(8):
            ct, hl = h // 2, h % 2
            rgw[:, j, hl, :, ct, hl, :] = src[:, h]
    rgw = rgw.reshape(L, 2, 128, 512)
    cst = np.zeros((128, 512 + NB), np.float32)
    m = np.triu(np.ones((64, 64), np.float32))
    cst[0:64, 0:512] = np.tile(m, (1, 8))
    r = np.ones((NB,), np.float32); r[::64] = 0.0
    cst[:, 512:512 + NB] = r[None, :]
    idb = np.concatenate([np.eye(128, dtype=np.float32), np.ones((128, 128), np.float32)], axis=1).astype(ml_dtypes.bfloat16)
    shared = {
        "ffn_gate": f(inp["ffn_gate"]), "ffn_up": f(inp["ffn_up"]), "ffn_down": f(inp["ffn_down"]),
        "w_in": f(inp["w_in"]), "branch_proj": f(inp["branch_proj"]), "w_out": f(inp["w_out"]),
        "s5_glu_w": f(inp["s5_glu_w"]), "pp": pp, "fnw": fnw, "s5row": s5row, "s5col": s5col,
        "s5bt": np.ascontiguousarray(s5bt), "s5c": np.ascontiguousarray(s5c), "rgw": np.ascontiguousarray(rgw),
        "cst": cst, "idb": idb,
    }
    return shared


PER_LAYER = ("ffn_gate", "ffn_up", "ffn_down", "w_in", "branch_proj", "w_out", "s5_glu_w", "pp", "s5row", "s5col",
             "s5bt", "s5c", "rgw")


def make_in_maps(inputs, n_pairs=4):
    x = np.asarray(inputs["x"], np.float32)
    shared = _prep_shared(inputs)
    in_maps = []
    for c in range(2 * n_pairs):
        b, l = c // 2, c % 2
        m = {}
        for k, v in shared.items():
            m[k] = np.ascontiguousarray(v[l:l + 1]) if k in PER_LAYER else v
        m["x"] = np.ascontiguousarray(x[b])
        role = np.zeros((128, 2), np.float32)
        role[:, l] = 1.0
        m["role"] = role
        in_maps.append(m)
    return in_maps


def kernel(**inputs):
    nc, _ = build({"pipe": True})
    in_maps = make_in_maps(inputs)
    res = run_bass_kernel_spmd(nc, in_maps, core_ids=list(range(8)))
    out = np.stack([np.asarray(res.results[2 * b + 1]["out"], np.float32).reshape(SEQ, D) for b in range(4)], axis=0)
    return out
```

```python
import math
from contextlib import ExitStack
import numpy as np
import concourse.bass as bass
import concourse.mybir as mybir
from concourse.bass_utils import run_bass_kernel_spmd

F32 = mybir.dt.float32
BF16 = mybir.dt.bfloat16
AF = mybir.ActivationFunctionType
ALU = mybir.AluOpType
ENG = ("pe", "act", "dve", "pool", "sp")

D = 1024
SEQ = 4096
DFF = 2816
NM = DFF // 128
INT = 6656
EPS = 1e-6
NB = 512
TT = NB // 128
NCH = NB // 64
TBL = 128
NCORE = 4


class Buf:
    __slots__ = ("name", "w", "r", "excl")

    def __init__(self, name, excl=False):
        self.name = name
        self.excl = excl
        self.w = None
        self.r = {}


class Sched:
    NDSEM = 24

    def __init__(self, nc, es):
        self.nc = nc
        self.ops = {e: [] for e in ENG}
        self.sem = {e: es.enter_context(nc.semaphore("s_" + e)) for e in ENG if e != "sp"}
        self.seen = {e: {} for e in ENG}
        self.QN = {"sp": 0, "pool": 1, "act": 2}
        self.NDSEM = 12 * 3
        self.dsem = [es.enter_context(nc.semaphore("d%d" % i)) for i in range(self.NDSEM)]
        self.dcnt = [0] * self.NDSEM
        self.dma_i = {"sp": 0, "pool": 0, "act": 0}
        self.cc_sem = es.enter_context(nc.semaphore("cc_sem"))
        self.cc_cnt = 0

    def _wait(self, eng, tok):
        if tok is None:
            return
        if tok[0] == "c":
            _, e2, rec = tok
            if e2 == eng and eng == "pe":
                return
            key = ("c", e2)
            if self.seen[eng].get(key, -1) >= rec["idx"]:
                return
            self.seen[eng][key] = rec["idx"]
            rec["sig"] = True
            self.ops[eng].append({"kind": "wc", "eng": e2, "rec": rec})
        elif tok[0] == "k":
            v = tok[2]
            key = ("k", "cc")
            if self.seen[eng].get(key, -1) >= v:
                return
            self.seen[eng][key] = v
            self.ops[eng].append({"kind": "wk", "v": v})
        else:
            _, slot, v = tok
            key = ("d", slot)
            if self.seen[eng].get(key, -1) >= v:
                return
            self.seen[eng][key] = v
            self.ops[eng].append({"kind": "wd", "slot": slot, "v": v})

    def _deps(self, eng, R, W):
        for b in R:
            self._wait(eng, b.w)
            if b.excl:
                for k, t in list(b.r.items()):
                    if k[1] != eng:
                        self._wait(eng, t)
        for b in W:
            self._wait(eng, b.w)
            for t in b.r.values():
                self._wait(eng, t)

    def _mark(self, tok, R, W):
        for b in W:
            b.w = tok
            b.r = {}
        for b in R:
            if b in W:
                continue
            b.r[(tok[0], tok[1])] = tok

    def op(self, eng, fn, R=(), W=()):
        self._deps(eng, R, W)
        rec = {"kind": "op", "fn": fn, "sig": False, "idx": len(self.ops[eng])}
        self.ops[eng].append(rec)
        self._mark(("c", eng, rec), R, W)

    def dma(self, q, fn, R=(), W=()):
        self._deps(q, R, W)
        slot = self.QN[q] * 12 + self.dma_i[q] % 12
        self.dma_i[q] += 1
        if self.dcnt[slot] > 0:
            self._wait(q, ("d", slot, self.dcnt[slot]))
        self.dcnt[slot] += 1
        self.ops[q].append({"kind": "dma", "fn": fn, "slot": slot})
        self._mark(("d", slot, self.dcnt[slot]), R, W)

    def cc(self, fn, R=(), W=()):
        self._deps("pool", R, W)
        self.cc_cnt += 1
        self.ops["pool"].append({"kind": "cc", "fn": fn})
        self._mark(("k", "cc", self.cc_cnt), R, W)

    def emit(self):
        nc = self.nc
        if self.cc_cnt:
            self._wait("pool", ("k", "cc", self.cc_cnt))
        for slot in range(self.NDSEM):
            if self.dcnt[slot] > 0:
                self._wait("sp", ("d", slot, self.dcnt[slot]))
        for e in ENG:
            k = 0
            for o in self.ops[e]:
                if o["kind"] == "op" and o["sig"]:
                    k += 1
                    o["sidx"] = k

        def body(e):
            def f(E):
                for o in self.ops[e]:
                    kd = o["kind"]
                    if kd == "wc":
                        E.wait_ge(self.sem[o["eng"]], o["rec"]["sidx"])
                    elif kd == "wd":
                        E.wait_ge(self.dsem[o["slot"]], 16 * o["v"])
                    elif kd == "wk":
                        E.wait_ge(self.cc_sem, o["v"])
                    elif kd == "cc":
                        o["fn"](E).then_inc(self.cc_sem)
                    elif kd == "op":
                        ins = o["fn"](E)
                        if o["sig"]:
                            ins.then_inc(self.sem[e], 1)
                    else:
                        o["fn"](E).then_inc(self.dsem[o["slot"]], 16)
            return f

        with nc.Block() as blk:
            blk.tensor(body("pe"))
            blk.scalar(body("act"))
            blk.vector(body("dve"))
            blk.gpsimd(body("pool"))
            blk.sync(body("sp"))


class Slots:
    def __init__(self, tiles, name, excl=False):
        self.items = [(t, Buf("%s%d" % (name, i), excl)) for i, t in enumerate(tiles)]
        self.free = list(range(len(tiles)))
        self.name = name

    def take(self):
        if not self.free:
            raise RuntimeError("out of slots: " + self.name)
        i = self.free.pop(0)
        t, b = self.items[i]
        return (i, t, b)

    def give(self, h):
        assert h[0] not in self.free
        self.free.append(h[0])


def build(cfg):
    NL = cfg.get("n_layers", 2)
    NBLK = cfg.get("n_blocks", SEQ // NB)
    dbg = cfg.get("dbg", False)
    parts = cfg.get("parts", ("ffn0", "mix", "ffn1"))
    pipe = cfg.get("pipe", False)
    LW = 1 if pipe else 2
    nc = bass.Bass("TRN2", target_bir_lowering=False)

    def din(name, shape, dt=F32):
        return nc.dram_tensor(name, list(shape), dt, kind="ExternalInput").ap()

    x_d = din("x", [SEQ, D])
    out_d = nc.dram_tensor("out", [SEQ, D], F32, kind="ExternalOutput").ap()
    wg_d = din("ffn_gate", [LW, 2, D, DFF])
    wu_d = din("ffn_up", [LW, 2, D, DFF])
    wd_d = din("ffn_down", [LW, 2, DFF, D])
    win_d = din("w_in", [LW, D, INT])
    bp_d = din("branch_proj", [LW, 3, 512, D])
    wo_d = din("w_out", [LW, D, D])
    glu_d = din("s5_glu_w", [LW, 512, 512])
    pp_d = din("pp", [LW, 128, 128])
    fnw_d = din("fnw", [128, D])
    s5row_d = din("s5row", [LW, 128, 3, 2048])
    s5col_d = din("s5col", [LW, 128, 3, 16])
    s5bt_d = din("s5bt", [LW, 2, 128, 2048])
    s5c_d = din("s5c", [LW, 2, 128, 2048])
    rgw_d = din("rgw", [LW, 2, 128, 512])
    cst_d = din("cst", [128, 512 + NB])
    idb_d = din("idb", [128, 256], BF16)
    role_d = din("role", [128, 2])
    NSC = 48
    wsc_on = pipe and cfg.get("wscratch", True)
    if pipe:
        xsrc_d = nc.dram_tensor("xch_src", [4, NB, 256], F32)
        xdst_d = nc.dram_tensor("xch_dst", [4, 2 * NB, 256], F32)
    if wsc_on:
        wsc_t = nc.dram_tensor("wscratch", [NSC, 128, NM * 256], BF16)
    dbg_d = {}

    def dout(name, shape):
        dbg_d[name] = nc.dram_tensor(name, list(shape), F32, kind="ExternalOutput").ap()
        return dbg_d[name]

    es = ExitStack()
    with es:
        S = Sched(nc, es)

        def sb(name, shape, dt=F32):
            return es.enter_context(nc.sbuf_tensor("sb_" + name, list(shape), dt))

        def act(out, in_, func, R, W, scale=None, bias=None, accum=None):
            kw = {}
            if scale is not None:
                kw["scale"] = scale
            if bias is not None:
                kw["bias"] = bias
            if accum is not None:
                kw["accum_out"] = accum
            S.op("act", lambda E: E.activation(out=out, in_=in_, func=func, **kw), R, W)

        def tt(eng, out, a, b, op, R, W):
            S.op(eng, lambda E: E.tensor_tensor(out=out, in0=a, in1=b, op=op), R, W)

        def ts(eng, out, a, s1, s2, op0, op1, R, W):
            if op1 is None:
                S.op(eng, lambda E: E.tensor_scalar(out=out, in0=a, scalar1=s1, scalar2=None, op0=op0), R, W)
            else:
                S.op(eng, lambda E: E.tensor_scalar(out=out, in0=a, scalar1=s1, scalar2=s2, op0=op0, op1=op1), R, W)

        def stt(out, a, s, b, op0, op1, R, W):
            S.op("dve", lambda E: E.scalar_tensor_tensor(out=out, in0=a, scalar=s, in1=b, op0=op0, op1=op1), R, W)

        def mm(out, lhsT, rhs, start, stop, R, W):
            S.op("pe", lambda E: E.matmul(out, lhsT=lhsT, rhs=rhs, start=start, stop=stop), R, W)

        def tr(out, in_, ident, R, W):
            S.op("pe", lambda E: E.transpose(out, in_, ident), R, W)

        def scan(out, d0, d1, init, R, W):
            S.op("dve", lambda E: E.tensor_tensor_scan(out=out, data0=d0, data1=d1, initial=init,
                                                       op0=ALU.mult, op1=ALU.add), R, W)

        def cp(eng, out, in_, R, W):
            if eng == "act":
                act(out, in_, AF.Copy, R, W)
            else:
                S.op(eng, lambda E: E.tensor_copy(out=out, in_=in_), R, W)

        def recip(out, in_, R, W):
            S.op("dve", lambda E: E.reciprocal(out=out, in_=in_), R, W)

        def memset(eng, ap, val, W):
            S.op(eng, lambda E: E.memset(ap, val), (), W)

        def dma(q, out, in_, R, W):
            S.dma(q, lambda E: E.dma_start(out=out, in_=in_), R, W)

        NF, NH = 24, 32
        Fs = Slots([sb("F%d" % i, [128, 512]) for i in range(NF)], "F")
        Hs = Slots([sb("H%d" % i, [128, 512], BF16) for i in range(NH)], "H")
        PF = Slots([es.enter_context(nc.psum_tensor("pf%d" % i, [128, 512], F32)) for i in range(6)], "pf", True)
        PB = Slots([es.enter_context(nc.psum_tensor("pb%d" % i, [128, 1024], BF16)) for i in range(2)], "pb", True)

        cst = sb("cst", [128, 512 + NB]); b_cst = Buf("cst")
        idb = sb("idb", [128, 256], BF16); b_idb = Buf("idb")
        dma("sp", cst[:], cst_d, (), [b_cst])
        dma("sp", idb[:], idb_d, (), [b_idb])
        mask = cst[0:64, 0:512]
        rst = cst[:, 512:512 + NB]
        identb = idb[:, 0:128]
        onesb = idb[:, 128:256]

        xt = sb("xt", [128, TT, D]); b_x = [Buf("x%d" % t) for t in range(TT)]
        xs = sb("xs", [128, 2, D], BF16); b_xs = [Buf("xs0"), Buf("xs1")]
        sm = sb("sm", [128, 64]); b_sm = Buf("sm")
        pp = sb("pp", [128, 128]); b_pp = Buf("pp")
        role = sb("role", [128, 2]); b_role = Buf("role")
        dma("sp", role[:], role_d, (), [b_role])
        ppx = sb("ppx", [128, 32]); b_ppx = Buf("ppx")
        NWB = 3
        wbt = [sb("wb%d" % i, [128, NM * 256], BF16) for i in range(NWB)]
        b_wb = [Buf("wb%d" % i) for i in range(NWB)]
        wctr = [0]

        def wtake():
            i = wctr[0] % NWB
            wctr[0] += 1
            return i

        cur_it = [0]
        b_wsc = [Buf("wsc%d" % k) for k in range(NSC)]

        def wload(i, n, pieces, cid):
            if (not wsc_on) or cur_it[0] == 0:
                for view, src in pieces:
                    dma("pool", view, src, (), [b_wb[i]])
                if wsc_on:
                    dma("sp", wsc_t.ap()[cid, :, 0:n], wbt[i][:, 0:n], [b_wb[i]], [b_wsc[cid]])
            else:
                dma("sp", wbt[i][:, 0:n], wsc_t.ap()[cid, :, 0:n], [b_wsc[cid]], [b_wb[i]])

        def v_win(i):
            return wbt[i][:, 0:4096].rearrange("p (k n) -> p k n", n=512)

        def v_gu(i, j):
            return wbt[i][:, j * 2048:(j + 1) * 2048].rearrange("p (k n) -> p k n", n=256)

        def v_wd(i):
            return wbt[i][:, :].rearrange("p (m n) -> p m n", n=256)

        def v_bp(i):
            return wbt[i][:, 0:4096].rearrange("p (c n) -> p c n", n=D)
        glub = sb("glub", [128, 4, 512], BF16); b_glub = Buf("glub")
        brT = sb("brT", [128, 16, 128], BF16); biT = sb("biT", [128, 16, 128], BF16)
        crT = sb("crT", [128, 16, 128], BF16); nciT = sb("nciT", [128, 16, 128], BF16)
        b_s5m = Buf("s5m")
        cosT = sb("cosT", [128, 16, TBL]); sinT = sb("sinT", [128, 16, TBL]); b_tab = Buf("tab")
        s5c = sb("s5c", [128, 8, 16]); b_s5c = Buf("s5c")
        s5car = sb("s5car", [128, 2, 16]); b_car = [Buf("car%d" % i) for i in range(4)]
        s5cs = sb("s5cs", [128, 16]); b_s5cs = Buf("s5cs")
        rho0 = sb("rho0", [128, 16, TBL]); b_rho0 = Buf("rho0")
        rgw = sb("rgw", [128, 2, 512], BF16); b_rgw = Buf("rgw")
        xce = sb("xce", [128, 4, 3 + NB]); b_xce = [Buf("xce%d" % i) for i in range(4)]
        hcar = sb("hcar", [128, 4]); b_hcar = Buf("hcar")
        hgS = sb("hgS", [128, 4, 128]); b_hgS = [Buf("hgS%d" % i) for i in range(4)]
        smb8s = [sb("smb8_%d" % i, [128, NCH * 128], BF16) for i in range(2)]; b_smb8s = [Buf("smb8_%d" % i) for i in range(2)]
        vTs = [sb("vT%d" % i, [128, NCH * 128], BF16) for i in range(2)]; b_vTs = [Buf("vT%d" % i) for i in range(2)]
        kdTs = [sb("kdT%d" % i, [128, NCH * 128], BF16) for i in range(2)]; b_kdTs = [Buf("kdT%d" % i) for i in range(2)]
        hss = [sb("hs%d" % i, [128, 4, NCH]) for i in range(4)]; b_hss = [Buf("hs%d" % i) for i in range(4)]

        C_NORM = 0
        C_D = 24; C_GLUB = 28; C_LB0 = 32; C_LB1 = 36; C_HNW = 40
        C_CW = 44
        C_CB = 60; C_BA = 64; C_BX = 68; C_LAM = 72

        xin_v = x_d.rearrange("(b t p) d -> b p t d", p=128, t=TT)
        out_v = out_d.rearrange("(b t p) d -> b p t d", p=128, t=TT)
        b_xh = [Buf("xh%d" % i) for i in range(SEQ // NB)]

        hT = [None] * 8

        def norm_stats(col0):
            for t in range(TT):
                act(xs[:, t % 2, :], xt[:, t, :], AF.Square, [b_x[t]], [b_xs[t % 2], b_sm], accum=sm[:, col0 + t:col0 + t + 1])
            act(sm[:, col0 + 8:col0 + 8 + TT], sm[:, col0:col0 + TT], AF.Sqrt, [b_sm], [b_sm], scale=1.0 / D, bias=EPS)
            recip(sm[:, col0 + 16:col0 + 16 + TT], sm[:, col0 + 8:col0 + 8 + TT], [b_sm], [b_sm])
            return col0 + 16

        def do_norm(ncol):
            nst = cfg.get('norm_stage', 9)
            rc = norm_stats(0)
            hs_ = [Hs.take() for _ in range(8)]
            for tp in range(TT // 2):
                for j in range(2):
                    t = tp * 2 + j
                    if nst >= 1:
                        act(xs[:, j, :], xt[:, t, :], AF.Copy, [b_x[t], b_sm], [b_xs[j]], scale=sm[:, rc + t:rc + t + 1])
                for kq in range(2):
                    pbh = PB.take()
                    for kk in range(4):
                        k = kq * 4 + kk
                        for j in range(2):
                            if nst >= 2:
                                tr(pbh[1][:, (kk * 2 + j) * 128:(kk * 2 + j + 1) * 128], xs[:, j, k * 128:(k + 1) * 128], identb,
                                   [b_xs[j], b_idb], [pbh[2]])
                    for kk in range(4):
                        k = kq * 4 + kk
                        src = pbh[1][:, kk * 256:(kk + 1) * 256]
                        dst = hs_[k][1][:, tp * 256:(tp + 1) * 256]
                        wcol = pp[:, ncol + k:ncol + k + 1]
                        emode = cfg.get('evac_mode', 'mix')
                        if (kk % 2 == 0 and emode == 'mix') or emode == 'act':
                            if nst >= 3:
                                act(dst, src, AF.Copy, [pbh[2], b_pp], [hs_[k][2]], scale=wcol)
                        else:
                            if nst >= 4:
                                ts("dve", dst, src, wcol, None, ALU.mult, None, [pbh[2], b_pp], [hs_[k][2]])
                    PB.give(pbh)
            for k in range(8):
                hT[k] = hs_[k]

        def free_hT():
            for k in range(8):
                Hs.give(hT[k])
                hT[k] = None

        def ffn(l, which, ncol, after_chunk=None):
            do_norm(ncol)
            wg_v = wg_d[l, which].rearrange("(k p) n -> p k n", p=128)
            wu_v = wu_d[l, which].rearrange("(k p) n -> p k n", p=128)
            wd_v = wd_d[l, which].rearrange("(m p) n -> p m n", p=128)
            acts = []
            stage = cfg.get('ffn_stage', 2)
            if stage == 0:
                free_hT()
                return
            for c in range(NM // 2):
                i = wtake()
                wload(i, 4096, [(v_gu(i, 0), wg_v[:, :, c * 256:(c + 1) * 256]),
                                (v_gu(i, 1), wu_v[:, :, c * 256:(c + 1) * 256])], which * 15 + c)
                for mi in range(2):
                    pg = PF.take(); pu = PF.take()
                    for k in range(8):
                        mm(pg[1][:], v_gu(i, 0)[:, k, mi * 128:(mi + 1) * 128], hT[k][1][:], k == 0, k == 7,
                           [b_wb[i], hT[k][2]], [pg[2]])
                    for k in range(8):
                        mm(pu[1][:], v_gu(i, 1)[:, k, mi * 128:(mi + 1) * 128], hT[k][1][:], k == 0, k == 7,
                           [b_wb[i], hT[k][2]], [pu[2]])
                    sg = Fs.take()
                    act(sg[1][:], pg[1][:], AF.Silu, [pg[2]], [sg[2]])
                    a = Hs.take()
                    tt("dve", a[1][:], sg[1][:], pu[1][:], ALU.mult, [sg[2], pu[2]], [a[2]])
                    Fs.give(sg); PF.give(pg); PF.give(pu)
                    acts.append(a)
            free_hT()
            if stage == 1:
                for a in acts:
                    Hs.give(a)
                return
            for c4 in range(4):
                i = wtake()
                wload(i, NM * 256, [(v_wd(i), wd_v[:, :, c4 * 256:(c4 + 1) * 256])], which * 15 + 11 + c4)
                for t in range(TT):
                    pd = PF.take()
                    for m in range(NM):
                        mm(pd[1][:, 0:256], acts[m][1][:, t * 128:(t + 1) * 128], v_wd(i)[:, m, :], m == 0, m == NM - 1,
                           [acts[m][2], b_wb[i]], [pd[2]])
                    xsl = xt[:, t, c4 * 256:(c4 + 1) * 256]
                    stt(xsl, pd[1][:, 0:256], 0.5, xsl, ALU.mult, ALU.add, [pd[2], b_x[t]], [b_x[t]])
                    PF.give(pd)
                if after_chunk is not None:
                    after_chunk(c4)
            for a in acts:
                Hs.give(a)

        def layer_prep(l):
            dma("sp", pp[:], pp_d[l], (), [b_pp])
            if pipe:
                tt("dve", ppx[:, 0:4], pp[:, C_LB1:C_LB1 + 4], pp[:, C_LB0:C_LB0 + 4], ALU.subtract, [b_pp], [b_ppx])
                act(ppx[:, 0:4], ppx[:, 0:4], AF.Sigmoid, [b_ppx], [b_ppx])
                ts("dve", ppx[:, 0:4], ppx[:, 0:4], role[:, 1:2], None, ALU.mult, None, [b_ppx, b_role], [b_ppx])
            elif l == 0:
                ts("dve", ppx[:, 0:4], pp[:, C_LB0:C_LB0 + 4], 0.0, None, ALU.mult, None, [b_pp], [b_ppx])
            else:
                tt("dve", ppx[:, 0:4], pp[:, C_LB1:C_LB1 + 4], pp[:, C_LB0:C_LB0 + 4], ALU.subtract, [b_pp], [b_ppx])
                act(ppx[:, 0:4], ppx[:, 0:4], AF.Sigmoid, [b_ppx], [b_ppx])
            ts("dve", ppx[:, 4:8], ppx[:, 0:4], -1.0, 1.0, ALU.mult, ALU.add, [b_ppx], [b_ppx])
            act(ppx[:, 16:20], pp[:, C_LAM:C_LAM + 4], AF.Exp, [b_pp], [b_ppx], scale=-1.0)
            act(ppx[:, 16:20], ppx[:, 16:20], AF.Ln, [b_ppx], [b_ppx], bias=1.0)
            ts("dve", ppx[:, 8:12], ppx[:, 16:20], -8.0, None, ALU.mult, None, [b_ppx], [b_ppx])
            ts("dve", ppx[:, 12:16], ppx[:, 16:20], -16.0, None, ALU.mult, None, [b_ppx], [b_ppx])
            dma("pool", glub[:], glu_d[l].rearrange("(c p) n -> p c n", p=128), (), [b_glub])
            for j in range(2):
                dma("pool", rgw[:, j, :], rgw_d[l, j], (), [b_rgw])
            memset("pool", s5car[:], 0.0, b_car)
            memset("pool", hcar[:], 0.0, [b_hcar])
            for i in range(4):
                memset("pool", hgS[:, i, :], 0.0, [b_hgS[i]])
                memset("pool", xce[:, i, 0:3], 0.0, [b_xce[i]])
            if cfg.get('s5prep', True):
                s5_prep(l)

        def trig_chain(n, th, out_c, out_s, f, R, W, width):
            c, s_, t1, t2 = f
            act(s_, th, AF.Sin, R, W, scale=1.0 / 32)
            act(c, th, AF.Sin, R, W, scale=-1.0 / 32, bias=math.pi / 2)
            for i in range(5):
                tt("dve", t1, c, c, ALU.mult, W, W)
                tt("dve", t2, s_, s_, ALU.mult, W, W)
                stt(s_, s_, 2.0, c, ALU.mult, ALU.mult, W, W)
                tt("dve", c, t1, t2, ALU.subtract, W, W)
            return c, s_

        def s5_prep(l):
            dma("sp", s5c[:, 0:3, :], s5col_d[l], (), [b_s5c])
            W = [b_s5c]
            lr = s5c[:, 0, :]; li = s5c[:, 1, :]; dt = s5c[:, 2, :]
            act(dt, dt, AF.Exp, W, W)
            ts("dve", lr, lr, -1e-4, None, ALU.min, None, W, W)
            tt("dve", s5c[:, 3, :], lr, dt, ALU.mult, W, W)
            act(s5c[:, 3, :], s5c[:, 3, :], AF.Exp, W, W)
            tt("dve", s5c[:, 4, :], li, dt, ALU.mult, W, W)
            trig_chain(16, s5c[:, 4, :], None, None, [s5c[:, 5, :], s5c[:, 6, :], s5c[:, 7, :], s5c[:, 0, :]], W, W, 16)
            cp("dve", rho0[:], s5c[:, 3, :].unsqueeze(2).broadcast_to([128, 16, TBL]), W, [b_rho0])
            memset("dve", rho0[:, :, 0:1], 0.0, [b_rho0])
            WT = [b_tab]
            cp("dve", cosT[:, :, 0:1], s5c[:, 5, :].unsqueeze(2), W, WT)
            cp("dve", sinT[:, :, 0:1], s5c[:, 6, :].unsqueeze(2), W, WT)
            n = 1
            while n < TBL:
                for h0 in range(0, 16, 8):
                    if n * 8 > 512:
                        raise RuntimeError("tbl")
                    fa = Fs.take(); fb = Fs.take()
                    sl = slice(h0, h0 + 8)
                    cn = cosT[:, sl, n - 1:n].broadcast_to([128, 8, n])
                    sn = sinT[:, sl, n - 1:n].broadcast_to([128, 8, n])
                    c0 = cosT[:, sl, 0:n]; s0 = sinT[:, sl, 0:n]
                    ta = fa[1][:, 0:8 * n].rearrange("p (a b) -> p a b", b=n)
                    tb = fb[1][:, 0:8 * n].rearrange("p (a b) -> p a b", b=n)
                    tt("dve", ta, c0, cn, ALU.mult, WT, [fa[2]])
                    tt("dve", tb, s0, sn, ALU.mult, WT, [fb[2]])
                    tt("dve", cosT[:, sl, n:2 * n], ta, tb, ALU.subtract, [fa[2], fb[2]], WT)
                    tt("dve", ta, s0, cn, ALU.mult, WT, [fa[2]])
                    tt("dve", tb, c0, sn, ALU.mult, WT, [fb[2]])
                    tt("dve", sinT[:, sl, n:2 * n], ta, tb, ALU.add, [fa[2], fb[2]], WT)
                    Fs.give(fa); Fs.give(fb)
                n *= 2
            WM = [b_s5m]
            for cq in range(4):
                cs = slice(cq * 512, (cq + 1) * 512)
                f = [Fs.take() for _ in range(10)]
                A = [h[1][:] for h in f]
                Bf = [h[2] for h in f]
                lr, li, dt, rho, th, c, s_, t1, t2, t3 = A
                dma("sp", lr, s5row_d[l, :, 0, cs], (), [Bf[0]])
                dma("sp", li, s5row_d[l, :, 1, cs], (), [Bf[1]])
                dma("sp", dt, s5row_d[l, :, 2, cs], (), [Bf[2]])
                act(dt, dt, AF.Exp, Bf, Bf)
                ts("dve", lr, lr, -1e-4, None, ALU.min, None, Bf, Bf)
                tt("dve", rho, lr, dt, ALU.mult, Bf, Bf)
                act(rho, rho, AF.Exp, Bf, Bf)
                tt("dve", th, li, dt, ALU.mult, Bf, Bf)
                trig_chain(512, th, None, None, [c, s_, t1, t2], Bf, Bf, 512)
                tt("dve", c, c, rho, ALU.mult, Bf, Bf)
                ts("dve", c, c, -1.0, None, ALU.add, None, Bf, Bf)
                tt("dve", s_, s_, rho, ALU.mult, Bf, Bf)
                tt("dve", t1, lr, lr, ALU.mult, Bf, Bf)
                tt("dve", t2, li, li, ALU.mult, Bf, Bf)
                tt("dve", t1, t1, t2, ALU.add, Bf, Bf)
                recip(t1, t1, Bf, Bf)
                tt("dve", t2, c, lr, ALU.mult, Bf, Bf)
                tt("dve", t3, s_, li, ALU.mult, Bf, Bf)
                tt("dve", t2, t2, t3, ALU.add, Bf, Bf)
                tt("dve", t2, t2, t1, ALU.mult, Bf, Bf)
                tt("dve", t3, s_, lr, ALU.mult, Bf, Bf)
                tt("dve", th, c, li, ALU.mult, Bf, Bf)
                tt("dve", t3, t3, th, ALU.subtract, Bf, Bf)
                tt("dve", t3, t3, t1, ALU.mult, Bf, Bf)
                fr, fi = t2, t3
                dma("sp", lr, s5bt_d[l, 0, :, cs], Bf, [Bf[0]])
                dma("sp", li, s5bt_d[l, 1, :, cs], Bf, [Bf[1]])
                st4 = slice(cq * 4, cq * 4 + 4)
                tt("dve", dt, fr, lr, ALU.mult, Bf, Bf)
                tt("dve", rho, fi, li, ALU.mult, Bf, Bf)
                tt("dve", brT[:, st4, :].rearrange("p a b -> p (a b)"), dt, rho, ALU.subtract, Bf, WM)
                tt("dve", dt, fr, li, ALU.mult, Bf, Bf)
                tt("dve", rho, fi, lr, ALU.mult, Bf, Bf)
                tt("dve", biT[:, st4, :].rearrange("p a b -> p (a b)"), dt, rho, ALU.add, Bf, WM)
                dma("sp", c, s5c_d[l, 0, :, cs], Bf, [Bf[5]])
                dma("sp", s_, s5c_d[l, 1, :, cs], Bf, [Bf[6]])
                cp("dve", crT[:, st4, :].rearrange("p a b -> p (a b)"), c, Bf, WM)
                ts("dve", nciT[:, st4, :].rearrange("p a b -> p (a b)"), s_, -1.0, None, ALU.mult, None, Bf, WM)
                for h in f:
                    Fs.give(h)

        def load_win(l, chunk):
            i = wtake()
            wv = win_d[l].rearrange("(k p) n -> p k n", p=128)
            wload(i, 4096, [(v_win(i), wv[:, :, chunk * 512:(chunk + 1) * 512])], 30 + chunk)
            return i

        def proj(i, ct):
            p = PF.take()
            for k in range(8):
                mm(p[1][:], v_win(i)[:, k, ct * 128:(ct + 1) * 128], hT[k][1][:], k == 0, k == 7,
                   [b_wb[i], hT[k][2]], [p[2]])
            return p

        def s5_branch(l, blk):
            wi0 = load_win(l, 0)
            uf, ub = [], []
            for ct in range(4):
                p = proj(wi0, ct)
                f = Fs.take(); h = Hs.take()
                act(f[1][:], p[1][:], AF.Copy, [p[2]], [f[2]])
                cp("dve", h[1][:], p[1][:], [p[2]], [h[2]])
                PF.give(p)
                uf.append(f); ub.append(h)
            zf, zb = [None] * 4, [None] * 4
            M = {k: Fs.take() for k in ("m1", "m2", "m3", "m4", "bpr", "bpi", "t1", "t2", "t3", "t4")}
            Wb = [{k: Fs.take() for k in ("wr", "wi")} for _ in range(2)]
            Tb = [{k: Hs.take() for k in ("t1", "t2", "t3", "t4")} for _ in range(2)]

            def v3(h):
                return h[1][:].rearrange("p (a b) -> p a b", b=TBL)
            iters = [(ct, tb) for ct in range(4) for tb in range(NB // TBL)]
            NTB = NB // TBL
            bu = {}
            pyh = {}

            def issue_bu(i):
                ct, tb = iters[i]
                tsl = slice(tb * TBL, (tb + 1) * TBL)
                pbr = PF.take(); pbi = PF.take()
                for j in range(4):
                    st = ct * 4 + j
                    mm(pbr[1][:, j * TBL:(j + 1) * TBL], brT[:, st, :], ub[ct][1][:, tsl], True, True,
                       [b_s5m, ub[ct][2]], [pbr[2]])
                    mm(pbi[1][:, j * TBL:(j + 1) * TBL], biT[:, st, :], ub[ct][1][:, tsl], True, True,
                       [b_s5m, ub[ct][2]], [pbi[2]])
                bu[i] = (pbr, pbi)

            def rot_in(i):
                ct, tb = iters[i]
                st4 = slice(ct * 4, ct * 4 + 4)
                cT = cosT[:, st4, :]; sT = sinT[:, st4, :]
                pbr, pbi = bu.pop(i)
                tt("dve", v3(M["m1"]), v3(pbr), cT, ALU.mult, [pbr[2], b_tab], [M["m1"][2]])
                tt("dve", v3(M["m2"]), v3(pbi), sT, ALU.mult, [pbi[2], b_tab], [M["m2"][2]])
                tt("pool", M["bpr"][1][:], M["m1"][1][:], M["m2"][1][:], ALU.add, [M["m1"][2], M["m2"][2]], [M["bpr"][2]])
                tt("dve", v3(M["m3"]), v3(pbi), cT, ALU.mult, [pbi[2], b_tab], [M["m3"][2]])
                tt("dve", v3(M["m4"]), v3(pbr), sT, ALU.mult, [pbr[2], b_tab], [M["m4"][2]])
                tt("pool", M["bpi"][1][:], M["m3"][1][:], M["m4"][1][:], ALU.subtract, [M["m3"][2], M["m4"][2]], [M["bpi"][2]])
                PF.give(pbr); PF.give(pbi)

            def scans(i):
                ct, tb = iters[i]
                st4 = slice(ct * 4, ct * 4 + 4)
                Wc = Wb[i % 2]
                for nm, src, ri in (("wr", "bpr", 0), ("wi", "bpi", 1)):
                    o = ri * 8
                    tt("dve", s5cs[:, o:o + 4], s5c[:, 3, st4], s5car[:, ri, st4], ALU.mult, [b_s5c, b_car[ct]], [b_s5cs])
                    first = v3(M[src])[:, :, 0:1]
                    tt("dve", first, first, s5cs[:, o:o + 4].unsqueeze(2), ALU.add, [M[src][2], b_s5cs], [M[src][2]])
                    scan(Wc[nm][1][:], rho0[:, st4, :].rearrange("p a b -> p (a b)"), M[src][1][:], 0.0,
                         [b_rho0, M[src][2]], [Wc[nm][2]])

            def rot_out(i):
                ct, tb = iters[i]
                tsl = slice(tb * TBL, (tb + 1) * TBL)
                st4 = slice(ct * 4, ct * 4 + 4)
                cT = cosT[:, st4, :]; sT = sinT[:, st4, :]
                Wc = Wb[i % 2]; Tc = Tb[i % 2]
                L = slice(TBL - 1, TBL)
                tt("dve", v3(M["t1"]), v3(Wc["wr"]), cT, ALU.mult, [Wc["wr"][2], b_tab], [M["t1"][2]])
                tt("dve", v3(M["t2"]), v3(Wc["wi"]), sT, ALU.mult, [Wc["wi"][2], b_tab], [M["t2"][2]])
                tt("pool", v3(M["t3"]), v3(Wc["wr"]), sT, ALU.mult, [Wc["wr"][2], b_tab], [M["t3"][2]])
                tt("pool", v3(M["t4"]), v3(Wc["wi"]), cT, ALU.mult, [Wc["wi"][2], b_tab], [M["t4"][2]])
                tt("dve", s5car[:, 0, st4].unsqueeze(2), v3(M["t1"])[:, :, L], v3(M["t2"])[:, :, L], ALU.subtract,
                   [M["t1"][2], M["t2"][2]], [b_car[ct]])
                tt("dve", s5cs[:, 4:8].unsqueeze(2), v3(Wc["wr"])[:, :, L], sT[:, :, L], ALU.mult, [Wc["wr"][2], b_tab], [b_s5cs])
                tt("dve", s5cs[:, 12:16].unsqueeze(2), v3(Wc["wi"])[:, :, L], cT[:, :, L], ALU.mult, [Wc["wi"][2], b_tab], [b_s5cs])
                tt("dve", s5car[:, 1, st4], s5cs[:, 4:8], s5cs[:, 12:16], ALU.add, [b_s5cs], [b_car[ct]])
                act(Tc["t1"][1][:], M["t1"][1][:], AF.Copy, [M["t1"][2]], [Tc["t1"][2]])
                act(Tc["t2"][1][:], M["t2"][1][:], AF.Copy, [M["t2"][2]], [Tc["t2"][2]], scale=-1.0)
                act(Tc["t3"][1][:], M["t3"][1][:], AF.Copy, [M["t3"][2]], [Tc["t3"][2]])
                act(Tc["t4"][1][:], M["t4"][1][:], AF.Copy, [M["t4"][2]], [Tc["t4"][2]])
                if tb == 0:
                    pyh[ct] = PF.take()
                py = pyh[ct]
                n = 0
                for j in range(4):
                    st = ct * 4 + j
                    js = slice(j * TBL, (j + 1) * TBL)
                    for nm, wT in (("t1", crT), ("t2", crT), ("t3", nciT), ("t4", nciT)):
                        mm(py[1][:, tsl], wT[:, st, :], Tc[nm][1][:, js], n == 0, n == 15, [b_s5m, Tc[nm][2]], [py[2]])
                        n += 1
                if tb == NTB - 1:
                    yf = Fs.take()
                    stt(yf[1][:], uf[ct][1][:], pp[:, C_D + ct:C_D + ct + 1], py[1][:], ALU.mult, ALU.add,
                        [uf[ct][2], b_pp, py[2]], [yf[2]])
                    PF.give(py)
                    act(yf[1][:], yf[1][:], AF.Gelu_apprx_tanh, [yf[2]], [yf[2]])
                    zh = Hs.take()
                    cp("pool", zh[1][:], yf[1][:], [yf[2]], [zh[2]])
                    zf[ct] = yf; zb[ct] = zh
                    if dbg and blk == 0:
                        dma("sp", dbg_d["dbg_z"][ct * 128:(ct + 1) * 128, :], yf[1][:], [yf[2]], [])

            issue_bu(0)
            for i in range(len(iters)):
                if i + 1 < len(iters):
                    issue_bu(i + 1)
                rot_in(i)
                if i > 0:
                    rot_out(i - 1)
                scans(i)
            rot_out(len(iters) - 1)
            for k in M.values():
                Fs.give(k)
            for d_ in Wb:
                for k in d_.values():
                    Fs.give(k)
            for d_ in Tb:
                for k in d_.values():
                    Hs.give(k)
            for ct in range(4):
                Fs.give(uf[ct]); Hs.give(ub[ct])
            ya = []
            for ct in range(4):
                pg = PF.take()
                for c2 in range(4):
                    mm(pg[1][:], glub[:, c2, ct * 128:(ct + 1) * 128], zb[c2][1][:], c2 == 0, c2 == 3,
                       [b_glub, zb[c2][2]], [pg[2]])
                sg = Fs.take()
                act(sg[1][:], pg[1][:], AF.Sigmoid, [pg[2], b_pp], [sg[2]], bias=pp[:, C_GLUB + ct:C_GLUB + ct + 1])
                PF.give(pg)
                y = Hs.take()
                tt("dve", y[1][:], zf[ct][1][:], sg[1][:], ALU.mult, [zf[ct][2], sg[2]], [y[2]])
                Fs.give(sg)
                ya.append(y)
            for ct in range(4):
                Fs.give(zf[ct]); Hs.give(zb[ct])
            return ya

        def hg_branch(l, blk, extra=None):
            qs, sg_, vb, gs = [], [], [], []
            for chunk, lst, kind in ((1, qs, "silu"), (2, sg_, "sig"), (3, vb, "v"), (4, gs, "silu")):
                i = load_win(l, chunk)
                for hd in range(4):
                    p = proj(i, hd)
                    if kind == "v":
                        h = Hs.take()
                        cp("dve", h[1][:], p[1][:], [p[2]], [h[2]])
                        lst.append(h)
                    else:
                        f = Fs.take()
                        act(f[1][:], p[1][:], AF.Silu if kind == "silu" else AF.Sigmoid, [p[2]], [f[2]])
                        lst.append(f)
                    PF.give(p)
            yb = [None] * 4

            def head_gen(hd, si):
                vT = vTs[si]; kdT = kdTs[si]; smb8 = smb8s[si]; hs = hss[hd]
                b_vT = b_vTs[si]; b_kdT = b_kdTs[si]; b_smb8 = b_smb8s[si]; b_hs = b_hss[hd]
                ff = Fs.take(); lf = Fs.take(); kk = Fs.take()
                ts("dve", ff[1][:], sg_[hd][1][:], ppx[:, 4 + hd:5 + hd], ppx[:, hd:hd + 1], ALU.mult, ALU.add,
                   [sg_[hd][2], b_ppx], [ff[2]])
                Fs.give(sg_[hd])
                act(lf[1][:], ff[1][:], AF.Ln, [ff[2]], [lf[2]])
                ts("pool", kk[1][:], ff[1][:], -1.0, 1.0, ALU.mult, ALU.add, [ff[2]], [kk[2]])
                yield
                bb = ff
                scan(bb[1][:], rst, lf[1][:], 0.0, [b_cst, lf[2], kk[2]], [bb[2]])
                bv = bb[1][:].rearrange("p (c t) -> p c t", t=64)
                bm = lf
                tt("dve", bm[1][:].rearrange("p (c t) -> p c t", t=64), bv, bv[:, :, 31:32].broadcast_to([128, NCH, 64]),
                   ALU.subtract, [bb[2]], [bm[2]])
                yield
                act(hs[:, 0, :].unsqueeze(2), bv[:, :, 31:32], AF.Exp, [bb[2]], [b_hs])
                act(hs[:, 1, :].unsqueeze(2), bv[:, :, 63:64], AF.Exp, [bb[2]], [b_hs])
                tt("dve", hs[:, 3, :].unsqueeze(2), bv[:, :, 63:64], bv[:, :, 31:32], ALU.subtract, [bb[2]], [b_hs])
                act(hs[:, 2, :], hs[:, 3, :], AF.Exp, [b_hs], [b_hs])
                e1 = Fs.take(); e2 = bb
                act(e1[1][:], bm[1][:], AF.Exp, [bm[2]], [e1[2]])
                act(e2[1][:], bm[1][:], AF.Exp, [bm[2], b_hs], [e2[2]], scale=-1.0)
                yield
                qd = Hs.take(); kd = Hs.take()
                tt("pool", qd[1][:], qs[hd][1][:], e1[1][:], ALU.mult, [qs[hd][2], e1[2]], [qd[2]])
                tt("dve", kd[1][:], kk[1][:], e2[1][:], ALU.mult, [kk[2], e2[2]], [kd[2]])
                Fs.give(qs[hd]); Fs.give(e1); Fs.give(e2); Fs.give(kk); Fs.give(lf)
                yield
                psc = PF.take()
                for c in range(NCH):
                    cs = slice(c * 64, (c + 1) * 64)
                    mm(psc[1][0:64, cs], kd[1][:, cs], qd[1][:, cs], True, True, [kd[2], qd[2]], [psc[2]])
                scm = Hs.take()
                tt("dve", scm[1][0:64, :], psc[1][0:64, :], mask, ALU.mult, [psc[2], b_cst], [scm[2]])
                PF.give(psc)
                yield
                for src, dstT, bd in ((vb[hd], vT, b_vT), (kd, kdT, b_kdT)):
                    pt = PB.take()
                    for c in range(NCH):
                        tr(pt[1][0:64, c * 128:(c + 1) * 128], src[1][:, c * 64:(c + 1) * 64], identb,
                           [src[2], b_idb], [pt[2]])
                    act(dstT[0:64, :], pt[1][0:64, 0:NCH * 128], AF.Copy, [pt[2]], [bd])
                    PB.give(pt)
                Hs.give(vb[hd])
                yield
                pkv = [PF.take(), PF.take()]
                for c in range(NCH):
                    c128 = slice(c * 128, (c + 1) * 128)
                    pk = pkv[c // 4]
                    mm(pk[1][:, (c % 4) * 128:(c % 4 + 1) * 128], kdT[0:64, c128], vT[0:64, c128], True, True,
                       [b_kdT, b_vT], [pk[2]])
                yield
                t_ = Fs.take()
                for c in range(NCH):
                    pk = pkv[c // 4]
                    ts("dve", smb8[:, c * 128:(c + 1) * 128], hgS[:, hd, :], hs[:, 0, c:c + 1], None, ALU.mult, None,
                       [b_hgS[hd], b_hs], [b_smb8])
                    ts("dve", t_[1][:, 0:128], pk[1][:, (c % 4) * 128:(c % 4 + 1) * 128], hs[:, 2, c:c + 1], None, ALU.mult, None,
                       [pk[2], b_hs], [t_[2]])
                    stt(hgS[:, hd, :], hgS[:, hd, :], hs[:, 1, c:c + 1], t_[1][:, 0:128], ALU.mult, ALU.add,
                        [b_hgS[hd], b_hs, t_[2]], [b_hgS[hd]])
                Fs.give(t_)
                PF.give(pkv[0]); PF.give(pkv[1])
                yield
                po = PF.take()
                for c in range(NCH):
                    cs = slice(c * 64, (c + 1) * 64)
                    c128 = slice(c * 128, (c + 1) * 128)
                    mm(po[1][:, cs], vT[0:64, c128], scm[1][0:64, cs], True, False, [b_vT, scm[2]], [po[2]])
                    mm(po[1][:, cs], smb8[:, c128], qd[1][:, cs], False, True, [b_smb8, qd[2]], [po[2]])
                Hs.give(qd); Hs.give(kd); Hs.give(scm)
                yield
                if dbg and blk == 0:
                    od = Fs.take()
                    cp("dve", od[1][:], po[1][:], [po[2]], [od[2]])
                    dma("sp", dbg_d["dbg_o"][hd * 128:(hd + 1) * 128, :], od[1][:], [od[2]], [])
                    Fs.give(od)
                sq = Hs.take()
                act(sq[1][:], po[1][:], AF.Square, [po[2]], [sq[2]])
                pss = PF.take()
                mm(pss[1][:], onesb, sq[1][:], True, True, [b_idb, sq[2]], [pss[2]])
                Hs.give(sq)
                yield
                sr = Fs.take()
                act(sr[1][:], pss[1][:], AF.Ln, [pss[2]], [sr[2]], scale=1.0 / 128, bias=EPS)
                PF.give(pss)
                act(sr[1][:], sr[1][:], AF.Exp, [sr[2]], [sr[2]], scale=-0.5)
                tt("dve", sr[1][:], po[1][:], sr[1][:], ALU.mult, [po[2], sr[2]], [sr[2]])
                PF.give(po)
                y = Hs.take()
                stt(y[1][:], sr[1][:], pp[:, C_HNW + hd:C_HNW + hd + 1], gs[hd][1][:], ALU.mult, ALU.mult,
                    [sr[2], b_pp, gs[hd][2]], [y[2]])
                Fs.give(sr); Fs.give(gs[hd])
                yb[hd] = y
            HG_LAG = cfg.get("hg_lag", 5)
            pending = [(0, head_gen(0, 0)), (0, head_gen(1, 1)), (HG_LAG, head_gen(2, 0)), (HG_LAG, head_gen(3, 1))]
            if extra is not None:
                RG_LAG = cfg.get("rg_lag", 10)
                for n_, g_ in enumerate(extra()):
                    pending.append((RG_LAG + 2 * (n_ // 2), g_))
                pending.sort(key=lambda e: e[0])
            alive = []
            step = 0
            while pending or alive:
                while pending and pending[0][0] <= step:
                    alive.append(pending.pop(0)[1])
                for g in list(alive):
                    try:
                        next(g)
                    except StopIteration:
                        alive.remove(g)
                step += 1
            return yb

        def rg_branch(l, blk, defer=False):
            wi5 = load_win(l, 5)
            for ct in range(4):
                p = proj(wi5, ct)
                act(xce[:, ct, 3:3 + NB], p[1][:], AF.Copy, [p[2]], [b_xce[ct]])
                PF.give(p)
            wi6 = load_win(l, 6)
            gg = []
            for ct in range(4):
                p = proj(wi6, ct)
                f = Fs.take()
                act(f[1][:], p[1][:], AF.Gelu_apprx_tanh, [p[2]], [f[2]])
                PF.give(p)
                gg.append(f)
            yc = [None] * 4

            def ct_gen(ct):
                xc = Fs.take()
                cw = lambda i: pp[:, C_CW + i * 4 + ct:C_CW + i * 4 + ct + 1]
                ts("dve", xc[1][:], xce[:, ct, 0:NB], cw(0), pp[:, C_CB + ct:C_CB + ct + 1], ALU.mult, ALU.add,
                   [b_xce[ct], b_pp], [xc[2]])
                for i in range(1, 4):
                    stt(xc[1][:], xce[:, ct, i:i + NB], cw(i), xc[1][:], ALU.mult, ALU.add,
                        [b_xce[ct], b_pp, xc[2]], [xc[2]])
                cp("pool", xce[:, ct, 0:3], xce[:, ct, NB:NB + 3], [b_xce[ct]], [b_xce[ct]])
                yield
                xcb = Hs.take()
                cp("pool", xcb[1][:], xc[1][:], [xc[2]], [xcb[2]])
                pr = PF.take(); pi_ = PF.take()
                mm(pr[1][:], rgw[:, 0, ct * 128:(ct + 1) * 128], xcb[1][:], True, True, [b_rgw, xcb[2]], [pr[2]])
                mm(pi_[1][:], rgw[:, 1, ct * 128:(ct + 1) * 128], xcb[1][:], True, True, [b_rgw, xcb[2]], [pi_[2]])
                Hs.give(xcb)
                r = Fs.take(); ii = Fs.take(); a = Fs.take()
                act(r[1][:], pr[1][:], AF.Sigmoid, [pr[2], b_pp], [r[2]], bias=pp[:, C_BA + ct:C_BA + ct + 1])
                act(ii[1][:], pi_[1][:], AF.Sigmoid, [pi_[2], b_pp], [ii[2]], bias=pp[:, C_BX + ct:C_BX + ct + 1])
                PF.give(pr); PF.give(pi_)
                yield
                act(a[1][:], r[1][:], AF.Exp, [r[2], b_ppx], [a[2]], scale=ppx[:, 8 + ct:9 + ct])
                act(r[1][:], r[1][:], AF.Exp, [r[2], b_ppx], [r[2]], scale=ppx[:, 12 + ct:13 + ct])
                ts("dve", r[1][:], r[1][:], 1.0, -1.0, ALU.min, ALU.mult, [r[2]], [r[2]])
                act(r[1][:], r[1][:], AF.Sqrt, [r[2]], [r[2]], bias=1.0)
                yield
                tt("pool", ii[1][:], ii[1][:], xc[1][:], ALU.mult, [ii[2], xc[2]], [ii[2]])
                tt("dve", ii[1][:], ii[1][:], r[1][:], ALU.mult, [ii[2], r[2]], [ii[2]])
                yield
                scan(xc[1][:], a[1][:], ii[1][:], hcar[:, ct:ct + 1], [a[2], ii[2], b_hcar], [xc[2]])
                cp("dve", hcar[:, ct:ct + 1], xc[1][:, NB - 1:NB], [xc[2]], [b_hcar])
                if dbg and blk == 0:
                    dma("sp", dbg_d["dbg_h"][ct * 128:(ct + 1) * 128, :], xc[1][:], [xc[2]], [])
                y = Hs.take()
                tt("dve", y[1][:], xc[1][:], gg[ct][1][:], ALU.mult, [xc[2], gg[ct][2]], [y[2]])
                Fs.give(r); Fs.give(ii); Fs.give(a); Fs.give(xc); Fs.give(gg[ct])
                yc[ct] = y
            if defer:
                return yc, [ct_gen(ct) for ct in range(4)]
            alive = [ct_gen(ct) for ct in range(4)]
            while alive:
                for g in list(alive):
                    try:
                        next(g)
                    except StopIteration:
                        alive.remove(g)
            return yc

        def mixer(l, blk, ncol):
            do_norm(ncol)
            only = cfg.get('only_branch')
            if only is not None:
                fn = {'s5': s5_branch, 'hg': hg_branch, 'rg': rg_branch}[only]
                yy = fn(l, blk)
                if dbg and blk == 0:
                    bidx = {'s5': 0, 'hg': 1, 'rg': 2}[only]
                    for ct in range(4):
                        f = Fs.take()
                        cp("dve", f[1][:], yy[ct][1][:], [yy[ct][2]], [f[2]])
                        dma("sp", dbg_d["dbg_y"][bidx, ct * 128:(ct + 1) * 128, :], f[1][:], [f[2]], [])
                        Fs.give(f)
                for h in yy:
                    Hs.give(h)
                free_hT()
                return
            if cfg.get("rg_overlap", True):
                ya_ = s5_branch(l, blk)
                rg_box = {}

                def rg_deferred():
                    yc_, gens_ = rg_branch(l, blk, defer=True)
                    rg_box["yc"] = yc_
                    return gens_
                yb_ = hg_branch(l, blk, extra=rg_deferred)
                ys = [ya_, yb_, rg_box["yc"]]
            else:
                ys = [s5_branch(l, blk), hg_branch(l, blk), rg_branch(l, blk)]
            if dbg and blk == 0:
                for b in range(3):
                    for ct in range(4):
                        f = Fs.take()
                        cp("dve", f[1][:], ys[b][ct][1][:], [ys[b][ct][2]], [f[2]])
                        dma("sp", dbg_d["dbg_y"][b, ct * 128:(ct + 1) * 128, :], f[1][:], [f[2]], [])
                        Fs.give(f)
            macc = [Fs.take() for _ in range(8)]
            mg = [Hs.take() for _ in range(8)]
            for b in range(3):
                bi = wtake()
                wload(bi, 4096, [(v_bp(bi), bp_d[l, b].rearrange("(c p) n -> p c n", p=128))], 43 + b)
                for half in range(2):
                    i = load_win(l, 7 + 2 * b + half)
                    for f4 in range(4):
                        ft = half * 4 + f4
                        pup = PF.take()
                        for ct in range(4):
                            mm(pup[1][:], v_bp(bi)[:, ct, ft * 128:(ft + 1) * 128], ys[b][ct][1][:], ct == 0, ct == 3,
                               [b_wb[bi], ys[b][ct][2]], [pup[2]])
                        pgt = proj(i, f4)
                        sg = Fs.take()
                        act(sg[1][:], pgt[1][:], AF.Sigmoid, [pgt[2]], [sg[2]])
                        PF.give(pgt)
                        if b == 0:
                            tt("dve", macc[ft][1][:], sg[1][:], pup[1][:], ALU.mult, [sg[2], pup[2]], [macc[ft][2]])
                        else:
                            tt("dve", sg[1][:], sg[1][:], pup[1][:], ALU.mult, [sg[2], pup[2]], [sg[2]])
                            if b == 1:
                                tt("dve", macc[ft][1][:], macc[ft][1][:], sg[1][:], ALU.add, [macc[ft][2], sg[2]], [macc[ft][2]])
                            else:
                                tt("dve", mg[ft][1][:], macc[ft][1][:], sg[1][:], ALU.add, [macc[ft][2], sg[2]], [mg[ft][2]])
                        PF.give(pup); Fs.give(sg)
                for ct in range(4):
                    Hs.give(ys[b][ct])
            for f in macc:
                Fs.give(f)
            free_hT()
            if cfg.get('merge_stage', 9) < 2:
                for h in mg:
                    Hs.give(h)
                return
            wov = wo_d[l].rearrange("(k p) n -> p k n", p=128)
            wis = []
            for half in range(2):
                i = wtake()
                wload(i, 4096, [(v_win(i), wov[:, :, half * 512:(half + 1) * 512])], 46 + half)
                wis.append(i)
            for t in range(TT):
                for half in range(2):
                    i = wis[half]
                    po = PF.take()
                    for ft in range(8):
                        mm(po[1][:], mg[ft][1][:, t * 128:(t + 1) * 128], v_win(i)[:, ft, :], ft == 0, ft == 7,
                           [mg[ft][2], b_wb[i]], [po[2]])
                    xsl = xt[:, t, half * 512:(half + 1) * 512]
                    stt(xsl, po[1][:], 1.0, xsl, ALU.mult, ALU.add, [po[2], b_x[t]], [b_x[t]])
                    PF.give(po)
            for h in mg:
                Hs.give(h)

        if dbg:
            dout("dbg_x1", [NB, D]); dout("dbg_x2", [NB, D])
            dout("dbg_y", [3, 512, NB]); dout("dbg_z", [512, NB]); dout("dbg_o", [512, NB]); dout("dbg_h", [512, NB])

        def dump_x(name, blk):
            if dbg and blk == 0:
                dv = dbg_d[name].rearrange("(t p) d -> p t d", p=128)
                for t in range(TT):
                    dma("sp", dv[:, t, :], xt[:, t, :], [b_x[t]], [])

        def final_norm_store(dst_blk, b_dst):
            rc = norm_stats(32)
            fw = [Fs.take(), Fs.take()]
            for hf in range(2):
                dma("sp", fw[hf][1][:], fnw_d[:, hf * 512:(hf + 1) * 512], (), [fw[hf][2]])
            for t in range(TT):
                for hf in range(2):
                    o = Fs.take()
                    act(o[1][:], xt[:, t, hf * 512:(hf + 1) * 512], AF.Copy, [b_x[t], b_sm], [o[2]],
                        scale=sm[:, rc + t:rc + t + 1])
                    tt("dve", o[1][:], o[1][:], fw[hf][1][:], ALU.mult, [o[2], fw[hf][2]], [o[2]])
                    dma("sp", dst_blk[:, t, hf * 512:(hf + 1) * 512], o[1][:], [o[2]], [b_dst])
                    Fs.give(o)
            Fs.give(fw[0]); Fs.give(fw[1])

        if pipe:
            b_srcs = [Buf("xch_src%d" % i) for i in range(4)]; b_dsts = [Buf("xch_dst%d" % i) for i in range(4)]
            src_v = [xsrc_d.ap()[c].rearrange("(t p) d -> p t d", p=128) for c in range(4)]
            dst_v = [xdst_d.ap()[c].rearrange("(r t p) d -> r p t d", p=128, t=TT) for c in range(4)]

            def handoff(c4):
                for t in range(TT):
                    dma("pool", src_v[c4][:, t, :], xt[:, t, c4 * 256:(c4 + 1) * 256], [b_x[t]], [b_srcs[c4]])
                S.cc(lambda E: E.collective_compute("AllGather", ALU.bypass, replica_groups=groups,
                                                    ins=[xsrc_d.ap()[c4].opt()], outs=[xdst_d.ap()[c4].opt()]),
                     [b_srcs[c4]], [b_dsts[c4]])
            groups = [[2 * i, 2 * i + 1] for i in range(4)]
            layer_prep(0)
            NIT = NBLK + 1
            for j in range(NIT):
                cur_it[0] = j
                for t in range(TT):
                    if j < NBLK:
                        dma("sp", xt[:, t, :], xin_v[j][:, t, :], [], [b_x[t]])
                        ts("dve", xt[:, t, :], xt[:, t, :], role[:, 0:1], None, ALU.mult, None, [b_x[t], b_role], [b_x[t]])
                    for c4 in range(4):
                        if j == 0:
                            continue
                        e = Fs.take()
                        dma("sp", e[1][:, 0:256], dst_v[c4][0][:, t, :], [b_dsts[c4]], [e[2]])
                        xsl = xt[:, t, c4 * 256:(c4 + 1) * 256]
                        if j < NBLK:
                            stt(xsl, e[1][:, 0:256], role[:, 1:2], xsl, ALU.mult, ALU.add, [e[2], b_role, b_x[t]], [b_x[t]])
                        else:
                            ts("dve", xsl, e[1][:, 0:256], role[:, 1:2], None, ALU.mult, None, [e[2], b_role], [b_x[t]])
                        Fs.give(e)
                if "ffn0" in parts:
                    ffn(0, 0, C_NORM + 0)
                if "mix" in parts:
                    mixer(0, j, C_NORM + 8)
                if "ffn1" in parts:
                    ffn(0, 1, C_NORM + 16, after_chunk=handoff if j < NBLK else None)
                if j >= 1:
                    final_norm_store(out_v[j - 1], b_xh[j - 1])
                if j == 0:
                    fa = role[:, 0:1]
                    for ct in range(4):
                        ts("dve", s5car[:, :, ct * 4:ct * 4 + 4], s5car[:, :, ct * 4:ct * 4 + 4], fa, None, ALU.mult, None,
                           [b_car[ct], b_role], [b_car[ct]])
                        ts("dve", hgS[:, ct, :], hgS[:, ct, :], fa, None, ALU.mult, None, [b_hgS[ct], b_role], [b_hgS[ct]])
                        ts("dve", xce[:, ct, 0:3], xce[:, ct, 0:3], fa, None, ALU.mult, None, [b_xce[ct], b_role], [b_xce[ct]])
                    ts("dve", hcar[:], hcar[:], fa, None, ALU.mult, None, [b_hcar, b_role], [b_hcar])
        else:
            for l in range(NL):
                if cfg.get('prep', True):
                    layer_prep(l)
                else:
                    dma('sp', pp[:], pp_d[l], (), [b_pp])
                for blk in range(NBLK):
                    src = xin_v if l == 0 else out_v
                    for t in range(TT):
                        dma("sp", xt[:, t, :], src[blk][:, t, :], [b_xh[blk]], [b_x[t]])
                    if "ffn0" in parts:
                        ffn(l, 0, C_NORM + 0)
                    if l == 0:
                        dump_x("dbg_x1", blk)
                    if "mix" in parts:
                        mixer(l, blk, C_NORM + 8)
                    if l == 0:
                        dump_x("dbg_x2", blk)
                    if "ffn1" in parts:
                        ffn(l, 1, C_NORM + 16)
                    if l == NL - 1:
                        rc = norm_stats(32)
                        fw = [Fs.take(), Fs.take()]
                        for hf in range(2):
                            dma("sp", fw[hf][1][:], fnw_d[:, hf * 512:(hf + 1) * 512], (), [fw[hf][2]])
                        for t in range(TT):
                            for hf in range(2):
                                o = Fs.take()
                                act(o[1][:], xt[:, t, hf * 512:(hf + 1) * 512], AF.Copy, [b_x[t], b_sm], [o[2]],
                                    scale=sm[:, rc + t:rc + t + 1])
                                tt("dve", o[1][:], o[1][:], fw[hf][1][:], ALU.mult, [o[2], fw[hf][2]], [o[2]])
                                dma("sp", out_v[blk][:, t, hf * 512:(hf + 1) * 512], o[1][:], [o[2]], [b_xh[blk]])
                                Fs.give(o)
                        Fs.give(fw[0]); Fs.give(fw[1])
                    else:
                        for t in range(TT):
                            dma("sp", out_v[blk][:, t, :], xt[:, t, :], [b_x[t]], [b_xh[blk]])
        S.emit()
    return nc, dbg_d


def _prep_shared(inp):
    import ml_dtypes
    f = lambda a: np.ascontiguousarray(np.asarray(a, dtype=np.float32))
    L = 2
    pp = np.zeros((L, 128, 128), np.float32)

    def colv(v, n):
        return np.asarray(v, np.float32).reshape(n, 128).T

    for l in range(L):
        for n in range(3):
            pp[l, :, n * 8:(n + 1) * 8] = colv(inp["norm_w"][l, n], 8)
        pp[l, :, 24:28] = colv(inp["s5_d"][l], 4)
        pp[l, :, 28:32] = colv(inp["s5_glu_b"][l], 4)
        pp[l, :, 32:36] = colv(inp["hg_lb_logits"][0], 4)
        pp[l, :, 36:40] = colv(inp["hg_lb_logits"][1], 4)
        pp[l, :, 40:44] = colv(inp["hg_norm_w"][l], 4)
        for i in range(4):
            pp[l, :, 44 + i * 4:48 + i * 4] = colv(inp["rg_conv_w"][l, i], 4)
        pp[l, :, 60:64] = colv(inp["rg_conv_b"][l], 4)
        pp[l, :, 64:68] = colv(inp["rg_ba"][l], 4)
        pp[l, :, 68:72] = colv(inp["rg_bx"][l], 4)
        pp[l, :, 72:76] = colv(inp["rg_lambda"][l], 4)
    fnw = np.ascontiguousarray(np.broadcast_to(np.asarray(inp["final_norm_w"], np.float32)[None, :], (128, D)))
    lam_re = np.asarray(inp["s5_lambda_re"], np.float32)
    lam_im = np.asarray(inp["s5_lambda_im"], np.float32)
    ldt = np.repeat(np.asarray(inp["s5_log_dt"], np.float32)[:, :, None], 64, axis=2)
    rows = np.stack([lam_re.reshape(L, 2048), lam_im.reshape(L, 2048), ldt.reshape(L, 2048)], axis=1)
    s5row = np.ascontiguousarray(np.broadcast_to(rows[:, None, :, :], (L, 128, 3, 2048)))
    cols = np.stack([a.reshape(L, 16, 128).transpose(0, 2, 1) for a in (lam_re, lam_im, ldt)], axis=2)
    s5col = np.ascontiguousarray(cols)
    s5bt = np.zeros((L, 2, 128, 16, 2, 64), np.float32)
    s5c = np.zeros((L, 2, 2, 64, 16, 128), np.float32)
    for ri, (bsrc, csrc) in enumerate(((inp["s5_b_re"], inp["s5_c_re"]), (inp["s5_b_im"], inp["s5_c_im"]))):
        bsrc = np.asarray(bsrc, np.float32)
        csrc = np.asarray(csrc, np.float32)
        for g in range(32):
            st, gl = g // 2, g % 2
            c0 = (g % 8) * 16
            s5bt[:, ri, c0:c0 + 16, st, gl, :] = bsrc[:, g].transpose(0, 2, 1)
            s5c[:, ri, gl, :, st, c0:c0 + 16] = csrc[:, g].transpose(0, 2, 1)
    s5bt = s5bt.reshape(L, 2, 128, 2048)
    s5c = s5c.reshape(L, 2, 128, 2048)
    rgw = np.zeros((L, 2, 2, 64, 4, 2, 64), np.float32)
    for j, src in enumerate((inp["rg_wa"], inp["rg_wx"])):
        src = np.asarray(src, np.float32)
        for h in range(8):
            ct, hl = h // 2, h % 2
            rgw[:, j, hl, :, ct, hl, :] = src[:, h]
    rgw = rgw.reshape(L, 2, 128, 512)
    cst = np.zeros((128, 512 + NB), np.float32)
    m = np.triu(np.ones((64, 64), np.float32))
    cst[0:64, 0:512] = np.tile(m, (1, 8))
    r = np.ones((NB,), np.float32); r[::64] = 0.0
    cst[:, 512:512 + NB] = r[None, :]
    idb = np.concatenate([np.eye(128, dtype=np.float32), np.ones((128, 128), np.float32)], axis=1).astype(ml_dtypes.bfloat16)
    shared = {
        "ffn_gate": f(inp["ffn_gate"]), "ffn_up": f(inp["ffn_up"]), "ffn_down": f(inp["ffn_down"]),
        "w_in": f(inp["w_in"]), "branch_proj": f(inp["branch_proj"]), "w_out": f(inp["w_out"]),
        "s5_glu_w": f(inp["s5_glu_w"]), "pp": pp, "fnw": fnw, "s5row": s5row, "s5col": s5col,
        "s5bt": np.ascontiguousarray(s5bt), "s5c": np.ascontiguousarray(s5c), "rgw": np.ascontiguousarray(rgw),
        "cst": cst, "idb": idb,
    }
    return shared


PER_LAYER = ("ffn_gate", "ffn_up", "ffn_down", "w_in", "branch_proj", "w_out", "s5_glu_w", "pp", "s5row", "s5col",
             "s5bt", "s5c", "rgw")


def make_in_maps(inputs, n_pairs=4):
    x = np.asarray(inputs["x"], np.float32)
    shared = _prep_shared(inputs)
    in_maps = []
    for c in range(2 * n_pairs):
        b, l = c // 2, c % 2
        m = {}
        for k, v in shared.items():
            m[k] = np.ascontiguousarray(v[l:l + 1]) if k in PER_LAYER else v
        m["x"] = np.ascontiguousarray(x[b])
        role = np.zeros((128, 2), np.float32)
        role[:, l] = 1.0
        m["role"] = role
        in_maps.append(m)
    return in_maps


def kernel(**inputs):
    nc, _ = build({"pipe": True})
    in_maps = make_in_maps(inputs)
    res = run_bass_kernel_spmd(nc, in_maps, core_ids=list(range(8)))
    out = np.stack([np.asarray(res.results[2 * b + 1]["out"], np.float32).reshape(SEQ, D) for b in range(4)], axis=0)
    return out
```

```python
import math
from contextlib import ExitStack
import numpy as np
import concourse.bass as bass
import concourse.mybir as mybir
from concourse.bass_utils import run_bass_kernel_spmd

F32 = mybir.dt.float32
BF16 = mybir.dt.bfloat16
AF = mybir.ActivationFunctionType
ALU = mybir.AluOpType
ENG = ("pe", "act", "dve", "pool", "sp")

D = 1024
SEQ = 4096
DFF = 2816
NM = DFF // 128
INT = 6656
EPS = 1e-6
NB = 512
TT = NB // 128
NCH = NB // 64
TBL = 128
NCORE = 4


class Buf:
    __slots__ = ("name", "w", "r", "excl")

    def __init__(self, name, excl=False):
        self.name = name
        self.excl = excl
        self.w = None
        self.r = {}


class Sched:
    NDSEM = 24

    def __init__(self, nc, es):
        self.nc = nc
        self.ops = {e: [] for e in ENG}
        self.sem = {e: es.enter_context(nc.semaphore("s_" + e)) for e in ENG if e != "sp"}
        self.seen = {e: {} for e in ENG}
        self.QN = {"sp": 0, "pool": 1, "act": 2}
        self.NDSEM = 12 * 3
        self.dsem = [es.enter_context(nc.semaphore("d%d" % i)) for i in range(self.NDSEM)]
        self.dcnt = [0] * self.NDSEM
        self.dma_i = {"sp": 0, "pool": 0, "act": 0}
        self.cc_sem = es.enter_context(nc.semaphore("cc_sem"))
        self.cc_cnt = 0

    def _wait(self, eng, tok):
        if tok is None:
            return
        if tok[0] == "c":
            _, e2, rec = tok
            if e2 == eng and eng == "pe":
                return
            key = ("c", e2)
            if self.seen[eng].get(key, -1) >= rec["idx"]:
                return
            self.seen[eng][key] = rec["idx"]
            rec["sig"] = True
            self.ops[eng].append({"kind": "wc", "eng": e2, "rec": rec})
        elif tok[0] == "k":
            v = tok[2]
            key = ("k", "cc")
            if self.seen[eng].get(key, -1) >= v:
                return
            self.seen[eng][key] = v
            self.ops[eng].append({"kind": "wk", "v": v})
        else:
            _, slot, v = tok
            key = ("d", slot)
            if self.seen[eng].get(key, -1) >= v:
                return
            self.seen[eng][key] = v
            self.ops[eng].append({"kind": "wd", "slot": slot, "v": v})

    def _deps(self, eng, R, W):
        for b in R:
            self._wait(eng, b.w)
            if b.excl:
                for k, t in list(b.r.items()):
                    if k[1] != eng:
                        self._wait(eng, t)
        for b in W:
            self._wait(eng, b.w)
            for t in b.r.values():
                self._wait(eng, t)

    def _mark(self, tok, R, W):
        for b in W:
            b.w = tok
            b.r = {}
        for b in R:
            if b in W:
                continue
            b.r[(tok[0], tok[1])] = tok

    def op(self, eng, fn, R=(), W=()):
        self._deps(eng, R, W)
        rec = {"kind": "op", "fn": fn, "sig": False, "idx": len(self.ops[eng])}
        self.ops[eng].append(rec)
        self._mark(("c", eng, rec), R, W)

    def dma(self, q, fn, R=(), W=()):
        self._deps(q, R, W)
        slot = self.QN[q] * 12 + self.dma_i[q] % 12
        self.dma_i[q] += 1
        if self.dcnt[slot] > 0:
            self._wait(q, ("d", slot, self.dcnt[slot]))
        self.dcnt[slot] += 1
        self.ops[q].append({"kind": "dma", "fn": fn, "slot": slot})
        self._mark(("d", slot, self.dcnt[slot]), R, W)

    def cc(self, fn, R=(), W=()):
        self._deps("pool", R, W)
        self.cc_cnt += 1
        self.ops["pool"].append({"kind": "cc", "fn": fn})
        self._mark(("k", "cc", self.cc_cnt), R, W)

    def emit(self):
        nc = self.nc
        if self.cc_cnt:
            self._wait("pool", ("k", "cc", self.cc_cnt))
        for slot in range(self.NDSEM):
            if self.dcnt[slot] > 0:
                self._wait("sp", ("d", slot, self.dcnt[slot]))
        for e in ENG:
            k = 0
            for o in self.ops[e]:
                if o["kind"] == "op" and o["sig"]:
                    k += 1
                    o["sidx"] = k

        def body(e):
            def f(E):
                for o in self.ops[e]:
                    kd = o["kind"]
                    if kd == "wc":
                        E.wait_ge(self.sem[o["eng"]], o["rec"]["sidx"])
                    elif kd == "wd":
                        E.wait_ge(self.dsem[o["slot"]], 16 * o["v"])
                    elif kd == "wk":
                        E.wait_ge(self.cc_sem, o["v"])
                    elif kd == "cc":
                        o["fn"](E).then_inc(self.cc_sem)
                    elif kd == "op":
                        ins = o["fn"](E)
                        if o["sig"]:
                            ins.then_inc(self.sem[e], 1)
                    else:
                        o["fn"](E).then_inc(self.dsem[o["slot"]], 16)
            return f

        with nc.Block() as blk:
            blk.tensor(body("pe"))
            blk.scalar(body("act"))
            blk.vector(body("dve"))
            blk.gpsimd(body("pool"))
            blk.sync(body("sp"))


class Slots:
    def __init__(self, tiles, name, excl=False):
        self.items = [(t, Buf("%s%d" % (name, i), excl)) for i, t in enumerate(tiles)]
        self.free = list(range(len(tiles)))
        self.name = name

    def take(self):
        if not self.free:
            raise RuntimeError("out of slots: " + self.name)
        i = self.free.pop(0)
        t, b = self.items[i]
        return (i, t, b)

    def give(self, h):
        assert h[0] not in self.free
        self.free.append(h[0])


def build(cfg):
    NL = cfg.get("n_layers", 2)
    NBLK = cfg.get("n_blocks", SEQ // NB)
    dbg = cfg.get("dbg", False)
    parts = cfg.get("parts", ("ffn0", "mix", "ffn1"))
    pipe = cfg.get("pipe", False)
    LW = 1 if pipe else 2
    nc = bass.Bass("TRN2", target_bir_lowering=False)

    def din(name, shape, dt=F32):
        return nc.dram_tensor(name, list(shape), dt, kind="ExternalInput").ap()

    x_d = din("x", [SEQ, D])
    out_d = nc.dram_tensor("out", [SEQ, D], F32, kind="ExternalOutput").ap()
    wg_d = din("ffn_gate", [LW, 2, D, DFF])
    wu_d = din("ffn_up", [LW, 2, D, DFF])
    wd_d = din("ffn_down", [LW, 2, DFF, D])
    win_d = din("w_in", [LW, D, INT])
    bp_d = din("branch_proj", [LW, 3, 512, D])
    wo_d = din("w_out", [LW, D, D])
    glu_d = din("s5_glu_w", [LW, 512, 512])
    pp_d = din("pp", [LW, 128, 128])
    fnw_d = din("fnw", [128, D])
    s5row_d = din("s5row", [LW, 128, 3, 2048])
    s5col_d = din("s5col", [LW, 128, 3, 16])
    s5bt_d = din("s5bt", [LW, 2, 128, 2048])
    s5c_d = din("s5c", [LW, 2, 128, 2048])
    rgw_d = din("rgw", [LW, 2, 128, 512])
    cst_d = din("cst", [128, 512 + NB])
    idb_d = din("idb", [128, 256], BF16)
    role_d = din("role", [128, 2])
    NSC = 48
    wsc_on = pipe and cfg.get("wscratch", True)
    if pipe:
        xsrc_d = nc.dram_tensor("xch_src", [4, NB, 256], F32)
        xdst_d = nc.dram_tensor("xch_dst", [4, 2 * NB, 256], F32)
    if wsc_on:
        wsc_t = nc.dram_tensor("wscratch", [NSC, 128, NM * 256], BF16)
    dbg_d = {}

    def dout(name, shape):
        dbg_d[name] = nc.dram_tensor(name, list(shape), F32, kind="ExternalOutput").ap()
        return dbg_d[name]

    es = ExitStack()
    with es:
        S = Sched(nc, es)

        def sb(name, shape, dt=F32):
            return es.enter_context(nc.sbuf_tensor("sb_" + name, list(shape), dt))

        def act(out, in_, func, R, W, scale=None, bias=None, accum=None):
            kw = {}
            if scale is not None:
                kw["scale"] = scale
            if bias is not None:
                kw["bias"] = bias
            if accum is not None:
                kw["accum_out"] = accum
            S.op("act", lambda E: E.activation(out=out, in_=in_, func=func, **kw), R, W)

        def tt(eng, out, a, b, op, R, W):
            S.op(eng, lambda E: E.tensor_tensor(out=out, in0=a, in1=b, op=op), R, W)

        def ts(eng, out, a, s1, s2, op0, op1, R, W):
            if op1 is None:
                S.op(eng, lambda E: E.tensor_scalar(out=out, in0=a, scalar1=s1, scalar2=None, op0=op0), R, W)
            else:
                S.op(eng, lambda E: E.tensor_scalar(out=out, in0=a, scalar1=s1, scalar2=s2, op0=op0, op1=op1), R, W)

        def stt(out, a, s, b, op0, op1, R, W):
            S.op("dve", lambda E: E.scalar_tensor_tensor(out=out, in0=a, scalar=s, in1=b, op0=op0, op1=op1), R, W)

        def mm(out, lhsT, rhs, start, stop, R, W):
            S.op("pe", lambda E: E.matmul(out, lhsT=lhsT, rhs=rhs, start=start, stop=stop), R, W)

        def tr(out, in_, ident, R, W):
            S.op("pe", lambda E: E.transpose(out, in_, ident), R, W)

        def scan(out, d0, d1, init, R, W):
            S.op("dve", lambda E: E.tensor_tensor_scan(out=out, data0=d0, data1=d1, initial=init,
                                                       op0=ALU.mult, op1=ALU.add), R, W)

        def cp(eng, out, in_, R, W):
            if eng == "act":
                act(out, in_, AF.Copy, R, W)
            else:
                S.op(eng, lambda E: E.tensor_copy(out=out, in_=in_), R, W)

        def recip(out, in_, R, W):
            S.op("dve", lambda E: E.reciprocal(out=out, in_=in_), R, W)

        def memset(eng, ap, val, W):
            S.op(eng, lambda E: E.memset(ap, val), (), W)

        def dma(q, out, in_, R, W):
            S.dma(q, lambda E: E.dma_start(out=out, in_=in_), R, W)

        NF, NH = 24, 32
        Fs = Slots([sb("F%d" % i, [128, 512]) for i in range(NF)], "F")
        Hs = Slots([sb("H%d" % i, [128, 512], BF16) for i in range(NH)], "H")
        PF = Slots([es.enter_context(nc.psum_tensor("pf%d" % i, [128, 512], F32)) for i in range(6)], "pf", True)
        PB = Slots([es.enter_context(nc.psum_tensor("pb%d" % i, [128, 1024], BF16)) for i in range(2)], "pb", True)

        cst = sb("cst", [128, 512 + NB]); b_cst = Buf("cst")
        idb = sb("idb", [128, 256], BF16); b_idb = Buf("idb")
        dma("sp", cst[:], cst_d, (), [b_cst])
        dma("sp", idb[:], idb_d, (), [b_idb])
        mask = cst[0:64, 0:512]
        rst = cst[:, 512:512 + NB]
        identb = idb[:, 0:128]
        onesb = idb[:, 128:256]

        xt = sb("xt", [128, TT, D]); b_x = [Buf("x%d" % t) for t in range(TT)]
        xs = sb("xs", [128, 2, D], BF16); b_xs = [Buf("xs0"), Buf("xs1")]
        sm = sb("sm", [128, 64]); b_sm = Buf("sm")
        pp = sb("pp", [128, 128]); b_pp = Buf("pp")
        role = sb("role", [128, 2]); b_role = Buf("role")
        dma("sp", role[:], role_d, (), [b_role])
        ppx = sb("ppx", [128, 32]); b_ppx = Buf("ppx")
        NWB = 3
        wbt = [sb("wb%d" % i, [128, NM * 256], BF16) for i in range(NWB)]
        b_wb = [Buf("wb%d" % i) for i in range(NWB)]
        wctr = [0]

        def wtake():
            i = wctr[0] % NWB
            wctr[0] += 1
            return i

        cur_it = [0]
        b_wsc = [Buf("wsc%d" % k) for k in range(NSC)]

        def wload(i, n, pieces, cid):
            if (not wsc_on) or cur_it[0] == 0:
                for view, src in pieces:
                    dma("pool", view, src, (), [b_wb[i]])
                if wsc_on:
                    dma("sp", wsc_t.ap()[cid, :, 0:n], wbt[i][:, 0:n], [b_wb[i]], [b_wsc[cid]])
            else:
                dma("sp", wbt[i][:, 0:n], wsc_t.ap()[cid, :, 0:n], [b_wsc[cid]], [b_wb[i]])

        def v_win(i):
            return wbt[i][:, 0:4096].rearrange("p (k n) -> p k n", n=512)

        def v_gu(i, j):
            return wbt[i][:, j * 2048:(j + 1) * 2048].rearrange("p (k n) -> p k n", n=256)

        def v_wd(i):
            return wbt[i][:, :].rearrange("p (m n) -> p m n", n=256)

        def v_bp(i):
            return wbt[i][:, 0:4096].rearrange("p (c n) -> p c n", n=D)
        glub = sb("glub", [128, 4, 512], BF16); b_glub = Buf("glub")
        brT = sb("brT", [128, 16, 128], BF16); biT = sb("biT", [128, 16, 128], BF16)
        crT = sb("crT", [128, 16, 128], BF16); nciT = sb("nciT", [128, 16, 128], BF16)
        b_s5m = Buf("s5m")
        cosT = sb("cosT", [128, 16, TBL]); sinT = sb("sinT", [128, 16, TBL]); b_tab = Buf("tab")
        s5c = sb("s5c", [128, 8, 16]); b_s5c = Buf("s5c")
        s5car = sb("s5car", [128, 2, 16]); b_car = [Buf("car%d" % i) for i in range(4)]
        s5cs = sb("s5cs", [128, 16]); b_s5cs = Buf("s5cs")
        rho0 = sb("rho0", [128, 16, TBL]); b_rho0 = Buf("rho0")
        rgw = sb("rgw", [128, 2, 512], BF16); b_rgw = Buf("rgw")
        xce = sb("xce", [128, 4, 3 + NB]); b_xce = [Buf("xce%d" % i) for i in range(4)]
        hcar = sb("hcar", [128, 4]); b_hcar = Buf("hcar")
        hgS = sb("hgS", [128, 4, 128]); b_hgS = [Buf("hgS%d" % i) for i in range(4)]
        smb8s = [sb("smb8_%d" % i, [128, NCH * 128], BF16) for i in range(2)]; b_smb8s = [Buf("smb8_%d" % i) for i in range(2)]
        vTs = [sb("vT%d" % i, [128, NCH * 128], BF16) for i in range(2)]; b_vTs = [Buf("vT%d" % i) for i in range(2)]
        kdTs = [sb("kdT%d" % i, [128, NCH * 128], BF16) for i in range(2)]; b_kdTs = [Buf("kdT%d" % i) for i in range(2)]
        hss = [sb("hs%d" % i, [128, 4, NCH]) for i in range(4)]; b_hss = [Buf("hs%d" % i) for i in range(4)]

        C_NORM = 0
        C_D = 24; C_GLUB = 28; C_LB0 = 32; C_LB1 = 36; C_HNW = 40
        C_CW = 44
        C_CB = 60; C_BA = 64; C_BX = 68; C_LAM = 72

        xin_v = x_d.rearrange("(b t p) d -> b p t d", p=128, t=TT)
        out_v = out_d.rearrange("(b t p) d -> b p t d", p=128, t=TT)
        b_xh = [Buf("xh%d" % i) for i in range(SEQ // NB)]

        hT = [None] * 8

        def norm_stats(col0):
            for t in range(TT):
                act(xs[:, t % 2, :], xt[:, t, :], AF.Square, [b_x[t]], [b_xs[t % 2], b_sm], accum=sm[:, col0 + t:col0 + t + 1])
            act(sm[:, col0 + 8:col0 + 8 + TT], sm[:, col0:col0 + TT], AF.Sqrt, [b_sm], [b_sm], scale=1.0 / D, bias=EPS)
            recip(sm[:, col0 + 16:col0 + 16 + TT], sm[:, col0 + 8:col0 + 8 + TT], [b_sm], [b_sm])
            return col0 + 16

        def do_norm(ncol):
            nst = cfg.get('norm_stage', 9)
            rc = norm_stats(0)
            hs_ = [Hs.take() for _ in range(8)]
            for tp in range(TT // 2):
                for j in range(2):
                    t = tp * 2 + j
                    if nst >= 1:
                        act(xs[:, j, :], xt[:, t, :], AF.Copy, [b_x[t], b_sm], [b_xs[j]], scale=sm[:, rc + t:rc + t + 1])
                for kq in range(2):
                    pbh = PB.take()
                    for kk in range(4):
                        k = kq * 4 + kk
                        for j in range(2):
                            if nst >= 2:
                                tr(pbh[1][:, (kk * 2 + j) * 128:(kk * 2 + j + 1) * 128], xs[:, j, k * 128:(k + 1) * 128], identb,
                                   [b_xs[j], b_idb], [pbh[2]])
                    for kk in range(4):
                        k = kq * 4 + kk
                        src = pbh[1][:, kk * 256:(kk + 1) * 256]
                        dst = hs_[k][1][:, tp * 256:(tp + 1) * 256]
                        wcol = pp[:, ncol + k:ncol + k + 1]
                        emode = cfg.get('evac_mode', 'mix')
                        if (kk % 2 == 0 and emode == 'mix') or emode == 'act':
                            if nst >= 3:
                                act(dst, src, AF.Copy, [pbh[2], b_pp], [hs_[k][2]], scale=wcol)
                        else:
                            if nst >= 4:
                                ts("dve", dst, src, wcol, None, ALU.mult, None, [pbh[2], b_pp], [hs_[k][2]])
                    PB.give(pbh)
            for k in range(8):
                hT[k] = hs_[k]

        def free_hT():
            for k in range(8):
                Hs.give(hT[k])
                hT[k] = None

        def ffn(l, which, ncol, after_chunk=None):
            do_norm(ncol)
            wg_v = wg_d[l, which].rearrange("(k p) n -> p k n", p=128)
            wu_v = wu_d[l, which].rearrange("(k p) n -> p k n", p=128)
            wd_v = wd_d[l, which].rearrange("(m p) n -> p m n", p=128)
            acts = []
            stage = cfg.get('ffn_stage', 2)
            if stage == 0:
                free_hT()
                return
            for c in range(NM // 2):
                i = wtake()
                wload(i, 4096, [(v_gu(i, 0), wg_v[:, :, c * 256:(c + 1) * 256]),
                                (v_gu(i, 1), wu_v[:, :, c * 256:(c + 1) * 256])], which * 15 + c)
                for mi in range(2):
                    pg = PF.take(); pu = PF.take()
                    for k in range(8):
                        mm(pg[1][:], v_gu(i, 0)[:, k, mi * 128:(mi + 1) * 128], hT[k][1][:], k == 0, k == 7,
                           [b_wb[i], hT[k][2]], [pg[2]])
                    for k in range(8):
                        mm(pu[1][:], v_gu(i, 1)[:, k, mi * 128:(mi + 1) * 128], hT[k][1][:], k == 0, k == 7,
                           [b_wb[i], hT[k][2]], [pu[2]])
                    sg = Fs.take()
                    act(sg[1][:], pg[1][:], AF.Silu, [pg[2]], [sg[2]])
                    a = Hs.take()
                    tt("dve", a[1][:], sg[1][:], pu[1][:], ALU.mult, [sg[2], pu[2]], [a[2]])
                    Fs.give(sg); PF.give(pg); PF.give(pu)
                    acts.append(a)
            free_hT()
            if stage == 1:
                for a in acts:
                    Hs.give(a)
                return
            for c4 in range(4):
                i = wtake()
                wload(i, NM * 256, [(v_wd(i), wd_v[:, :, c4 * 256:(c4 + 1) * 256])], which * 15 + 11 + c4)
                for t in range(TT):
                    pd = PF.take()
                    for m in range(NM):
                        mm(pd[1][:, 0:256], acts[m][1][:, t * 128:(t + 1) * 128], v_wd(i)[:, m, :], m == 0, m == NM - 1,
                           [acts[m][2], b_wb[i]], [pd[2]])
                    xsl = xt[:, t, c4 * 256:(c4 + 1) * 256]
                    stt(xsl, pd[1][:, 0:256], 0.5, xsl, ALU.mult, ALU.add, [pd[2], b_x[t]], [b_x[t]])
                    PF.give(pd)
                if after_chunk is not None:
                    after_chunk(c4)
            for a in acts:
                Hs.give(a)

        def layer_prep(l):
            dma("sp", pp[:], pp_d[l], (), [b_pp])
            if pipe:
                tt("dve", ppx[:, 0:4], pp[:, C_LB1:C_LB1 + 4], pp[:, C_LB0:C_LB0 + 4], ALU.subtract, [b_pp], [b_ppx])
                act(ppx[:, 0:4], ppx[:, 0:4], AF.Sigmoid, [b_ppx], [b_ppx])
                ts("dve", ppx[:, 0:4], ppx[:, 0:4], role[:, 1:2], None, ALU.mult, None, [b_ppx, b_role], [b_ppx])
            elif l == 0:
                ts("dve", ppx[:, 0:4], pp[:, C_LB0:C_LB0 + 4], 0.0, None, ALU.mult, None, [b_pp], [b_ppx])
            else:
                tt("dve", ppx[:, 0:4], pp[:, C_LB1:C_LB1 + 4], pp[:, C_LB0:C_LB0 + 4], ALU.subtract, [b_pp], [b_ppx])
                act(ppx[:, 0:4], ppx[:, 0:4], AF.Sigmoid, [b_ppx], [b_ppx])
            ts("dve", ppx[:, 4:8], ppx[:, 0:4], -1.0, 1.0, ALU.mult, ALU.add, [b_ppx], [b_ppx])
            act(ppx[:, 16:20], pp[:, C_LAM:C_LAM + 4], AF.Exp, [b_pp], [b_ppx], scale=-1.0)
            act(ppx[:, 16:20], ppx[:, 16:20], AF.Ln, [b_ppx], [b_ppx], bias=1.0)
            ts("dve", ppx[:, 8:12], ppx[:, 16:20], -8.0, None, ALU.mult, None, [b_ppx], [b_ppx])
            ts("dve", ppx[:, 12:16], ppx[:, 16:20], -16.0, None, ALU.mult, None, [b_ppx], [b_ppx])
            dma("pool", glub[:], glu_d[l].rearrange("(c p) n -> p c n", p=128), (), [b_glub])
            for j in range(2):
                dma("pool", rgw[:, j, :], rgw_d[l, j], (), [b_rgw])
            memset("pool", s5car[:], 0.0, b_car)
            memset("pool", hcar[:], 0.0, [b_hcar])
            for i in range(4):
                memset("pool", hgS[:, i, :], 0.0, [b_hgS[i]])
                memset("pool", xce[:, i, 0:3], 0.0, [b_xce[i]])
            if cfg.get('s5prep', True):
                s5_prep(l)

        def trig_chain(n, th, out_c, out_s, f, R, W, width):
            c, s_, t1, t2 = f
            act(s_, th, AF.Sin, R, W, scale=1.0 / 32)
            act(c, th, AF.Sin, R, W, scale=-1.0 / 32, bias=math.pi / 2)
            for i in range(5):
                tt("dve", t1, c, c, ALU.mult, W, W)
                tt("dve", t2, s_, s_, ALU.mult, W, W)
                stt(s_, s_, 2.0, c, ALU.mult, ALU.mult, W, W)
                tt("dve", c, t1, t2, ALU.subtract, W, W)
            return c, s_

        def s5_prep(l):
            dma("sp", s5c[:, 0:3, :], s5col_d[l], (), [b_s5c])
            W = [b_s5c]
            lr = s5c[:, 0, :]; li = s5c[:, 1, :]; dt = s5c[:, 2, :]
            act(dt, dt, AF.Exp, W, W)
            ts("dve", lr, lr, -1e-4, None, ALU.min, None, W, W)
            tt("dve", s5c[:, 3, :], lr, dt, ALU.mult, W, W)
            act(s5c[:, 3, :], s5c[:, 3, :], AF.Exp, W, W)
            tt("dve", s5c[:, 4, :], li, dt, ALU.mult, W, W)
            trig_chain(16, s5c[:, 4, :], None, None, [s5c[:, 5, :], s5c[:, 6, :], s5c[:, 7, :], s5c[:, 0, :]], W, W, 16)
            cp("dve", rho0[:], s5c[:, 3, :].unsqueeze(2).broadcast_to([128, 16, TBL]), W, [b_rho0])
            memset("dve", rho0[:, :, 0:1], 0.0, [b_rho0])
            WT = [b_tab]
            cp("dve", cosT[:, :, 0:1], s5c[:, 5, :].unsqueeze(2), W, WT)
            cp("dve", sinT[:, :, 0:1], s5c[:, 6, :].unsqueeze(2), W, WT)
            n = 1
            while n < TBL:
                for h0 in range(0, 16, 8):
                    if n * 8 > 512:
                        raise RuntimeError("tbl")
                    fa = Fs.take(); fb = Fs.take()
                    sl = slice(h0, h0 + 8)
                    cn = cosT[:, sl, n - 1:n].broadcast_to([128, 8, n])
                    sn = sinT[:, sl, n - 1:n].broadcast_to([128, 8, n])
                    c0 = cosT[:, sl, 0:n]; s0 = sinT[:, sl, 0:n]
                    ta = fa[1][:, 0:8 * n].rearrange("p (a b) -> p a b", b=n)
                    tb = fb[1][:, 0:8 * n].rearrange("p (a b) -> p a b", b=n)
                    tt("dve", ta, c0, cn, ALU.mult, WT, [fa[2]])
                    tt("dve", tb, s0, sn, ALU.mult, WT, [fb[2]])
                    tt("dve", cosT[:, sl, n:2 * n], ta, tb, ALU.subtract, [fa[2], fb[2]], WT)
                    tt("dve", ta, s0, cn, ALU.mult, WT, [fa[2]])
                    tt("dve", tb, c0, sn, ALU.mult, WT, [fb[2]])
                    tt("dve", sinT[:, sl, n:2 * n], ta, tb, ALU.add, [fa[2], fb[2]], WT)
                    Fs.give(fa); Fs.give(fb)
                n *= 2
            WM = [b_s5m]
            for cq in range(4):
                cs = slice(cq * 512, (cq + 1) * 512)
                f = [Fs.take() for _ in range(10)]
                A = [h[1][:] for h in f]
                Bf = [h[2] for h in f]
                lr, li, dt, rho, th, c, s_, t1, t2, t3 = A
                dma("sp", lr, s5row_d[l, :, 0, cs], (), [Bf[0]])
                dma("sp", li, s5row_d[l, :, 1, cs], (), [Bf[1]])
                dma("sp", dt, s5row_d[l, :, 2, cs], (), [Bf[2]])
                act(dt, dt, AF.Exp, Bf, Bf)
                ts("dve", lr, lr, -1e-4, None, ALU.min, None, Bf, Bf)
                tt("dve", rho, lr, dt, ALU.mult, Bf, Bf)
                act(rho, rho, AF.Exp, Bf, Bf)
                tt("dve", th, li, dt, ALU.mult, Bf, Bf)
                trig_chain(512, th, None, None, [c, s_, t1, t2], Bf, Bf, 512)
                tt("dve", c, c, rho, ALU.mult, Bf, Bf)
                ts("dve", c, c, -1.0, None, ALU.add, None, Bf, Bf)
                tt("dve", s_, s_, rho, ALU.mult, Bf, Bf)
                tt("dve", t1, lr, lr, ALU.mult, Bf, Bf)
                tt("dve", t2, li, li, ALU.mult, Bf, Bf)
                tt("dve", t1, t1, t2, ALU.add, Bf, Bf)
                recip(t1, t1, Bf, Bf)
                tt("dve", t2, c, lr, ALU.mult, Bf, Bf)
                tt("dve", t3, s_, li, ALU.mult, Bf, Bf)
                tt("dve", t2, t2, t3, ALU.add, Bf, Bf)
                tt("dve", t2, t2, t1, ALU.mult, Bf, Bf)
                tt("dve", t3, s_, lr, ALU.mult, Bf, Bf)
                tt("dve", th, c, li, ALU.mult, Bf, Bf)
                tt("dve", t3, t3, th, ALU.subtract, Bf, Bf)
                tt("dve", t3, t3, t1, ALU.mult, Bf, Bf)
                fr, fi = t2, t3
                dma("sp", lr, s5bt_d[l, 0, :, cs], Bf, [Bf[0]])
                dma("sp", li, s5bt_d[l, 1, :, cs], Bf, [Bf[1]])
                st4 = slice(cq * 4, cq * 4 + 4)
                tt("dve", dt, fr, lr, ALU.mult, Bf, Bf)
                tt("dve", rho, fi, li, ALU.mult, Bf, Bf)
                tt("dve", brT[:, st4, :].rearrange("p a b -> p (a b)"), dt, rho, ALU.subtract, Bf, WM)
                tt("dve", dt, fr, li, ALU.mult, Bf, Bf)
                tt("dve", rho, fi, lr, ALU.mult, Bf, Bf)
                tt("dve", biT[:, st4, :].rearrange("p a b -> p (a b)"), dt, rho, ALU.add, Bf, WM)
                dma("sp", c, s5c_d[l, 0, :, cs], Bf, [Bf[5]])
                dma("sp", s_, s5c_d[l, 1, :, cs], Bf, [Bf[6]])
                cp("dve", crT[:, st4, :].rearrange("p a b -> p (a b)"), c, Bf, WM)
                ts("dve", nciT[:, st4, :].rearrange("p a b -> p (a b)"), s_, -1.0, None, ALU.mult, None, Bf, WM)
                for h in f:
                    Fs.give(h)

        def load_win(l, chunk):
            i = wtake()
            wv = win_d[l].rearrange("(k p) n -> p k n", p=128)
            wload(i, 4096, [(v_win(i), wv[:, :, chunk * 512:(chunk + 1) * 512])], 30 + chunk)
            return i

        def proj(i, ct):
            p = PF.take()
            for k in range(8):
                mm(p[1][:], v_win(i)[:, k, ct * 128:(ct + 1) * 128], hT[k][1][:], k == 0, k == 7,
                   [b_wb[i], hT[k][2]], [p[2]])
            return p

        def s5_branch(l, blk):
            wi0 = load_win(l, 0)
            uf, ub = [], []
            for ct in range(4):
                p = proj(wi0, ct)
                f = Fs.take(); h = Hs.take()
                act(f[1][:], p[1][:], AF.Copy, [p[2]], [f[2]])
                cp("dve", h[1][:], p[1][:], [p[2]], [h[2]])
                PF.give(p)
                uf.append(f); ub.append(h)
            zf, zb = [None] * 4, [None] * 4
            M = {k: Fs.take() for k in ("m1", "m2", "m3", "m4", "bpr", "bpi", "t1", "t2", "t3", "t4")}
            Wb = [{k: Fs.take() for k in ("wr", "wi")} for _ in range(2)]
            Tb = [{k: Hs.take() for k in ("t1", "t2", "t3", "t4")} for _ in range(2)]

            def v3(h):
                return h[1][:].rearrange("p (a b) -> p a b", b=TBL)
            iters = [(ct, tb) for ct in range(4) for tb in range(NB // TBL)]
            NTB = NB // TBL
            bu = {}
            pyh = {}

            def issue_bu(i):
                ct, tb = iters[i]
                tsl = slice(tb * TBL, (tb + 1) * TBL)
                pbr = PF.take(); pbi = PF.take()
                for j in range(4):
                    st = ct * 4 + j
                    mm(pbr[1][:, j * TBL:(j + 1) * TBL], brT[:, st, :], ub[ct][1][:, tsl], True, True,
                       [b_s5m, ub[ct][2]], [pbr[2]])
                    mm(pbi[1][:, j * TBL:(j + 1) * TBL], biT[:, st, :], ub[ct][1][:, tsl], True, True,
                       [b_s5m, ub[ct][2]], [pbi[2]])
                bu[i] = (pbr, pbi)

            def rot_in(i):
                ct, tb = iters[i]
                st4 = slice(ct * 4, ct * 4 + 4)
                cT = cosT[:, st4, :]; sT = sinT[:, st4, :]
                pbr, pbi = bu.pop(i)
                tt("dve", v3(M["m1"]), v3(pbr), cT, ALU.mult, [pbr[2], b_tab], [M["m1"][2]])
                tt("dve", v3(M["m2"]), v3(pbi), sT, ALU.mult, [pbi[2], b_tab], [M["m2"][2]])
                tt("pool", M["bpr"][1][:], M["m1"][1][:], M["m2"][1][:], ALU.add, [M["m1"][2], M["m2"][2]], [M["bpr"][2]])
                tt("dve", v3(M["m3"]), v3(pbi), cT, ALU.mult, [pbi[2], b_tab], [M["m3"][2]])
                tt("dve", v3(M["m4"]), v3(pbr), sT, ALU.mult, [pbr[2], b_tab], [M["m4"][2]])
                tt("pool", M["bpi"][1][:], M["m3"][1][:], M["m4"][1][:], ALU.subtract, [M["m3"][2], M["m4"][2]], [M["bpi"][2]])
                PF.give(pbr); PF.give(pbi)

            def scans(i):
                ct, tb = iters[i]
                st4 = slice(ct * 4, ct * 4 + 4)
                Wc = Wb[i % 2]
                for nm, src, ri in (("wr", "bpr", 0), ("wi", "bpi", 1)):
                    o = ri * 8
                    tt("dve", s5cs[:, o:o + 4], s5c[:, 3, st4], s5car[:, ri, st4], ALU.mult, [b_s5c, b_car[ct]], [b_s5cs])
                    first = v3(M[src])[:, :, 0:1]
                    tt("dve", first, first, s5cs[:, o:o + 4].unsqueeze(2), ALU.add, [M[src][2], b_s5cs], [M[src][2]])
                    scan(Wc[nm][1][:], rho0[:, st4, :].rearrange("p a b -> p (a b)"), M[src][1][:], 0.0,
                         [b_rho0, M[src][2]], [Wc[nm][2]])

            def rot_out(i):
                ct, tb = iters[i]
                tsl = slice(tb * TBL, (tb + 1) * TBL)
                st4 = slice(ct * 4, ct * 4 + 4)
                cT = cosT[:, st4, :]; sT = sinT[:, st4, :]
                Wc = Wb[i % 2]; Tc = Tb[i % 2]
                L = slice(TBL - 1, TBL)
                tt("dve", v3(M["t1"]), v3(Wc["wr"]), cT, ALU.mult, [Wc["wr"][2], b_tab], [M["t1"][2]])
                tt("dve", v3(M["t2"]), v3(Wc["wi"]), sT, ALU.mult, [Wc["wi"][2], b_tab], [M["t2"][2]])
                tt("pool", v3(M["t3"]), v3(Wc["wr"]), sT, ALU.mult, [Wc["wr"][2], b_tab], [M["t3"][2]])
                tt("pool", v3(M["t4"]), v3(Wc["wi"]), cT, ALU.mult, [Wc["wi"][2], b_tab], [M["t4"][2]])
                tt("dve", s5car[:, 0, st4].unsqueeze(2), v3(M["t1"])[:, :, L], v3(M["t2"])[:, :, L], ALU.subtract,
                   [M["t1"][2], M["t2"][2]], [b_car[ct]])
                tt("dve", s5cs[:, 4:8].unsqueeze(2), v3(Wc["wr"])[:, :, L], sT[:, :, L], ALU.mult, [Wc["wr"][2], b_tab], [b_s5cs])
                tt("dve", s5cs[:, 12:16].unsqueeze(2), v3(Wc["wi"])[:, :, L], cT[:, :, L], ALU.mult, [Wc["wi"][2], b_tab], [b_s5cs])
                tt("dve", s5car[:, 1, st4], s5cs[:, 4:8], s5cs[:, 12:16], ALU.add, [b_s5cs], [b_car[ct]])
                act(Tc["t1"][1][:], M["t1"][1][:], AF.Copy, [M["t1"][2]], [Tc["t1"][2]])
                act(Tc["t2"][1][:], M["t2"][1][:], AF.Copy, [M["t2"][2]], [Tc["t2"][2]], scale=-1.0)
                act(Tc["t3"][1][:], M["t3"][1][:], AF.Copy, [M["t3"][2]], [Tc["t3"][2]])
                act(Tc["t4"][1][:], M["t4"][1][:], AF.Copy, [M["t4"][2]], [Tc["t4"][2]])
                if tb == 0:
                    pyh[ct] = PF.take()
                py = pyh[ct]
                n = 0
                for j in range(4):
                    st = ct * 4 + j
                    js = slice(j * TBL, (j + 1) * TBL)
                    for nm, wT in (("t1", crT), ("t2", crT), ("t3", nciT), ("t4", nciT)):
                        mm(py[1][:, tsl], wT[:, st, :], Tc[nm][1][:, js], n == 0, n == 15, [b_s5m, Tc[nm][2]], [py[2]])
                        n += 1
                if tb == NTB - 1:
                    yf = Fs.take()
                    stt(yf[1][:], uf[ct][1][:], pp[:, C_D + ct:C_D + ct + 1], py[1][:], ALU.mult, ALU.add,
                        [uf[ct][2], b_pp, py[2]], [yf[2]])
                    PF.give(py)
                    act(yf[1][:], yf[1][:], AF.Gelu_apprx_tanh, [yf[2]], [yf[2]])
                    zh = Hs.take()
                    cp("pool", zh[1][:], yf[1][:], [yf[2]], [zh[2]])
                    zf[ct] = yf; zb[ct] = zh
                    if dbg and blk == 0:
                        dma("sp", dbg_d["dbg_z"][ct * 128:(ct + 1) * 128, :], yf[1][:], [yf[2]], [])

            issue_bu(0)
            for i in range(len(iters)):
                if i + 1 < len(iters):
                    issue_bu(i + 1)
                rot_in(i)
                if i > 0:
                    rot_out(i - 1)
                scans(i)
            rot_out(len(iters) - 1)
            for k in M.values():
                Fs.give(k)
            for d_ in Wb:
                for k in d_.values():
                    Fs.give(k)
            for d_ in Tb:
                for k in d_.values():
                    Hs.give(k)
            for ct in range(4):
                Fs.give(uf[ct]); Hs.give(ub[ct])
            ya = []
            for ct in range(4):
                pg = PF.take()
                for c2 in range(4):
                    mm(pg[1][:], glub[:, c2, ct * 128:(ct + 1) * 128], zb[c2][1][:], c2 == 0, c2 == 3,
                       [b_glub, zb[c2][2]], [pg[2]])
                sg = Fs.take()
                act(sg[1][:], pg[1][:], AF.Sigmoid, [pg[2], b_pp], [sg[2]], bias=pp[:, C_GLUB + ct:C_GLUB + ct + 1])
                PF.give(pg)
                y = Hs.take()
                tt("dve", y[1][:], zf[ct][1][:], sg[1][:], ALU.mult, [zf[ct][2], sg[2]], [y[2]])
                Fs.give(sg)
                ya.append(y)
            for ct in range(4):
                Fs.give(zf[ct]); Hs.give(zb[ct])
            return ya

        def hg_branch(l, blk):
            qs, sg_, vb, gs = [], [], [], []
            for chunk, lst, kind in ((1, qs, "silu"), (2, sg_, "sig"), (3, vb, "v"), (4, gs, "silu")):
                i = load_win(l, chunk)
                for hd in range(4):
                    p = proj(i, hd)
                    if kind == "v":
                        h = Hs.take()
                        cp("dve", h[1][:], p[1][:], [p[2]], [h[2]])
                        lst.append(h)
                    else:
                        f = Fs.take()
                        act(f[1][:], p[1][:], AF.Silu if kind == "silu" else AF.Sigmoid, [p[2]], [f[2]])
                        lst.append(f)
                    PF.give(p)
            yb = [None] * 4

            def head_gen(hd, si):
                vT = vTs[si]; kdT = kdTs[si]; smb8 = smb8s[si]; hs = hss[hd]
                b_vT = b_vTs[si]; b_kdT = b_kdTs[si]; b_smb8 = b_smb8s[si]; b_hs = b_hss[hd]
                ff = Fs.take(); lf = Fs.take(); kk = Fs.take()
                ts("dve", ff[1][:], sg_[hd][1][:], ppx[:, 4 + hd:5 + hd], ppx[:, hd:hd + 1], ALU.mult, ALU.add,
                   [sg_[hd][2], b_ppx], [ff[2]])
                Fs.give(sg_[hd])
                act(lf[1][:], ff[1][:], AF.Ln, [ff[2]], [lf[2]])
                ts("pool", kk[1][:], ff[1][:], -1.0, 1.0, ALU.mult, ALU.add, [ff[2]], [kk[2]])
                yield
                bb = ff
                scan(bb[1][:], rst, lf[1][:], 0.0, [b_cst, lf[2], kk[2]], [bb[2]])
                bv = bb[1][:].rearrange("p (c t) -> p c t", t=64)
                bm = lf
                tt("dve", bm[1][:].rearrange("p (c t) -> p c t", t=64), bv, bv[:, :, 31:32].broadcast_to([128, NCH, 64]),
                   ALU.subtract, [bb[2]], [bm[2]])
                yield
                act(hs[:, 0, :].unsqueeze(2), bv[:, :, 31:32], AF.Exp, [bb[2]], [b_hs])
                act(hs[:, 1, :].unsqueeze(2), bv[:, :, 63:64], AF.Exp, [bb[2]], [b_hs])
                tt("dve", hs[:, 3, :].unsqueeze(2), bv[:, :, 63:64], bv[:, :, 31:32], ALU.subtract, [bb[2]], [b_hs])
                act(hs[:, 2, :], hs[:, 3, :], AF.Exp, [b_hs], [b_hs])
                e1 = Fs.take(); e2 = bb
                act(e1[1][:], bm[1][:], AF.Exp, [bm[2]], [e1[2]])
                act(e2[1][:], bm[1][:], AF.Exp, [bm[2], b_hs], [e2[2]], scale=-1.0)
                yield
                qd = Hs.take(); kd = Hs.take()
                tt("pool", qd[1][:], qs[hd][1][:], e1[1][:], ALU.mult, [qs[hd][2], e1[2]], [qd[2]])
                tt("dve", kd[1][:], kk[1][:], e2[1][:], ALU.mult, [kk[2], e2[2]], [kd[2]])
                Fs.give(qs[hd]); Fs.give(e1); Fs.give(e2); Fs.give(kk); Fs.give(lf)
                yield
                psc = PF.take()
                for c in range(NCH):
                    cs = slice(c * 64, (c + 1) * 64)
                    mm(psc[1][0:64, cs], kd[1][:, cs], qd[1][:, cs], True, True, [kd[2], qd[2]], [psc[2]])
                scm = Hs.take()
                tt("dve", scm[1][0:64, :], psc[1][0:64, :], mask, ALU.mult, [psc[2], b_cst], [scm[2]])
                PF.give(psc)
                yield
                for src, dstT, bd in ((vb[hd], vT, b_vT), (kd, kdT, b_kdT)):
                    pt = PB.take()
                    for c in range(NCH):
                        tr(pt[1][0:64, c * 128:(c + 1) * 128], src[1][:, c * 64:(c + 1) * 64], identb,
                           [src[2], b_idb], [pt[2]])
                    act(dstT[0:64, :], pt[1][0:64, 0:NCH * 128], AF.Copy, [pt[2]], [bd])
                    PB.give(pt)
                Hs.give(vb[hd])
                yield
                pkv = [PF.take(), PF.take()]
                for c in range(NCH):
                    c128 = slice(c * 128, (c + 1) * 128)
                    pk = pkv[c // 4]
                    mm(pk[1][:, (c % 4) * 128:(c % 4 + 1) * 128], kdT[0:64, c128], vT[0:64, c128], True, True,
                       [b_kdT, b_vT], [pk[2]])
                yield
                t_ = Fs.take()
                for c in range(NCH):
                    pk = pkv[c // 4]
                    ts("dve", smb8[:, c * 128:(c + 1) * 128], hgS[:, hd, :], hs[:, 0, c:c + 1], None, ALU.mult, None,
                       [b_hgS[hd], b_hs], [b_smb8])
                    ts("dve", t_[1][:, 0:128], pk[1][:, (c % 4) * 128:(c % 4 + 1) * 128], hs[:, 2, c:c + 1], None, ALU.mult, None,
                       [pk[2], b_hs], [t_[2]])
                    stt(hgS[:, hd, :], hgS[:, hd, :], hs[:, 1, c:c + 1], t_[1][:, 0:128], ALU.mult, ALU.add,
                        [b_hgS[hd], b_hs, t_[2]], [b_hgS[hd]])
                Fs.give(t_)
                PF.give(pkv[0]); PF.give(pkv[1])
                yield
                po = PF.take()
                for c in range(NCH):
                    cs = slice(c * 64, (c + 1) * 64)
                    c128 = slice(c * 128, (c + 1) * 128)
                    mm(po[1][:, cs], vT[0:64, c128], scm[1][0:64, cs], True, False, [b_vT, scm[2]], [po[2]])
                    mm(po[1][:, cs], smb8[:, c128], qd[1][:, cs], False, True, [b_smb8, qd[2]], [po[2]])
                Hs.give(qd); Hs.give(kd); Hs.give(scm)
                yield
                if dbg and blk == 0:
                    od = Fs.take()
                    cp("dve", od[1][:], po[1][:], [po[2]], [od[2]])
                    dma("sp", dbg_d["dbg_o"][hd * 128:(hd + 1) * 128, :], od[1][:], [od[2]], [])
                    Fs.give(od)
                sq = Hs.take()
                act(sq[1][:], po[1][:], AF.Square, [po[2]], [sq[2]])
                pss = PF.take()
                mm(pss[1][:], onesb, sq[1][:], True, True, [b_idb, sq[2]], [pss[2]])
                Hs.give(sq)
                yield
                sr = Fs.take()
                act(sr[1][:], pss[1][:], AF.Ln, [pss[2]], [sr[2]], scale=1.0 / 128, bias=EPS)
                PF.give(pss)
                act(sr[1][:], sr[1][:], AF.Exp, [sr[2]], [sr[2]], scale=-0.5)
                tt("dve", sr[1][:], po[1][:], sr[1][:], ALU.mult, [po[2], sr[2]], [sr[2]])
                PF.give(po)
                y = Hs.take()
                stt(y[1][:], sr[1][:], pp[:, C_HNW + hd:C_HNW + hd + 1], gs[hd][1][:], ALU.mult, ALU.mult,
                    [sr[2], b_pp, gs[hd][2]], [y[2]])
                Fs.give(sr); Fs.give(gs[hd])
                yb[hd] = y
            HG_LAG = cfg.get("hg_lag", 4)
            pending = [(0, head_gen(0, 0)), (0, head_gen(1, 1)), (HG_LAG, head_gen(2, 0)), (HG_LAG, head_gen(3, 1))]
            alive = []
            step = 0
            while pending or alive:
                while pending and pending[0][0] <= step:
                    alive.append(pending.pop(0)[1])
                for g in list(alive):
                    try:
                        next(g)
                    except StopIteration:
                        alive.remove(g)
                step += 1
            return yb

        def rg_branch(l, blk):
            wi5 = load_win(l, 5)
            for ct in range(4):
                p = proj(wi5, ct)
                act(xce[:, ct, 3:3 + NB], p[1][:], AF.Copy, [p[2]], [b_xce[ct]])
                PF.give(p)
            wi6 = load_win(l, 6)
            gg = []
            for ct in range(4):
                p = proj(wi6, ct)
                f = Fs.take()
                act(f[1][:], p[1][:], AF.Gelu_apprx_tanh, [p[2]], [f[2]])
                PF.give(p)
                gg.append(f)
            yc = [None] * 4

            def ct_gen(ct):
                xc = Fs.take()
                cw = lambda i: pp[:, C_CW + i * 4 + ct:C_CW + i * 4 + ct + 1]
                ts("dve", xc[1][:], xce[:, ct, 0:NB], cw(0), pp[:, C_CB + ct:C_CB + ct + 1], ALU.mult, ALU.add,
                   [b_xce[ct], b_pp], [xc[2]])
                for i in range(1, 4):
                    stt(xc[1][:], xce[:, ct, i:i + NB], cw(i), xc[1][:], ALU.mult, ALU.add,
                        [b_xce[ct], b_pp, xc[2]], [xc[2]])
                cp("pool", xce[:, ct, 0:3], xce[:, ct, NB:NB + 3], [b_xce[ct]], [b_xce[ct]])
                yield
                xcb = Hs.take()
                cp("pool", xcb[1][:], xc[1][:], [xc[2]], [xcb[2]])
                pr = PF.take(); pi_ = PF.take()
                mm(pr[1][:], rgw[:, 0, ct * 128:(ct + 1) * 128], xcb[1][:], True, True, [b_rgw, xcb[2]], [pr[2]])
                mm(pi_[1][:], rgw[:, 1, ct * 128:(ct + 1) * 128], xcb[1][:], True, True, [b_rgw, xcb[2]], [pi_[2]])
                Hs.give(xcb)
                r = Fs.take(); ii = Fs.take(); a = Fs.take()
                act(r[1][:], pr[1][:], AF.Sigmoid, [pr[2], b_pp], [r[2]], bias=pp[:, C_BA + ct:C_BA + ct + 1])
                act(ii[1][:], pi_[1][:], AF.Sigmoid, [pi_[2], b_pp], [ii[2]], bias=pp[:, C_BX + ct:C_BX + ct + 1])
                PF.give(pr); PF.give(pi_)
                yield
                act(a[1][:], r[1][:], AF.Exp, [r[2], b_ppx], [a[2]], scale=ppx[:, 8 + ct:9 + ct])
                act(r[1][:], r[1][:], AF.Exp, [r[2], b_ppx], [r[2]], scale=ppx[:, 12 + ct:13 + ct])
                ts("dve", r[1][:], r[1][:], 1.0, -1.0, ALU.min, ALU.mult, [r[2]], [r[2]])
                act(r[1][:], r[1][:], AF.Sqrt, [r[2]], [r[2]], bias=1.0)
                yield
                tt("pool", ii[1][:], ii[1][:], xc[1][:], ALU.mult, [ii[2], xc[2]], [ii[2]])
                tt("dve", ii[1][:], ii[1][:], r[1][:], ALU.mult, [ii[2], r[2]], [ii[2]])
                yield
                scan(xc[1][:], a[1][:], ii[1][:], hcar[:, ct:ct + 1], [a[2], ii[2], b_hcar], [xc[2]])
                cp("dve", hcar[:, ct:ct + 1], xc[1][:, NB - 1:NB], [xc[2]], [b_hcar])
                if dbg and blk == 0:
                    dma("sp", dbg_d["dbg_h"][ct * 128:(ct + 1) * 128, :], xc[1][:], [xc[2]], [])
                y = Hs.take()
                tt("dve", y[1][:], xc[1][:], gg[ct][1][:], ALU.mult, [xc[2], gg[ct][2]], [y[2]])
                Fs.give(r); Fs.give(ii); Fs.give(a); Fs.give(xc); Fs.give(gg[ct])
                yc[ct] = y
            alive = [ct_gen(ct) for ct in range(4)]
            while alive:
                for g in list(alive):
                    try:
                        next(g)
                    except StopIteration:
                        alive.remove(g)
            return yc

        def mixer(l, blk, ncol):
            do_norm(ncol)
            only = cfg.get('only_branch')
            if only is not None:
                fn = {'s5': s5_branch, 'hg': hg_branch, 'rg': rg_branch}[only]
                yy = fn(l, blk)
                if dbg and blk == 0:
                    bidx = {'s5': 0, 'hg': 1, 'rg': 2}[only]
                    for ct in range(4):
                        f = Fs.take()
                        cp("dve", f[1][:], yy[ct][1][:], [yy[ct][2]], [f[2]])
                        dma("sp", dbg_d["dbg_y"][bidx, ct * 128:(ct + 1) * 128, :], f[1][:], [f[2]], [])
                        Fs.give(f)
                for h in yy:
                    Hs.give(h)
                free_hT()
                return
            ys = [s5_branch(l, blk), hg_branch(l, blk), rg_branch(l, blk)]
            if dbg and blk == 0:
                for b in range(3):
                    for ct in range(4):
                        f = Fs.take()
                        cp("dve", f[1][:], ys[b][ct][1][:], [ys[b][ct][2]], [f[2]])
                        dma("sp", dbg_d["dbg_y"][b, ct * 128:(ct + 1) * 128, :], f[1][:], [f[2]], [])
                        Fs.give(f)
            macc = [Fs.take() for _ in range(8)]
            mg = [Hs.take() for _ in range(8)]
            for b in range(3):
                bi = wtake()
                wload(bi, 4096, [(v_bp(bi), bp_d[l, b].rearrange("(c p) n -> p c n", p=128))], 43 + b)
                for half in range(2):
                    i = load_win(l, 7 + 2 * b + half)
                    for f4 in range(4):
                        ft = half * 4 + f4
                        pup = PF.take()
                        for ct in range(4):
                            mm(pup[1][:], v_bp(bi)[:, ct, ft * 128:(ft + 1) * 128], ys[b][ct][1][:], ct == 0, ct == 3,
                               [b_wb[bi], ys[b][ct][2]], [pup[2]])
                        pgt = proj(i, f4)
                        sg = Fs.take()
                        act(sg[1][:], pgt[1][:], AF.Sigmoid, [pgt[2]], [sg[2]])
                        PF.give(pgt)
                        if b == 0:
                            tt("dve", macc[ft][1][:], sg[1][:], pup[1][:], ALU.mult, [sg[2], pup[2]], [macc[ft][2]])
                        else:
                            tt("dve", sg[1][:], sg[1][:], pup[1][:], ALU.mult, [sg[2], pup[2]], [sg[2]])
                            if b == 1:
                                tt("dve", macc[ft][1][:], macc[ft][1][:], sg[1][:], ALU.add, [macc[ft][2], sg[2]], [macc[ft][2]])
                            else:
                                tt("dve", mg[ft][1][:], macc[ft][1][:], sg[1][:], ALU.add, [macc[ft][2], sg[2]], [mg[ft][2]])
                        PF.give(pup); Fs.give(sg)
                for ct in range(4):
                    Hs.give(ys[b][ct])
            for f in macc:
                Fs.give(f)
            free_hT()
            if cfg.get('merge_stage', 9) < 2:
                for h in mg:
                    Hs.give(h)
                return
            wov = wo_d[l].rearrange("(k p) n -> p k n", p=128)
            wis = []
            for half in range(2):
                i = wtake()
                wload(i, 4096, [(v_win(i), wov[:, :, half * 512:(half + 1) * 512])], 46 + half)
                wis.append(i)
            for t in range(TT):
                for half in range(2):
                    i = wis[half]
                    po = PF.take()
                    for ft in range(8):
                        mm(po[1][:], mg[ft][1][:, t * 128:(t + 1) * 128], v_win(i)[:, ft, :], ft == 0, ft == 7,
                           [mg[ft][2], b_wb[i]], [po[2]])
                    xsl = xt[:, t, half * 512:(half + 1) * 512]
                    stt(xsl, po[1][:], 1.0, xsl, ALU.mult, ALU.add, [po[2], b_x[t]], [b_x[t]])
                    PF.give(po)
            for h in mg:
                Hs.give(h)

        if dbg:
            dout("dbg_x1", [NB, D]); dout("dbg_x2", [NB, D])
            dout("dbg_y", [3, 512, NB]); dout("dbg_z", [512, NB]); dout("dbg_o", [512, NB]); dout("dbg_h", [512, NB])

        def dump_x(name, blk):
            if dbg and blk == 0:
                dv = dbg_d[name].rearrange("(t p) d -> p t d", p=128)
                for t in range(TT):
                    dma("sp", dv[:, t, :], xt[:, t, :], [b_x[t]], [])

        def final_norm_store(dst_blk, b_dst):
            rc = norm_stats(32)
            fw = [Fs.take(), Fs.take()]
            for hf in range(2):
                dma("sp", fw[hf][1][:], fnw_d[:, hf * 512:(hf + 1) * 512], (), [fw[hf][2]])
            for t in range(TT):
                for hf in range(2):
                    o = Fs.take()
                    act(o[1][:], xt[:, t, hf * 512:(hf + 1) * 512], AF.Copy, [b_x[t], b_sm], [o[2]],
                        scale=sm[:, rc + t:rc + t + 1])
                    tt("dve", o[1][:], o[1][:], fw[hf][1][:], ALU.mult, [o[2], fw[hf][2]], [o[2]])
                    dma("sp", dst_blk[:, t, hf * 512:(hf + 1) * 512], o[1][:], [o[2]], [b_dst])
                    Fs.give(o)
            Fs.give(fw[0]); Fs.give(fw[1])

        if pipe:
            b_srcs = [Buf("xch_src%d" % i) for i in range(4)]; b_dsts = [Buf("xch_dst%d" % i) for i in range(4)]
            src_v = [xsrc_d.ap()[c].rearrange("(t p) d -> p t d", p=128) for c in range(4)]
            dst_v = [xdst_d.ap()[c].rearrange("(r t p) d -> r p t d", p=128, t=TT) for c in range(4)]

            def handoff(c4):
                for t in range(TT):
                    dma("pool", src_v[c4][:, t, :], xt[:, t, c4 * 256:(c4 + 1) * 256], [b_x[t]], [b_srcs[c4]])
                S.cc(lambda E: E.collective_compute("AllGather", ALU.bypass, replica_groups=groups,
                                                    ins=[xsrc_d.ap()[c4].opt()], outs=[xdst_d.ap()[c4].opt()]),
                     [b_srcs[c4]], [b_dsts[c4]])
            groups = [[2 * i, 2 * i + 1] for i in range(4)]
            layer_prep(0)
            NIT = NBLK + 1
            for j in range(NIT):
                cur_it[0] = j
                for t in range(TT):
                    if j < NBLK:
                        dma("sp", xt[:, t, :], xin_v[j][:, t, :], [], [b_x[t]])
                        ts("dve", xt[:, t, :], xt[:, t, :], role[:, 0:1], None, ALU.mult, None, [b_x[t], b_role], [b_x[t]])
                    for c4 in range(4):
                        if j == 0:
                            continue
                        e = Fs.take()
                        dma("sp", e[1][:, 0:256], dst_v[c4][0][:, t, :], [b_dsts[c4]], [e[2]])
                        xsl = xt[:, t, c4 * 256:(c4 + 1) * 256]
                        if j < NBLK:
                            stt(xsl, e[1][:, 0:256], role[:, 1:2], xsl, ALU.mult, ALU.add, [e[2], b_role, b_x[t]], [b_x[t]])
                        else:
                            ts("dve", xsl, e[1][:, 0:256], role[:, 1:2], None, ALU.mult, None, [e[2], b_role], [b_x[t]])
                        Fs.give(e)
                if "ffn0" in parts:
                    ffn(0, 0, C_NORM + 0)
                if "mix" in parts:
                    mixer(0, j, C_NORM + 8)
                if "ffn1" in parts:
                    ffn(0, 1, C_NORM + 16, after_chunk=handoff if j < NBLK else None)
                if j >= 1:
                    final_norm_store(out_v[j - 1], b_xh[j - 1])
                if j == 0:
                    fa = role[:, 0:1]
                    for ct in range(4):
                        ts("dve", s5car[:, :, ct * 4:ct * 4 + 4], s5car[:, :, ct * 4:ct * 4 + 4], fa, None, ALU.mult, None,
                           [b_car[ct], b_role], [b_car[ct]])
                        ts("dve", hgS[:, ct, :], hgS[:, ct, :], fa, None, ALU.mult, None, [b_hgS[ct], b_role], [b_hgS[ct]])
                        ts("dve", xce[:, ct, 0:3], xce[:, ct, 0:3], fa, None, ALU.mult, None, [b_xce[ct], b_role], [b_xce[ct]])
                    ts("dve", hcar[:], hcar[:], fa, None, ALU.mult, None, [b_hcar, b_role], [b_hcar])
        else:
            for l in range(NL):
                if cfg.get('prep', True):
                    layer_prep(l)
                else:
                    dma('sp', pp[:], pp_d[l], (), [b_pp])
                for blk in range(NBLK):
                    src = xin_v if l == 0 else out_v
                    for t in range(TT):
                        dma("sp", xt[:, t, :], src[blk][:, t, :], [b_xh[blk]], [b_x[t]])
                    if "ffn0" in parts:
                        ffn(l, 0, C_NORM + 0)
                    if l == 0:
                        dump_x("dbg_x1", blk)
                    if "mix" in parts:
                        mixer(l, blk, C_NORM + 8)
                    if l == 0:
                        dump_x("dbg_x2", blk)
                    if "ffn1" in parts:
                        ffn(l, 1, C_NORM + 16)
                    if l == NL - 1:
                        rc = norm_stats(32)
                        fw = [Fs.take(), Fs.take()]
                        for hf in range(2):
                            dma("sp", fw[hf][1][:], fnw_d[:, hf * 512:(hf + 1) * 512], (), [fw[hf][2]])
                        for t in range(TT):
                            for hf in range(2):
                                o = Fs.take()
                                act(o[1][:], xt[:, t, hf * 512:(hf + 1) * 512], AF.Copy, [b_x[t], b_sm], [o[2]],
                                    scale=sm[:, rc + t:rc + t + 1])
                                tt("dve", o[1][:], o[1][:], fw[hf][1][:], ALU.mult, [o[2], fw[hf][2]], [o[2]])
                                dma("sp", out_v[blk][:, t, hf * 512:(hf + 1) * 512], o[1][:], [o[2]], [b_xh[blk]])
                                Fs.give(o)
                        Fs.give(fw[0]); Fs.give(fw[1])
                    else:
                        for t in range(TT):
                            dma("sp", out_v[blk][:, t, :], xt[:, t, :], [b_x[t]], [b_xh[blk]])
        S.emit()
    return nc, dbg_d


def _prep_shared(inp):
    import ml_dtypes
    f = lambda a: np.ascontiguousarray(np.asarray(a, dtype=np.float32))
    L = 2
    pp = np.zeros((L, 128, 128), np.float32)

    def colv(v, n):
        return np.asarray(v, np.float32).reshape(n, 128).T

    for l in range(L):
        for n in range(3):
            pp[l, :, n * 8:(n + 1) * 8] = colv(inp["norm_w"][l, n], 8)
        pp[l, :, 24:28] = colv(inp["s5_d"][l], 4)
        pp[l, :, 28:32] = colv(inp["s5_glu_b"][l], 4)
        pp[l, :, 32:36] = colv(inp["hg_lb_logits"][0], 4)
        pp[l, :, 36:40] = colv(inp["hg_lb_logits"][1], 4)
        pp[l, :, 40:44] = colv(inp["hg_norm_w"][l], 4)
        for i in range(4):
            pp[l, :, 44 + i * 4:48 + i * 4] = colv(inp["rg_conv_w"][l, i], 4)
        pp[l, :, 60:64] = colv(inp["rg_conv_b"][l], 4)
        pp[l, :, 64:68] = colv(inp["rg_ba"][l], 4)
        pp[l, :, 68:72] = colv(inp["rg_bx"][l], 4)
        pp[l, :, 72:76] = colv(inp["rg_lambda"][l], 4)
    fnw = np.ascontiguousarray(np.broadcast_to(np.asarray(inp["final_norm_w"], np.float32)[None, :], (128, D)))
    lam_re = np.asarray(inp["s5_lambda_re"], np.float32)
    lam_im = np.asarray(inp["s5_lambda_im"], np.float32)
    ldt = np.repeat(np.asarray(inp["s5_log_dt"], np.float32)[:, :, None], 64, axis=2)
    rows = np.stack([lam_re.reshape(L, 2048), lam_im.reshape(L, 2048), ldt.reshape(L, 2048)], axis=1)
    s5row = np.ascontiguousarray(np.broadcast_to(rows[:, None, :, :], (L, 128, 3, 2048)))
    cols = np.stack([a.reshape(L, 16, 128).transpose(0, 2, 1) for a in (lam_re, lam_im, ldt)], axis=2)
    s5col = np.ascontiguousarray(cols)
    s5bt = np.zeros((L, 2, 128, 16, 2, 64), np.float32)
    s5c = np.zeros((L, 2, 2, 64, 16, 128), np.float32)
    for ri, (bsrc, csrc) in enumerate(((inp["s5_b_re"], inp["s5_c_re"]), (inp["s5_b_im"], inp["s5_c_im"]))):
        bsrc = np.asarray(bsrc, np.float32)
        csrc = np.asarray(csrc, np.float32)
        for g in range(32):
            st, gl = g // 2, g % 2
            c0 = (g % 8) * 16
            s5bt[:, ri, c0:c0 + 16, st, gl, :] = bsrc[:, g].transpose(0, 2, 1)
            s5c[:, ri, gl, :, st, c0:c0 + 16] = csrc[:, g].transpose(0, 2, 1)
    s5bt = s5bt.reshape(L, 2, 128, 2048)
    s5c = s5c.reshape(L, 2, 128, 2048)
    rgw = np.zeros((L, 2, 2, 64, 4, 2, 64), np.float32)
    for j, src in enumerate((inp["rg_wa"], inp["rg_wx"])):
        src = np.asarray(src, np.float32)
        for h in range(8):
            ct, hl = h // 2, h % 2
            rgw[:, j, hl, :, ct, hl, :] = src[:, h]
    rgw = rgw.reshape(L, 2, 128, 512)
    cst = np.zeros((128, 512 + NB), np.float32)
    m = np.triu(np.ones((64, 64), np.float32))
    cst[0:64, 0:512] = np.tile(m, (1, 8))
    r = np.ones((NB,), np.float32); r[::64] = 0.0
    cst[:, 512:512 + NB] = r[None, :]
    idb = np.concatenate([np.eye(128, dtype=np.float32), np.ones((128, 128), np.float32)], axis=1).astype(ml_dtypes.bfloat16)
    shared = {
        "ffn_gate": f(inp["ffn_gate"]), "ffn_up": f(inp["ffn_up"]), "ffn_down": f(inp["ffn_down"]),
        "w_in": f(inp["w_in"]), "branch_proj": f(inp["branch_proj"]), "w_out": f(inp["w_out"]),
        "s5_glu_w": f(inp["s5_glu_w"]), "pp": pp, "fnw": fnw, "s5row": s5row, "s5col": s5col,
        "s5bt": np.ascontiguousarray(s5bt), "s5c": np.ascontiguousarray(s5c), "rgw": np.ascontiguousarray(rgw),
        "cst": cst, "idb": idb,
    }
    return shared


PER_LAYER = ("ffn_gate", "ffn_up", "ffn_down", "w_in", "branch_proj", "w_out", "s5_glu_w", "pp", "s5row", "s5col",
             "s5bt", "s5c", "rgw")


def make_in_maps(inputs, n_pairs=4):
    x = np.asarray(inputs["x"], np.float32)
    shared = _prep_shared(inputs)
    in_maps = []
    for c in range(2 * n_pairs):
        b, l = c // 2, c % 2
        m = {}
        for k, v in shared.items():
            m[k] = np.ascontiguousarray(v[l:l + 1]) if k in PER_LAYER else v
        m["x"] = np.ascontiguousarray(x[b])
        role = np.zeros((128, 2), np.float32)
        role[:, l] = 1.0
        m["role"] = role
        in_maps.append(m)
    return in_maps


def kernel(**inputs):
    nc, _ = build({"pipe": True})
    in_maps = make_in_maps(inputs)
    res = run_bass_kernel_spmd(nc, in_maps, core_ids=list(range(8)))
    out = np.stack([np.asarray(res.results[2 * b + 1]["out"], np.float32).reshape(SEQ, D) for b in range(4)], axis=0)
    return out
```
